# Optimizing a Trainium2 kernel written in Bass

```python
import math
import jax, jax.numpy as jnp
from jax import lax
import numpy as np

D_MODEL = 1024
BATCH = 16
SEQ = 4096
DEPTH = 2
DEC_BATCH = 16
DEC_SEQ = 32
PAST_LEN = 1024

CHUNK = 64
N_MEM = 256
BRANCH = 2 * D_MODEL
MIX_W = 3 * D_MODEL // 2
MEM_W = D_MODEL // 2
MEM_HEADS = 4
MEM_HD = MEM_W // MEM_HEADS
RW_HD = 64
RW_HEADS = MIX_W // RW_HD
LORA = 64
RW_PROJ = 3 * MIX_W + 2 * LORA
RW_IN = RW_PROJ + MEM_W + BRANCH
DF_HD = 64
DF_HEADS = MIX_W // (2 * DF_HD)
DF_VD = 2 * DF_HD
DF_IN = 3 * MIX_W + MEM_W + BRANCH
Q_BLOCK = 128
N_RWKV = (DEPTH + 1) // 2
N_DIFF = DEPTH // 2
EPS = 1e-6
GN_EPS = 64e-5
NEG_INF = -1e30

kernel_name = 'hybrid_rwkv7_diffattn_stream_step'


def rmsnorm(x, w):
    xf = x.astype(jnp.float32)
    y = xf * lax.rsqrt(jnp.mean(xf * xf, axis=-1, keepdims=True) + EPS)
    return (y * w.astype(jnp.float32)).astype(x.dtype)


def mem_kv(mem, norm_w, wk, wv):
    B = mem.shape[0]
    h = rmsnorm(mem, norm_w)
    return ((h @ wk).reshape(B, N_MEM, MEM_HEADS, MEM_HD),
            (h @ wv).reshape(B, N_MEM, MEM_HEADS, MEM_HD))


def mem_attention(qm, mk, mv):
    B, T, _ = qm.shape
    q = qm.reshape(B, T, MEM_HEADS, MEM_HD)
    s = jnp.einsum('bthd,bmhd->bhtm', q, mk.astype(q.dtype)).astype(jnp.float32) * MEM_HD ** -0.5
    p = jax.nn.softmax(s, axis=-1).astype(mv.dtype)
    return jnp.einsum('bhtm,bmhd->bthd', p, mv).astype(qm.dtype).reshape(B, T, MEM_W)


def rwkv_mixer(p, prev_row, S0, mu, w0, w_up, a0, a_up, k_k, k_a, r_k, ln_w, ln_b):
    B, T, _ = p.shape
    f32 = jnp.float32
    p_prev = jnp.concatenate([prev_row[:, None, :].astype(p.dtype), p[:, :-1]], axis=1)
    ps = p + mu.astype(p.dtype) * (p_prev - p)
    r, k, v, pw, pa = jnp.split(ps, [MIX_W, 2 * MIX_W, 3 * MIX_W, 3 * MIX_W + LORA], axis=-1)
    log_w = -jax.nn.softplus(-(w0 + jnp.tanh(pw) @ w_up).astype(f32)) - 0.5
    a = jax.nn.sigmoid((a0 + pa @ a_up).astype(f32))
    heads = lambda t: t.astype(f32).reshape(B, T, RW_HEADS, RW_HD)
    r, k, v, decay, a = heads(r), heads(k), heads(v), heads(jnp.exp(-jnp.exp(log_w))), heads(a)
    kk = k * k_k.astype(f32).reshape(RW_HEADS, RW_HD)
    kk = kk / jnp.maximum(jnp.linalg.norm(kk, axis=-1, keepdims=True), 1e-12)
    k = k * (1.0 + (a - 1.0) * k_a.astype(f32).reshape(RW_HEADS, RW_HD))

    def step(S, inp):
        r_t, k_t, v_t, w_t, kk_t, a_t = inp
        sa = jnp.einsum('bhvk,bhk->bhv', S, -kk_t)
        S = (S * w_t[:, :, None, :] + sa[..., None] * (kk_t * a_t)[:, :, None, :]
             + v_t[..., None] * k_t[:, :, None, :])
        return S, jnp.einsum('bhvk,bhk->bhv', S, r_t)

    xs = tuple(jnp.swapaxes(t, 0, 1) for t in (r, k, v, decay, kk, a))
    S_T, y = lax.scan(step, S0.astype(f32), xs)
    y = jnp.swapaxes(y, 0, 1)
    mean = jnp.mean(y, axis=-1, keepdims=True)
    var = jnp.mean(jnp.square(y - mean), axis=-1, keepdims=True)
    y = ((y - mean) * lax.rsqrt(var + GN_EPS) * ln_w.astype(f32).reshape(RW_HEADS, RW_HD)
         + ln_b.astype(f32).reshape(RW_HEADS, RW_HD))
    y = y + jnp.sum(r * k * r_k.astype(f32), axis=-1, keepdims=True) * v
    return y.reshape(B, T, MIX_W), S_T, p[:, -1]


def rwkv_layer(x, mk, mv, prev_row, S0, norm_w, w_out, w_in, mu, w0, w_up, a0, a_up,
               k_k, k_a, r_k, ln_w, ln_b):
    h = rmsnorm(x, norm_w)
    p, qm, gate = jnp.split(h @ w_in, [RW_PROJ, RW_PROJ + MEM_W], axis=-1)
    y_mix, S_T, last_row = rwkv_mixer(p, prev_row, S0, mu, w0, w_up, a0, a_up, k_k, k_a, r_k, ln_w, ln_b)
    y_mem = mem_attention(qm, mk, mv)
    out = jnp.concatenate([y_mix.astype(x.dtype), y_mem], axis=-1) * jax.nn.silu(gate)
    return x + out @ w_out, S_T, last_row


def diff_attention(q, k_all, v_all, q_pos, k_pos, lam):
    q1, q2 = jnp.split(q, 2, axis=-1)
    k1, k2 = jnp.split(k_all, 2, axis=-1)
    mask = k_pos[None, :] < (q_pos[:, None] // CHUNK + 1) * CHUNK

    def probs(qa, ka):
        s = jnp.einsum('bqhd,bkhd->bhqk', qa, ka).astype(jnp.float32) * DF_HD ** -0.5
        return jax.nn.softmax(jnp.where(mask, s, NEG_INF), axis=-1)

    attn = probs(q1, k1) - lam * probs(q2, k2)
    return jnp.einsum('bhqk,bkhd->bqhd', attn.astype(v_all.dtype), v_all)


def diff_layer(x, mk, mv, k_past, v_past, layer_idx, norm_w, w_out, w_in, lq1, lk1, lq2, lk2, subln):
    B, T, _ = x.shape
    h = rmsnorm(x, norm_w)
    q, k, v, qm, gate = jnp.split(h @ w_in, [MIX_W, 2 * MIX_W, 3 * MIX_W, 3 * MIX_W + MEM_W], axis=-1)
    q = q.reshape(B, T, DF_HEADS, 2 * DF_HD)
    k = k.reshape(B, T, DF_HEADS, 2 * DF_HD)
    v = v.reshape(B, T, DF_HEADS, DF_VD)
    if k_past is None:
        past, k_all, v_all = 0, k, v
    else:
        past = k_past.shape[1]
        k_all = jnp.concatenate([k_past.astype(k.dtype), k], axis=1)
        v_all = jnp.concatenate([v_past.astype(v.dtype), v], axis=1)
    lam_init = 0.8 - 0.6 * math.exp(-0.3 * layer_idx)
    f32 = jnp.float32
    lam = (jnp.exp(jnp.sum(lq1.astype(f32) * lk1.astype(f32)))
           - jnp.exp(jnp.sum(lq2.astype(f32) * lk2.astype(f32))) + lam_init)
    blocks = []
    for s in range(0, T, Q_BLOCK):
        e = min(s + Q_BLOCK, T)
        q_pos = past + jnp.arange(s, e)
        k_pos = jnp.arange(past + e)
        blocks.append(diff_attention(q[:, s:e], k_all[:, :past + e], v_all[:, :past + e], q_pos, k_pos, lam))
    o = jnp.concatenate(blocks, axis=1)
    o = rmsnorm(o, subln) * (1.0 - lam_init)
    y_mem = mem_attention(qm, mk, mv)
    out = jnp.concatenate([o.reshape(B, T, MIX_W).astype(x.dtype), y_mem], axis=-1) * jax.nn.silu(gate)
    return x + out @ w_out, k, v


def setup_inputs(seed: int = 0) -> dict:
    key = jax.random.key(seed)
    ks = iter(jax.random.split(key, 40))
    f32 = jnp.float32
    nrm = lambda shape, scale: jax.random.normal(next(ks), shape, f32) * scale
    uni = lambda shape, lo, hi: jax.random.uniform(next(ks), shape, f32, lo, hi)
    return {
        'x_prompt': nrm((BATCH, SEQ, D_MODEL), 1.0),
        'mem_prompt': nrm((BATCH, N_MEM, D_MODEL), 1.0),
        'x_sample': nrm((DEC_BATCH, DEC_SEQ, D_MODEL), 1.0),
        'state_rwkv': nrm((N_RWKV, DEC_BATCH, RW_HEADS, RW_HD, RW_HD), 0.1),
        'state_shift': nrm((N_RWKV, DEC_BATCH, RW_PROJ), 1.0),
        'cache_k': nrm((N_DIFF, DEC_BATCH, PAST_LEN, DF_HEADS, 2 * DF_HD), 1.0),
        'cache_v': nrm((N_DIFF, DEC_BATCH, PAST_LEN, DF_HEADS, DF_VD), 1.0),
        'cache_mem_k': nrm((DEPTH, DEC_BATCH, N_MEM, MEM_HEADS, MEM_HD), 1.0),
        'cache_mem_v': nrm((DEPTH, DEC_BATCH, N_MEM, MEM_HEADS, MEM_HD), 1.0),
        'norm_w': 1.0 + nrm((DEPTH, D_MODEL), 0.02),
        'mem_norm_w': 1.0 + nrm((DEPTH, D_MODEL), 0.02),
        'w_mem_k': nrm((DEPTH, D_MODEL, MEM_W), D_MODEL ** -0.5),
        'w_mem_v': nrm((DEPTH, D_MODEL, MEM_W), D_MODEL ** -0.5),
        'w_out': nrm((DEPTH, BRANCH, D_MODEL), 0.5 * BRANCH ** -0.5),
        'final_norm_w': 1.0 + nrm((D_MODEL,), 0.02),
        'rw_in': nrm((N_RWKV, D_MODEL, RW_IN), D_MODEL ** -0.5),
        'rw_mu': uni((N_RWKV, RW_PROJ), 0.0, 1.0),
        'rw_w0': uni((N_RWKV, MIX_W), -6.0, 0.0),
        'rw_w_up': nrm((N_RWKV, LORA, MIX_W), 0.1),
        'rw_a0': nrm((N_RWKV, MIX_W), 0.5),
        'rw_a_up': nrm((N_RWKV, LORA, MIX_W), 0.1),
        'rw_k_k': 0.85 + nrm((N_RWKV, MIX_W), 0.02),
        'rw_k_a': 1.0 + nrm((N_RWKV, MIX_W), 0.02),
        'rw_r_k': nrm((N_RWKV, RW_HEADS, RW_HD), 0.1),
        'rw_ln_w': 1.0 + nrm((N_RWKV, MIX_W), 0.02),
        'rw_ln_b': nrm((N_RWKV, MIX_W), 0.02),
        'df_in': nrm((N_DIFF, D_MODEL, DF_IN), D_MODEL ** -0.5),
        'df_lq1': nrm((N_DIFF, DF_HD), 0.1),
        'df_lk1': nrm((N_DIFF, DF_HD), 0.1),
        'df_lq2': nrm((N_DIFF, DF_HD), 0.1),
        'df_lk2': nrm((N_DIFF, DF_HD), 0.1),
        'df_subln': 1.0 + nrm((N_DIFF, DF_VD), 0.02),
    }


def reference(x_prompt, mem_prompt, x_sample, state_rwkv, state_shift, cache_k, cache_v,
              cache_mem_k, cache_mem_v, norm_w, mem_norm_w, w_mem_k, w_mem_v, w_out, final_norm_w,
              rw_in, rw_mu, rw_w0, rw_w_up, rw_a0, rw_a_up, rw_k_k, rw_k_a, rw_r_k, rw_ln_w, rw_ln_b,
              df_in, df_lq1, df_lk1, df_lq2, df_lk2, df_subln):
    xp, xs = x_prompt, x_sample
    Bp = xp.shape[0]
    p_S, p_shift, p_k, p_v, p_mk, p_mv = [], [], [], [], [], []
    s_S, s_shift, s_k, s_v = [], [], [], []
    for i in range(DEPTH):
        j = i // 2
        mkp, mvp = mem_kv(mem_prompt, mem_norm_w[i], w_mem_k[i], w_mem_v[i])
        p_mk.append(mkp)
        p_mv.append(mvp)
        if i % 2 == 0:
            rw = (rw_in[j], rw_mu[j], rw_w0[j], rw_w_up[j], rw_a0[j], rw_a_up[j],
                  rw_k_k[j], rw_k_a[j], rw_r_k[j], rw_ln_w[j], rw_ln_b[j])
            zero_row = jnp.zeros((Bp, RW_PROJ), xp.dtype)
            zero_S = jnp.zeros((Bp, RW_HEADS, RW_HD, RW_HD), jnp.float32)
            xp, Sp, rp = rwkv_layer(xp, mkp, mvp, zero_row, zero_S, norm_w[i], w_out[i], *rw)
            xs, Ss, rs = rwkv_layer(xs, cache_mem_k[i], cache_mem_v[i], state_shift[j], state_rwkv[j],
                                    norm_w[i], w_out[i], *rw)
            p_S.append(Sp)
            p_shift.append(rp)
            s_S.append(Ss)
            s_shift.append(rs)
        else:
            df = (df_in[j], df_lq1[j], df_lk1[j], df_lq2[j], df_lk2[j], df_subln[j])
            xp, kp, vp = diff_layer(xp, mkp, mvp, None, None, i, norm_w[i], w_out[i], *df)
            xs, ks_, vs_ = diff_layer(xs, cache_mem_k[i], cache_mem_v[i], cache_k[j], cache_v[j], i,
                                      norm_w[i], w_out[i], *df)
            p_k.append(kp)
            p_v.append(vp)
            s_k.append(ks_)
            s_v.append(vs_)
    y_prompt = rmsnorm(xp, final_norm_w)
    y_sample = rmsnorm(xs, final_norm_w)
    return (y_prompt, y_sample, jnp.stack(p_S), jnp.stack(p_shift), jnp.stack(p_k), jnp.stack(p_v),
            jnp.stack(p_mk), jnp.stack(p_mv), jnp.stack(s_S), jnp.stack(s_shift), jnp.stack(s_k), jnp.stack(s_v))
```

```python
import contextlib
import math
import numpy as np
import concourse.bass as bass
import concourse.mybir as mybir
from concourse.bass_utils import run_bass_kernel_spmd

F32 = mybir.dt.float32
BF16 = mybir.dt.bfloat16
AF = mybir.ActivationFunctionType
ALU = mybir.AluOpType

ENGS = ["pe", "dve", "act", "pool", "sp"]
NDMA = 48
import os
POOL_ENG = os.environ.get("POOL_ENG", "pool")

D = 1024
MIXW = 1536
MEMW = 512
BR = 2048
RWP = 4736
RWIN = RWP + MEMW + BR
DFIN = 3 * MIXW + MEMW + BR
NMEM = 256
EPS = 1e-6
GN_EPS = 64e-5
DECAY_C = math.exp(-0.5)


def _key(x):
    if isinstance(x, (str, tuple)):
        return x
    if hasattr(x, "tensor"):
        return x.tensor.name
    return x.name


class Prog:
    def __init__(self, nc, same_engine_sync=True):
        self.nc = nc
        self.ops = {e: [] for e in ENGS}
        self.n = {e: 0 for e in ENGS}
        self.seen = {e: {} for e in ENGS}
        self.lastw = {}
        self.readers = {}
        self.dma_val = [0] * NDMA
        self.dma_next = 0
        self.same = same_engine_sync
        self.nwaits = 0

    def _add(self, eng, waits, tok, is_war=False):
        if tok is None:
            return
        sk, val = tok
        if sk == ("e", eng):
            if eng == "pe" or eng == "sp":
                return
            if not self.same:
                return
        if self.seen[eng].get(sk, 0) >= val:
            return
        if waits.get(sk, 0) < val:
            waits[sk] = val

    def op(self, eng, fn, reads=(), writes=(), dma=False, inc=True, attach=None):
        if eng == "pool" and not dma:
            eng = POOL_ENG
        rk = [_key(r) for r in reads if r is not None]
        wk = [_key(w) for w in writes if w is not None]
        wk += [k for k in rk if isinstance(k, str) and k[:2] == "ps" and k[2:].isdigit() and k not in wk]
        waits = {}
        for k in rk:
            self._add(eng, waits, self.lastw.get(k))
        for k in wk:
            self._add(eng, waits, self.lastw.get(k))
            for sk, val in self.readers.get(k, {}).items():
                self._add(eng, waits, (sk, val), is_war=True)
        if dma:
            slot = self.dma_next
            self.dma_next = (self.dma_next + 1) % NDMA
            prev = self.dma_val[slot]
            if prev > 0:
                self._add(eng, waits, (("d", slot), prev))
            self.dma_val[slot] = prev + 16
            tok = (("d", slot), prev + 16)
            incinfo = ("d", slot)
        else:
            if inc:
                self.n[eng] += 1
                tok = (("e", eng), self.n[eng])
                incinfo = ("e", eng)
            else:
                tok = (("e", eng), self.n[eng] + 1)
                incinfo = None
        for sk, val in waits.items():
            self.seen[eng][sk] = val
        self.nwaits += len(waits)
        self.lastw_prev = {_key(attach): self.lastw.get(_key(attach))} if attach is not None else {}
        for k in wk:
            self.lastw[k] = tok
            self.readers[k] = {}
        for k in rk:
            d = self.readers.setdefault(k, {})
            if d.get(tok[0], 0) < tok[1]:
                d[tok[0]] = tok[1]
        wl = list(waits.items())
        if attach is not None and len(wl) > 1:
            t = self.lastw_prev.get(_key(attach))
            if t is not None:
                wl.sort(key=lambda w: w[0] == t[0])
        self.ops[eng].append((fn, wl, incinfo))
        return tok

    def barrier(self):
        for eng in ENGS:
            waits = {}
            for o in ENGS:
                if o != "sp" and self.n[o] > self.seen[eng].get(("e", o), 0):
                    if not (o == eng and eng == "pe"):
                        waits[("e", o)] = self.n[o]
            for i, v in enumerate(self.dma_val):
                if v > self.seen[eng].get(("d", i), 0):
                    waits[("d", i)] = v
            for sk, val in waits.items():
                self.seen[eng][sk] = val
            self.nwaits += len(waits)
            self.ops[eng].append((None, list(waits.items()), None))

    def emit(self):
        nc = self.nc
        with contextlib.ExitStack() as st:
            esem = {e: st.enter_context(nc.semaphore("s_" + e)) for e in ENGS}
            dsem = [st.enter_context(nc.semaphore("d_%d" % i)) for i in range(NDMA)]
            block = st.enter_context(nc.Block())

            def sem_of(sk):
                return esem[sk[1]] if sk[0] == "e" else dsem[sk[1]]

            fin = [(("e", e), self.n[e]) for e in ENGS if e != "sp" and self.n[e] > 0]
            fin += [(("d", i), v) for i, v in enumerate(self.dma_val) if v > 0]

            def run(e, name):
                for fn, waits, incinfo in self.ops[name]:
                    if fn is None:
                        for sk, val in waits:
                            e.wait_ge(sem_of(sk), val)
                        continue
                    for sk, val in waits:
                        e.wait_ge(sem_of(sk), val)
                    ins = fn(e)
                    if incinfo is not None:
                        ins.then_inc(sem_of(incinfo), 16 if incinfo[0] == "d" else 1)
                if name == "sp":
                    for sk, val in fin:
                        e.wait_ge(sem_of(sk), val)

            @block.tensor
            def _(e):
                run(e, "pe")

            @block.vector
            def _(e):
                run(e, "dve")

            @block.scalar
            def _(e):
                run(e, "act")

            @block.gpsimd
            def _(e):
                run(e, "pool")

            @block.sync
            def _(e):
                run(e, "sp")


class Bld:
    def __init__(self, nc, P, ident):
        self.nc, self.P, self.ident = nc, P, ident
        self.rr = 0

    def mm(self, out, lhsT, rhs, start=True, stop=True, rk=(), wk=None):
        self.P.op("pe", lambda e: e.matmul(out, lhsT, rhs, start=start, stop=stop),
                  [lhsT, rhs] + list(rk), [out] if wk is None else wk, inc=True, attach=lhsT)

    def tr(self, out, in_):
        k = in_.shape[0]
        idn = self.ident[0:k, 0:k]
        self.P.op("pe", lambda e: e.transpose(out, in_, idn), [in_, idn], [out], attach=in_)

    def act(self, out, in_, func, scale=1.0, bias=0.0, accum_out=None, eng="act"):
        r = [in_] + [x for x in (scale, bias) if not isinstance(x, (int, float))]
        w = [out] + ([accum_out] if accum_out is not None else [])
        if accum_out is None:
            fn = lambda e: e.activation(out=out, in_=in_, func=func, bias=bias, scale=scale)
        else:
            fn = lambda e: e.activation(out=out, in_=in_, func=func, bias=bias, scale=scale, accum_out=accum_out)
        self.P.op("act", fn, r, w)

    def tt(self, eng, out, in0, in1, op):
        self.P.op(eng, lambda e: e.tensor_tensor(out=out, in0=in0, in1=in1, op=op), [in0, in1], [out])

    def ts(self, eng, out, in0, s1, op0, s2=None, op1=None):
        r = [in0] + [x for x in (s1, s2) if x is not None and not isinstance(x, (int, float))]
        if op1 is None:
            fn = lambda e: e.tensor_scalar(out=out, in0=in0, scalar1=s1, scalar2=None, op0=op0)
        else:
            fn = lambda e: e.tensor_scalar(out=out, in0=in0, scalar1=s1, scalar2=s2, op0=op0, op1=op1)
        self.P.op(eng, fn, r, [out])

    def stt(self, out, in0, scalar, in1, op0, op1):
        r = [in0, in1] + ([] if isinstance(scalar, (int, float)) else [scalar])
        self.P.op("dve", lambda e: e.scalar_tensor_tensor(out=out, in0=in0, scalar=scalar, in1=in1, op0=op0, op1=op1), r, [out])

    def copy(self, eng, out, in_):
        if eng == "act":
            self.P.op("act", lambda e: e.copy(out=out, in_=in_), [in_], [out])
        else:
            self.P.op(eng, lambda e: e.tensor_copy(out=out, in_=in_), [in_], [out])

    def evac(self, out, in_):
        self.rr ^= 1
        self.copy("act" if self.rr else "dve", out, in_)

    def memset(self, eng, ap, val):
        self.P.op(eng, lambda e: e.memset(ap, val), [], [ap])

    def recip(self, out, in_):
        self.P.op("dve", lambda e: e.reciprocal(out=out, in_=in_), [in_], [out])

    def scan_add(self, out, ones, data, init=0.0):
        self.P.op("dve", lambda e: e.tensor_tensor_scan(out=out, data0=ones, data1=data, initial=init,
                                                        op0=ALU.mult, op1=ALU.add), [ones, data], [out])

    def dma(self, out, in_, eng="sp", rk=None, wk=None):
        self.P.op(eng, lambda e: e.dma_start(out=out, in_=in_), [in_] if rk is None else rk,
                  [out] if wk is None else wk, dma=True)


def host_consts():
    i = np.arange(128)
    c = {}
    c["ident"] = np.eye(128, dtype=np.float32)
    su = (i[None, :] > i[:, None]).astype(np.float32)
    iu = (i[None, :] >= i[:, None]).astype(np.float32)
    c["mask2"] = np.concatenate([su, iu], axis=1)
    c["masksl"] = (i[None, :] < i[:, None]).astype(np.float32)
    c["bones"] = (i[None, :] // 64 == i[:, None] // 64).astype(np.float32)
    c["ones"] = np.ones((128, 128), np.float32)
    c["cmask"] = (i[:, None] < (i[None, :] // 64 + 1) * 64).astype(np.float32)
    return c


def build(TP, TS, PAST, NB=2, PHASES="123"):
    nc = bass.Bass("TRN2", target_bir_lowering=False)
    P = Prog(nc)
    din = lambda n, s: nc.dram_tensor(n, list(s), F32, kind="ExternalInput").ap()
    dout = lambda n, s: nc.dram_tensor(n, list(s), F32, kind="ExternalOutput").ap()
    dscr = lambda n, s, dt=F32: nc.dram_tensor(n, list(s), dt, kind="Internal").ap()
    WB = {"rw_in": dscr("rw_in_b", [RWIN // 128, 128, D], BF16), "df_in": dscr("df_in_b", [DFIN // 128, 128, D], BF16),
          "w_out": dscr("w_out_b", [2 * BR, D], BF16)}

    I = {}
    I["x_prompt"] = din("x_prompt", [NB, TP, D])
    I["mem_prompt"] = din("mem_prompt", [NB, NMEM, D])
    I["x_sample"] = din("x_sample", [NB, TS, D])
    I["state_rwkv"] = din("state_rwkv", [NB, 24, 64, 64])
    I["state_shift"] = din("state_shift", [NB, RWP])
    I["cache_k"] = din("cache_k", [NB, PAST, MIXW])
    I["cache_v"] = din("cache_v", [NB, PAST, MIXW])
    I["cache_mem_k"] = din("cache_mem_k", [2, NB, NMEM, MEMW])
    I["cache_mem_v"] = din("cache_mem_v", [2, NB, NMEM, MEMW])
    for n, s in [("norm_w", [2, D]), ("mem_norm_w", [2, D]), ("w_mem_k", [2, D, MEMW]), ("w_mem_v", [2, D, MEMW]),
                 ("w_out", [2, BR, D]), ("final_norm_w", [D]), ("rw_in", [D, RWIN]), ("rw_mu", [RWP]),
                 ("rw_w0", [MIXW]), ("rw_w_up", [64, MIXW]), ("rw_a0", [MIXW]), ("rw_a_up", [64, MIXW]),
                 ("rw_k_k", [MIXW]), ("rw_k_a", [MIXW]), ("rw_r_k", [MIXW]), ("rw_ln_w", [MIXW]),
                 ("rw_ln_b", [MIXW]), ("df_in", [D, DFIN]), ("df_lq1", [64]), ("df_lk1", [64]),
                 ("df_lq2", [64]), ("df_lk2", [64]), ("df_subln", [128])]:
        I[n] = din(n, s)
    for n, s in [("ident", [128, 128]), ("mask2", [128, 256]), ("masksl", [128, 128]), ("bones", [128, 128]),
                 ("ones", [128, 128]), ("cmask", [128, 128])]:
        I[n] = din("c_" + n, s)
    O = {}
    O["y_prompt"] = dout("y_prompt", [NB, TP, D])
    O["y_sample"] = dout("y_sample", [NB, TS, D])
    O["p_S"] = dout("p_S", [NB, 24, 64, 64])
    O["p_shift"] = dout("p_shift", [NB, RWP])
    O["p_k"] = dout("p_k", [NB, TP, MIXW])
    O["p_v"] = dout("p_v", [NB, TP, MIXW])
    O["p_mk"] = dout("p_mk", [2, NB, NMEM, MEMW])
    O["p_mv"] = dout("p_mv", [2, NB, NMEM, MEMW])
    O["s_S"] = dout("s_S", [NB, 24, 64, 64])
    O["s_shift"] = dout("s_shift", [NB, RWP])
    O["s_k"] = dout("s_k", [NB, TS, MIXW])
    O["s_v"] = dout("s_v", [NB, TS, MIXW])

    seqs = []
    for b in range(NB):
        seqs.append(dict(kind="p", b=b, T=TP, C=128, x=I["x_prompt"][b], past=0, name="p%d" % b,
                         y=O["y_prompt"][b], oS=O["p_S"][b], oshift=O["p_shift"][b], ok=O["p_k"][b], ov=O["p_v"][b]))
    for b in range(NB):
        seqs.append(dict(kind="s", b=b, T=TS, C=TS, x=I["x_sample"][b], past=PAST, name="s%d" % b,
                         y=O["y_sample"][b], oS=O["s_S"][b], oshift=O["s_shift"][b], ok=O["s_k"][b], ov=O["s_v"][b]))
    for sq in seqs:
        T = sq["T"]
        sq["qT"] = dscr("qT_" + sq["name"], [12, 128, T], BF16)
        sq["sgT"] = dscr("sgT_" + sq["name"], [12, 128, T])
        sq["oT"] = dscr("oT_" + sq["name"], [12, 128, T])
        sq["x1"] = dscr("x1_" + sq["name"], [T, D])

    with contextlib.ExitStack() as st:
        sb = lambda n, s: st.enter_context(nc.sbuf_tensor(n, list(s), F32))
        ps = [st.enter_context(nc.psum_tensor("ps%d" % i, [128, 512], F32)) for i in range(8)]
        ident = sb("ident", [128, 128]); mask2 = sb("mask2", [128, 256]); masksl = sb("masksl", [128, 128])
        bones = sb("bones", [128, 128]); ones = sb("ones", [128, 128]); cmask = sb("cmask", [128, 128])
        B = Bld(nc, P, ident)
        for t, n in [(ident, "ident"), (mask2, "mask2"), (masksl, "masksl"), (bones, "bones"), (ones, "ones"), (cmask, "cmask")]:
            B.dma(t[:], I[n])
        stgA = sb("stgA", [128, 128]); stgB = sb("stgB", [128, 128]); colA = sb("colA", [128, 128]); colB = sb("colB", [128, 128])
        B.memset("pool", stgA[:], 0.0); B.memset("pool", stgB[:], 0.0)
        rowsA = [("mu", I["rw_mu"], 37)] + [(n, I["rw_" + n], 12) for n in ["w0", "a0", "k_k", "k_a", "r_k", "ln_w", "ln_b"]]
        rowsB = [("nw0", I["norm_w"][0], 8), ("nw1", I["norm_w"][1], 8), ("mnw0", I["mem_norm_w"][0], 8),
                 ("mnw1", I["mem_norm_w"][1], 8), ("subln", I["df_subln"], 1)]
        cols = {}
        for stg, col, rows in ((stgA, colA, rowsA), (stgB, colB, rowsB)):
            r0 = 0
            for n, src, k in rows:
                B.dma(stg[r0:r0 + k, :], src.rearrange("(c p) -> c p", p=128))
                cols[n] = col[:, r0:r0 + k]
                r0 += k
            B.tr(ps[0][:, 0:128], stg[:])
            B.copy("dve", col[:], ps[0][:, 0:128])
        mu, w0, a0, k_k, k_a, r_k, ln_w, ln_b = [cols[n] for n in ["mu", "w0", "a0", "k_k", "k_a", "r_k", "ln_w", "ln_b"]]
        nw0, nw1, mnw0, mnw1, subln = [cols[n] for n in ["nw0", "nw1", "mnw0", "mnw1", "subln"]]
        WUP = sb("WUP", [128, MIXW]); AUP = sb("AUP", [128, MIXW])
        B.memset("pool", WUP[64:128, :], 0.0); B.memset("pool", AUP[0:64, :], 0.0)
        B.dma(WUP[0:64, :], I["rw_w_up"]); B.dma(AUP[64:128, :], I["rw_a_up"])
        lam_init = 0.8 - 0.6 * math.exp(-0.3 * 1)
        lq = sb("lq", [128, 4, 64]); lsum = sb("lsum", [128, 2]); nlam = sb("nlam", [128, 1]); lprod = sb("lprod", [128, 2, 64])
        for i, n in enumerate(["df_lq1", "df_lk1", "df_lq2", "df_lk2"]):
            B.dma(lq[:, i, :], I[n].partition_broadcast(128))
        B.tt("dve", lprod[:, 0, :], lq[:, 0, :], lq[:, 1, :], ALU.mult)
        B.tt("dve", lprod[:, 1, :], lq[:, 2, :], lq[:, 3, :], ALU.mult)
        P.op("dve", lambda e: e.tensor_reduce(out=lsum[:], in_=lprod[:], axis=mybir.AxisListType.X, op=ALU.add), [lprod], [lsum])
        B.act(lsum[:], lsum[:], AF.Exp)
        B.tt("dve", nlam[:], lsum[:, 1:2], lsum[:, 0:1], ALU.subtract)
        B.ts("dve", nlam[:], nlam[:], -lam_init, ALU.add)

        with contextlib.ExitStack() as st0:
            cin = [st0.enter_context(nc.sbuf_tensor("cin%d" % i, [128, 2048], F32)) for i in range(2)]
            cout = [st0.enter_context(nc.sbuf_tensor("cout%d" % i, [128, 2048], BF16)) for i in range(2)]
            ci = 0
            for src, dst, C_ in ((I["rw_in"], WB["rw_in"], RWIN), (I["df_in"], WB["df_in"], DFIN)):
                nch = C_ // 128
                for r0 in range(0, D, 128):
                    d = r0 // 128
                    for c0 in range(0, nch, 16):
                        n = min(16, nch - c0)
                        a_, b_ = cin[ci % 2], cout[ci % 2]; ci += 1
                        B.dma(a_[:, 0:n * 128], src[r0:r0 + 128, c0 * 128:(c0 + n) * 128])
                        B.copy("act" if ci % 2 else "dve", b_[:, 0:n * 128], a_[:, 0:n * 128])
                        B.dma(dst[c0:c0 + n, :, d * 128:(d + 1) * 128].rearrange("c p x -> p c x"),
                              b_[:, 0:n * 128].rearrange("p (c x) -> p c x", x=128))
            wsrc2 = I["w_out"].rearrange("l r c -> (l r) c")
            for r0 in range(0, 2 * BR, 256):
                a_, b_ = cin[ci % 2], cout[ci % 2]; ci += 1
                B.dma(a_[:].rearrange("p (r c) -> p r c", r=2), wsrc2[r0:r0 + 256, :].rearrange("(r p) c -> p r c", p=128))
                B.copy("act" if ci % 2 else "dve", b_[:], a_[:])
                B.dma(WB["w_out"][r0:r0 + 256, :].rearrange("(r p) c -> p r c", p=128), b_[:].rearrange("p (r c) -> p r c", r=2))
        P.barrier()
        st1 = contextlib.ExitStack()
        sb = lambda n, s, dt=F32: st1.enter_context(nc.sbuf_tensor(n, list(s), dt))
        xt = sb("xt", [128, D]); xn = sb("xn", [128, D]); hT = sb("hT", [128, 8, 128], BF16)
        rstat = sb("rstat", [128, 4])
        NWB = 4
        Wb = [sb("W%d" % i, [128, 4, 1024], BF16) for i in range(NWB)]
        GTb = sb("GTb", [128, 16, 128], BF16); YTb = sb("YTb", [128, 12, 128], BF16)
        raw = [sb("raw0", [128, 130]), sb("raw1", [128, 130])]
        dif = [sb("dif0", [128, 128]), sb("dif1", [128, 128])]
        PS = sb("PS", [128, 37, 128])
        plast = sb("plast", [128, 37]); plstg = stgA
        SG = sb("SG", [128, 12, 128]); AA = sb("AA", [128, 12, 128]); KK = sb("KK", [128, 12, 128])
        CS = sb("CS", [128, 12, 128]); EP = sb("EP", [128, 12, 128]); EM = sb("EM", [128, 12, 128]); EQ = sb("EQ", [128, 12, 128])
        TMP = sb("TMP", [128, 12, 128]); BON = sb("BON", [128, 12, 128]); YT = SG
        onesrow = sb("onesrow", [128, 128])
        ARs = [sb("AR_%d" % i, [128, 256]) for i in range(2)]
        AZs = [[sb("AZ0_%d" % i, [128, 128]), sb("AZ1_%d" % i, [128, 128])] for i in range(2)]
        BZs = [[sb("BZ0_%d" % i, [128, 128]), sb("BZ1_%d" % i, [128, 128])] for i in range(2)]
        KZs = [[sb("KZ0_%d" % i, [128, 128]), sb("KZ1_%d" % i, [128, 128])] for i in range(2)]
        BGs = [sb("BG_%d" % i, [128, 256]) for i in range(2)]
        TMXs = [sb("TMX_%d" % i, [128, 3, 128]) for i in range(2)]
        AZ = AZs[0] + AZs[1]; BZ = BZs[0] + BZs[1]; KZ = KZs[0] + KZs[1]
        MA = [sb("MA0", [128, 256]), sb("MA1", [128, 256])]
        MB = [sb("MB0", [128, 256]), sb("MB1", [128, 256])]
        PXa = [[sb("PX%d%d" % (h, i), [128, 192]) for i in range(2)] for h in range(2)]
        PTa = [[sb("PT%d%d" % (h, i), [128, 128]) for i in range(2)] for h in range(2)]
        UP = sb("UP", [128, 128]); STMP = sb("STMP", [128, 128])
        Hone = sb("Hstate", [128, 12, 128])
        Hbd = {sq["name"]: Hone for sq in seqs}
        GT = sb("GT", [128, 16, 128]); QM = sb("QM", [128, 4, 128]); EE = sb("EE", [128, 2, 128])
        YM = sb("YM", [128, 4, 128]); RD = sb("RD", [128, 128])
        x1t = sb("x1t", [128, D]); KVT = sb("KVT", [128, 512])
        mkT = [sb("mkT0", [128, 4, NMEM]), sb("mkT1", [128, 4, NMEM])]
        mvS = [sb("mv0", [128, 2, MEMW]), sb("mv1", [128, 2, MEMW])]
        memx = GT[:].rearrange("p a b -> p (a b)").rearrange("p (m d) -> p m d", m=2)
        memh = PS[:, 0:16, :].rearrange("p a b -> p (a b)").rearrange("p (d m) -> p d m", d=8)
        memo = TMP[:, 0:8, :].rearrange("p a b -> p (a b)").rearrange("p (m f) -> p m f", m=2)
        Wf = [BON[:, 0:8, :], CS[:, 0:8, :]]
        B.memset("pool", onesrow[:], 1.0)
        for t in AZ + BZ + KZ:
            B.memset("pool", t[:], 0.0)

        def rstd_from_ss(ss, n, TT, eps):
            B.ts("dve", ss, ss, 1.0 / n, ALU.mult, eps, ALU.add)
            B.act(ss, ss, AF.Ln)
            B.act(ss, ss, AF.Exp, scale=-0.5)

        def rms_to_hT(xsrc, TT, nwc, hdst, junk):
            B.act(junk[:TT, :], xsrc[:TT, :], AF.Square, accum_out=rstat[:TT, 0:1])
            rstd_from_ss(rstat[:TT, 0:1], D, TT, EPS)
            B.ts("dve", junk[:TT, :], xsrc[:TT, :], rstat[:TT, 0:1], ALU.mult)
            for g in range(2):
                for d4 in range(4):
                    d = g * 4 + d4
                    B.tr(ps[g][:, d4 * 128:d4 * 128 + TT], junk[:TT, d * 128:(d + 1) * 128])
                for d4 in range(4):
                    d = g * 4 + d4
                    B.ts("dve", hdst[:, d, 0:TT], ps[g][:, d4 * 128:d4 * 128 + TT], nwc[:, d:d + 1], ALU.mult)

        wctr = [0]

        def proj_chunks(Wsrc, f0, nf, hsrc, TT, consume):
            f = f0
            while f < f0 + nf:
                n = min(4, f0 + nf - f)
                wb = Wb[wctr[0] % NWB]; pb = ps[wctr[0] % 2]; wctr[0] += 1
                B.dma(wb[:, 0:n, :], Wsrc[f:f + n].rearrange("c p x -> p c x"))
                for j in range(n):
                    for d in range(8):
                        B.mm(pb[:, j * 128:j * 128 + TT], wb[:, j, d * 128:(d + 1) * 128], hsrc[:, d, 0:TT],
                             start=(d == 0), stop=(d == 7))
                    consume(f + j, pb[:, j * 128:j * 128 + TT])
                f += n

        def mem_attn(qm, TT, li, ymdst):
            for h in range(4):
                for m in range(2):
                    B.mm(ps[2][:, m * 128:m * 128 + TT], mkT[li][:, h, m * 128:(m + 1) * 128], qm[:, h, 0:TT])
                    B.act(EE[:, m, 0:TT], ps[2][:, m * 128:m * 128 + TT], AF.Exp, scale=128 ** -0.5)
                for m in range(2):
                    B.mm(ps[3][:, 0:TT], mvS[li][:, m, h * 128:(h + 1) * 128], EE[:, m, 0:TT], start=(m == 0), stop=(m == 1))
                for m in range(2):
                    B.mm(ps[3][:, 128:128 + TT], ones[:], EE[:, m, 0:TT], start=(m == 0), stop=(m == 1))
                B.recip(RD[:, 0:TT], ps[3][:, 128:128 + TT])
                B.tt("dve", ymdst[:, h, 0:TT], ps[3][:, 0:TT], RD[:, 0:TT], ALU.mult)

        def out_proj(li, c0, nchunks, TT, xres, xdst):
            c = 0
            first = True
            while c < nchunks:
                n = min(4, nchunks - c)
                wb = Wb[wctr[0] % NWB]; wctr[0] += 1
                wv = wb[:].rearrange("p a b -> p (a b)")
                r0 = li * BR + (c0 + c) * 128
                B.dma(wv[:, 0:n * D].rearrange("p (c f) -> p c f", f=D),
                      WB["w_out"][r0:r0 + n * 128, :].rearrange("(c p) f -> p c f", p=128))
                for j in range(n):
                    for hf in range(2):
                        B.mm(ps[4 + hf][0:TT, :], GTb[:, c0 + c + j, 0:TT], wv[:, j * D + hf * 512:j * D + hf * 512 + 512],
                             start=first, stop=(c + j == nchunks - 1))
                    first = False
                c += n
            for hf in range(2):
                B.tt("dve", xdst[0:TT, hf * 512:(hf + 1) * 512], ps[4 + hf][0:TT, :], xres[0:TT, hf * 512:(hf + 1) * 512], ALU.add)

        def phase_mem(sq):
            b = sq["b"]
            for li in range(2):
                if sq["kind"] == "p":
                    B.dma(memx[:], I["mem_prompt"][b].rearrange("(m p) d -> p m d", p=128))
                    mnw = mnw0 if li == 0 else mnw1
                    for m in range(2):
                        rms_to_hT(memx[:, m, :], 128, mnw, memh[:, :, m * 128:(m + 1) * 128], xn)
                    for h in range(4):
                        wb = Wf[wctr[0] % 2]; wctr[0] += 1
                        B.dma(wb, I["w_mem_k"][li][:, h * 128:(h + 1) * 128].rearrange("(d p) f -> p d f", p=128))
                        for d in range(8):
                            B.mm(ps[2][:, 0:NMEM], wb[:, d, :], memh[:, d, :], start=(d == 0), stop=(d == 7))
                        B.evac(mkT[li][:, h, :], ps[2][:, 0:NMEM])
                    for which, wsrc, odst in ((0, I["w_mem_k"][li], O["p_mk"][li, b]), (1, I["w_mem_v"][li], O["p_mv"][li, b])):
                        for fh in range(4):
                            wb = Wf[wctr[0] % 2]; wctr[0] += 1
                            B.dma(wb, wsrc[:, fh * 128:(fh + 1) * 128].rearrange("(d p) f -> p d f", p=128))
                            for m in range(2):
                                for d in range(8):
                                    B.mm(ps[3][:, m * 128:(m + 1) * 128], memh[:, d, m * 128:(m + 1) * 128], wb[:, d, :],
                                         start=(d == 0), stop=(d == 7))
                                dst = (mvS[li] if which == 1 else memo)
                                B.evac(dst[:, m, fh * 128:(fh + 1) * 128], ps[3][:, m * 128:(m + 1) * 128])
                        src = mvS[li] if which == 1 else memo
                        B.dma(odst.rearrange("(m p) f -> p m f", p=128), src[:])
                else:
                    B.dma(memo[:], I["cache_mem_k"][li, b].rearrange("(m p) f -> p m f", p=128))
                    B.dma(mvS[li][:], I["cache_mem_v"][li, b].rearrange("(m p) f -> p m f", p=128))
                    for h in range(4):
                        for m in range(2):
                            B.tr(ps[2][:, m * 128:(m + 1) * 128], memo[:, m, h * 128:(h + 1) * 128])
                        B.evac(mkT[li][:, h, :], ps[2][:, 0:NMEM])

        def rwkv_scan(sq, TT):
            C = TT
            H = Hbd[sq["name"]]
            nlev = int(math.ceil(math.log2(C)))
            m2v = mask2[:].rearrange("p (a c) -> p a c", a=2)
            for pr in range(12):
                AR, AZ, BZ, KZ, BG, TMX = ARs[pr % 2], AZs[pr % 2], BZs[pr % 2], KZs[pr % 2], BGs[pr % 2], TMXs[pr % 2]
                rP, kP, vP = PS[:, pr, 0:C], PS[:, 12 + pr, 0:C], PS[:, 24 + pr, 0:C]
                B.stt(AR[:, 0:C], KK[:, pr, 0:C], -1.0, EQ[:, pr, 0:C], ALU.mult, ALU.mult)
                B.tt("pool", AR[:, 128:128 + C], rP, EP[:, pr, 0:C], ALU.mult)
                for h in range(2):
                    hs = slice(h * 64, h * 64 + 64)
                    B.copy("pool", AZ[h][hs, 0:C], AR[hs, 0:C])
                    B.tt("dve" if h else "pool", BZ[h][hs, 0:C], AA[hs, pr, 0:C], EM[hs, pr, 0:C], ALU.mult)
                    B.tt("pool" if h else "dve", KZ[h][hs, 0:C], kP[hs, :], EM[hs, pr, 0:C], ALU.mult)
                gC = EP[:, pr, C - 1:C]
                B.stt(BG[:, 0:C], AA[:, pr, 0:C], gC, EM[:, pr, 0:C], ALU.mult, ALU.mult)
                B.stt(BG[:, 128:128 + C], kP, gC, EM[:, pr, 0:C], ALU.mult, ALU.mult)
                B.tr(ps[3][0:C, 0:128], BG[:, 0:C]); B.tr(ps[3][0:C, 128:256], BG[:, 128:128 + C]); B.tr(ps[3][0:C, 256:384], vP)
                B.evac(TMX[0:C, :, :].rearrange("p a b -> p (a b)"), ps[3][0:C, 0:384])
                Vp = TMX[0:C, 2, :]
                for h in range(2):
                    B.mm(ps[4][0:C, h * 256:h * 256 + 128 + C], BZ[h][:, 0:C], AR[:, 0:128 + C])
                    B.mm(ps[5][0:C, h * 256:h * 256 + 128 + C], KZ[h][:, 0:C], AR[:, 0:128 + C])
                for h in range(2):
                    B.tt("dve", MA[h][0:C, :].rearrange("p (a c) -> p a c", a=2)[:, :, 0:C],
                         ps[4][0:C, h * 256:(h + 1) * 256].rearrange("p (a c) -> p a c", a=2)[:, :, 0:C], m2v[0:C, :, 0:C], ALU.mult)
                    B.tt("dve", MB[h][0:C, :].rearrange("p (a c) -> p a c", a=2)[:, :, 0:C],
                         ps[5][0:C, h * 256:(h + 1) * 256].rearrange("p (a c) -> p a c", a=2)[:, :, 0:C], m2v[0:C, :, 0:C], ALU.mult)
                for h in range(2):
                    B.mm(ps[6][0:C, h * 128:h * 128 + C], AZ[h][:, 0:C], BZ[h][:, 0:C])
                    B.tt("dve", PXa[h][0][0:C, 0:C], ps[6][0:C, h * 128:h * 128 + C], masksl[0:C, 0:C], ALU.mult)
                B.mm(ps[7][0:C, 0:128], AR[:, 0:C], H[:, pr, :], start=True, stop=False)
                for h in range(2):
                    B.mm(ps[7][0:C, h * 64:h * 64 + 64], MB[h][0:C, 0:C], Vp[:, h * 64:h * 64 + 64], start=False, stop=(h == 1))
                for h in range(2):
                    B.evac(PXa[h][0][0:C, 128:192], ps[7][0:C, h * 64:h * 64 + 64])
                cur = [0, 0]
                for lv in range(nlev):
                    last = lv == nlev - 1
                    for h in range(2):
                        Pm = PXa[h][cur[h]]
                        PT = MA[h][0:C, 0:C] if lv == 0 else PTa[h][cur[h]][0:C, 0:C]
                        pa_, pb_ = ps[4 + 2 * h], ps[5 + 2 * h]
                        if last:
                            B.mm(pa_[0:C, 128:192], PT, Pm[0:C, 128:192])
                            B.tt("dve", UP[0:C, h * 64:h * 64 + 64], pa_[0:C, 128:192], Pm[0:C, 128:192], ALU.add)
                        else:
                            nxt = 1 - cur[h]
                            if C == 128:
                                B.mm(pa_[0:C, 0:192], PT, Pm[0:C, 0:192])
                            else:
                                B.mm(pa_[0:C, 0:C], PT, Pm[0:C, 0:C])
                                B.mm(pa_[0:C, 128:192], PT, Pm[0:C, 128:192])
                            B.mm(pb_[0:C, 0:C], Pm[0:C, 0:C], PT)
                            B.copy("act", PXa[h][nxt][0:C, 0:C], pa_[0:C, 0:C])
                            B.tt("dve", PXa[h][nxt][0:C, 128:192], pa_[0:C, 128:192], Pm[0:C, 128:192], ALU.add)
                            B.copy("act" if h else "dve", PTa[h][nxt][0:C, 0:C], pb_[0:C, 0:C])
                            cur[h] = nxt
                for h in range(2):
                    hs = slice(h * 64, h * 64 + 64)
                    B.mm(ps[2][:, h * 128:h * 128 + C], H[:, pr, :], AR[:, 128:128 + C], start=True, stop=False)
                    B.mm(ps[2][:, h * 128:h * 128 + C], UP[0:C, :], MA[h][0:C, 128:128 + C], start=False, stop=False)
                    B.mm(ps[2][:, h * 128:h * 128 + C], Vp, MB[h][0:C, 128:128 + C], start=False, stop=True)
                    B.copy("act", YT[hs, pr, 0:C], ps[2][hs, h * 128:h * 128 + C])
                B.mm(ps[3][:, 384:512], TMX[0:C, 0, :], UP[0:C, :], start=True, stop=False)
                B.mm(ps[3][:, 384:512], TMX[0:C, 1, :], Vp, start=False, stop=True)
                B.tt("dve", STMP[:], ps[3][:, 384:512], bones[:], ALU.mult)
                B.stt(H[:, pr, :], H[:, pr, :], gC, STMP[:], ALU.mult, ALU.add)

        def phase1_tile(sq, ti):
            TT = sq["C"]; t0 = ti * TT; nm = sq["name"]
            B.dma(xt[0:TT, :], sq["x"][t0:t0 + TT, :])
            rms_to_hT(xt, TT, nw0, hT, xn)
            cnt = [0]

            def cons_p(f, pap):
                rw = raw[cnt[0] % 2]; df = dif[cnt[0] % 2]; cnt[0] += 1
                B.copy("pool", rw[:, 0:1], plast[:, f:f + 1])
                B.copy("act", rw[:, 1:TT + 1], pap)
                B.copy("pool", plast[:, f:f + 1], rw[:, TT:TT + 1])
                B.tt("pool", df[:, 0:TT], rw[:, 0:TT], rw[:, 1:TT + 1], ALU.subtract)
                B.stt(PS[:, f, 0:TT], df[:, 0:TT], mu[:, f:f + 1], rw[:, 1:TT + 1], ALU.mult, ALU.add)
            proj_chunks(WB["rw_in"], 0, 37, hT, TT, cons_p)
            B.act(PS[0:64, 36, 0:TT], PS[0:64, 36, 0:TT], AF.Tanh)
            for f in range(12):
                B.mm(ps[2][:, (f % 4) * 128:(f % 4) * 128 + TT], WUP[:, f * 128:(f + 1) * 128], PS[:, 36, 0:TT])
                B.act(SG[:, f, 0:TT], ps[2][:, (f % 4) * 128:(f % 4) * 128 + TT], AF.Sigmoid, bias=w0[:, f:f + 1])
                B.mm(ps[3][:, (f % 4) * 128:(f % 4) * 128 + TT], AUP[:, f * 128:(f + 1) * 128], PS[:, 36, 0:TT])
                B.act(AA[:, f, 0:TT], ps[3][:, (f % 4) * 128:(f % 4) * 128 + TT], AF.Sigmoid, bias=a0[:, f:f + 1])
            for f in range(12):
                kP = PS[:, 12 + f, 0:TT]
                B.ts("pool", KK[:, f, 0:TT], kP, k_k[:, f:f + 1], ALU.mult)
                B.tt("pool", TMP[:, f, 0:TT], KK[:, f, 0:TT], KK[:, f, 0:TT], ALU.mult)
                B.mm(ps[2][:, (f % 4) * 128:(f % 4) * 128 + TT], bones[:], TMP[:, f, 0:TT])
                B.act(TMP[:, f, 0:TT], ps[2][:, (f % 4) * 128:(f % 4) * 128 + TT], AF.Sqrt)
                B.ts("dve", TMP[:, f, 0:TT], TMP[:, f, 0:TT], 1e-12, ALU.max)
                B.recip(TMP[:, f, 0:TT], TMP[:, f, 0:TT])
                B.tt("dve", KK[:, f, 0:TT], KK[:, f, 0:TT], TMP[:, f, 0:TT], ALU.mult)
                B.ts("dve", TMP[:, f, 0:TT], AA[:, f, 0:TT], -1.0, ALU.add, k_a[:, f:f + 1], ALU.mult)
                B.stt(kP, TMP[:, f, 0:TT], 1.0, kP, ALU.add, ALU.mult)
                B.tt("pool", AA[:, f, 0:TT], KK[:, f, 0:TT], AA[:, f, 0:TT], ALU.mult)
                B.stt(TMP[:, f, 0:TT], PS[:, f, 0:TT], r_k[:, f:f + 1], kP, ALU.mult, ALU.mult)
                B.mm(ps[3][:, (f % 4) * 128:(f % 4) * 128 + TT], bones[:], TMP[:, f, 0:TT])
                B.tt("dve", BON[:, f, 0:TT], ps[3][:, (f % 4) * 128:(f % 4) * 128 + TT], PS[:, 24 + f, 0:TT], ALU.mult)
                B.scan_add(CS[:, f, 0:TT], onesrow[:, 0:TT], SG[:, f, 0:TT])
            B.tt("pool", TMP[:, :, 0:TT], CS[:, :, 0:TT], SG[:, :, 0:TT], ALU.subtract)
            B.act(EP[:, :, 0:TT], CS[:, :, 0:TT], AF.Exp, scale=-DECAY_C)
            B.act(EM[:, :, 0:TT], CS[:, :, 0:TT], AF.Exp, scale=DECAY_C)
            B.act(EQ[:, :, 0:TT], TMP[:, :, 0:TT], AF.Exp, scale=-DECAY_C)
            rwkv_scan(sq, TT)
            for f in range(12):
                pq = ps[2][:, (f % 4) * 128:(f % 4) * 128 + TT]
                B.mm(pq, bones[:], YT[:, f, 0:TT])
                B.stt(YT[:, f, 0:TT], pq, -1.0 / 64, YT[:, f, 0:TT], ALU.mult, ALU.add)
                B.tt("pool", TMP[:, f, 0:TT], YT[:, f, 0:TT], YT[:, f, 0:TT], ALU.mult)
                pq2 = ps[3][:, (f % 4) * 128:(f % 4) * 128 + TT]
                B.mm(pq2, bones[:], TMP[:, f, 0:TT])
                B.ts("dve", TMP[:, f, 0:TT], pq2, 1.0 / 64, ALU.mult, GN_EPS, ALU.add)
                B.act(TMP[:, f, 0:TT], TMP[:, f, 0:TT], AF.Sqrt)
                B.recip(TMP[:, f, 0:TT], TMP[:, f, 0:TT])
                B.tt("dve", YT[:, f, 0:TT], YT[:, f, 0:TT], TMP[:, f, 0:TT], ALU.mult)
                B.ts("dve", YT[:, f, 0:TT], YT[:, f, 0:TT], ln_w[:, f:f + 1], ALU.mult, ln_b[:, f:f + 1], ALU.add)
                B.tt("pool", YT[:, f, 0:TT], YT[:, f, 0:TT], BON[:, f, 0:TT], ALU.add)
            def cons_qg(f, pap):
                if f < 41:
                    B.evac(QM[:, f - 37, 0:TT], pap)
                else:
                    B.act(GT[:, f - 41, 0:TT], pap, AF.Silu)
            proj_chunks(WB["rw_in"], 37, 20, hT, TT, cons_qg)
            mem_attn(QM, TT, 0, YM)
            B.tt("pool", GTb[:, 0:12, 0:TT], YT[:, :, 0:TT], GT[:, 0:12, 0:TT], ALU.mult)
            B.tt("pool", GTb[:, 12:16, 0:TT], YM[:, :, 0:TT], GT[:, 12:16, 0:TT], ALU.mult)
            out_proj(0, 0, 16, TT, xt, x1t)
            rms_to_hT(x1t, TT, nw1, hT, xn)

            def cons_l1(f, pap):
                if f < 12:
                    B.evac(YTb[:, f, 0:TT], pap)
                    B.dma(sq["qT"][f, :, t0:t0 + TT], YTb[:, f, 0:TT])
                elif f < 36:
                    g = (f - 12) % 4
                    B.evac(TMP[:, g, 0:TT], pap)
                    B.tr(ps[3][0:TT, g * 128:(g + 1) * 128], TMP[:, g, 0:TT])
                    if g == 3:
                        B.evac(KVT[0:TT, :], ps[3][0:TT, :])
                        dst = sq["ok"] if f < 24 else sq["ov"]
                        c0 = ((f - 12) % 12 - 3) * 128
                        B.dma(dst[t0:t0 + TT, c0:c0 + 512], KVT[0:TT, :])
                elif f < 40:
                    B.evac(QM[:, f - 36, 0:TT], pap)
                else:
                    B.act(GT[:, f - 40, 0:TT], pap, AF.Silu)
            proj_chunks(WB["df_in"], 0, 56, hT, TT, cons_l1)
            B.dma(sq["sgT"][:, :, t0:t0 + TT].rearrange("c p t -> p c t"), GT[:, 0:12, 0:TT])
            mem_attn(QM, TT, 1, YM)
            B.tt("pool", GTb[:, 12:16, 0:TT], YM[:, :, 0:TT], GT[:, 12:16, 0:TT], ALU.mult)
            out_proj(1, 12, 4, TT, x1t, xt)
            B.dma(sq["x1"][t0:t0 + TT, :], xt[0:TT, :])

        def phase1_seq(sq):
            nm = sq["name"]; b = sq["b"]; H = Hbd[nm]
            phase_mem(sq)
            if sq["kind"] == "p":
                B.memset("pool", plast[:], 0.0)
                B.memset("pool", H[:], 0.0)
            else:
                B.memset("pool", plstg[:], 0.0)
                B.dma(plstg[0:37, :], I["state_shift"][b].rearrange("(c p) -> c p", p=128))
                B.tr(ps[0][:, 0:128], plstg[:])
                B.copy("dve", plast[:], ps[0][:, 0:37])
                B.memset("pool", TMP[:], 0.0)
                for pr in range(12):
                    for h in range(2):
                        B.dma(TMP[h * 64:h * 64 + 64, pr, h * 64:h * 64 + 64], I["state_rwkv"][b, 2 * pr + h])
                for pr in range(12):
                    B.tr(ps[pr % 2][:, 0:128], TMP[:, pr, :])
                    B.evac(H[:, pr, :], ps[pr % 2][:, 0:128])
            for ti in range(sq["T"] // sq["C"]):
                phase1_tile(sq, ti)
            B.tr(ps[0][0:37, 0:128], plast[:])
            B.copy("dve", plstg[0:37, :], ps[0][0:37, 0:128])
            B.dma(sq["oshift"].rearrange("(c p) -> c p", p=128), plstg[0:37, :])
            for pr in range(12):
                B.tr(ps[pr % 2][:, 0:128], H[:, pr, :])
                B.evac(TMP[:, pr, :], ps[pr % 2][:, 0:128])
            for pr in range(12):
                for h in range(2):
                    B.dma(sq["oS"][2 * pr + h], TMP[h * 64:h * 64 + 64, pr, h * 64:h * 64 + 64])

        def phase2(sq):
            T = sq["T"]; past = sq["past"]; b = sq["b"]
            L = past + T
            nkt = (L + 127) // 128
            Lp = nkt * 128
            with contextlib.ExitStack() as st2:
                sb2 = lambda n, s, dt=F32: st2.enter_context(nc.sbuf_tensor(n + sq["name"], list(s), dt))
                Ktm = sb2("Ktm", [128, nkt, 128]); Vtm = sb2("Vtm", [128, nkt, 128]); Vb = sb2("Vb", [128, nkt, 128], BF16)
                KZ1 = sb2("KZ1", [128, Lp], BF16); KZ2 = sb2("KZ2", [128, Lp], BF16); QT = sb2("QT", [128, T], BF16)
                E1 = [sb2("E1a", [128, 512], BF16), sb2("E1b", [128, 512], BF16)]
                E2 = [sb2("E2a", [128, 512], BF16), sb2("E2b", [128, 512], BF16)]
                AC1 = sb2("AC1", [128, 512]); AC2 = sb2("AC2", [128, 512]); R1 = sb2("R1", [128, 512]); R2 = sb2("R2", [128, 512])
                OT = sb2("OT", [128, 512]); SQ2 = sb2("SQ2", [128, 512]); kval = sb2("kval", [128, 1])
                B.memset("pool", KZ1[64:128, :], 0.0); B.memset("pool", KZ2[0:64, :], 0.0)
                lastn = L - (nkt - 1) * 128
                if lastn < 128:
                    B.memset("pool", kval[:], 0.0); B.memset("pool", kval[0:lastn, :], 1.0)
                for h in range(12):
                    if lastn < 128:
                        B.memset("pool", Ktm[:, nkt - 1, :], 0.0); B.memset("pool", Vtm[:, nkt - 1, :], 0.0)
                    def ld(dst, src, n0, ntl):
                        for g0 in range(0, ntl, 8):
                            g = min(8, ntl - g0)
                            B.dma(dst[:, n0 + g0:n0 + g0 + g, :],
                                  src[g0 * 128:(g0 + g) * 128, h * 128:(h + 1) * 128].rearrange("(n p) f -> p n f", p=128))
                    if past:
                        ld(Ktm, I["cache_k"][b], 0, past // 128)
                        ld(Vtm, I["cache_v"][b], 0, past // 128)
                    nk0 = past // 128
                    if T >= 128:
                        ld(Ktm, sq["ok"], nk0, T // 128)
                        ld(Vtm, sq["ov"], nk0, T // 128)
                    else:
                        B.dma(Ktm[0:T, nk0, :], sq["ok"][:, h * 128:(h + 1) * 128], rk=[sq["ok"]])
                        B.dma(Vtm[0:T, nk0, :], sq["ov"][:, h * 128:(h + 1) * 128], rk=[sq["ov"]])
                    B.dma(QT[:], sq["qT"][h], rk=[sq["qT"]])
                    hk = nkt // 2
                    if hk:
                        B.copy("act", Vb[:, 0:hk, :], Vtm[:, 0:hk, :])
                    B.copy("dve", Vb[:, hk:nkt, :], Vtm[:, hk:nkt, :])
                    for kt in range(nkt):
                        B.tr(ps[kt % 2][:, 0:128], Ktm[:, kt, :])
                        B.copy("act", KZ1[0:64, kt * 128:(kt + 1) * 128], ps[kt % 2][0:64, 0:128])
                        B.copy("dve", KZ2[64:128, kt * 128:(kt + 1) * 128], ps[kt % 2][64:128, 0:128])
                    QB = min(512, T)
                    for q0 in range(0, T, QB):
                        kend = (past + q0 + QB + 127) // 128
                        ectr = 0
                        for kt in range(kend):
                            if sq["kind"] == "p":
                                r = kt - q0 // 128
                                c0 = max(0, r) * 128
                            else:
                                r = -1; c0 = 0
                            N = QB - c0
                            e1, e2 = E1[ectr % 2], E2[ectr % 2]; ectr += 1
                            sA, sB = ps[2 * (kt % 2)], ps[2 * (kt % 2) + 1]
                            B.mm(sA[:, c0:QB], KZ1[:, kt * 128:(kt + 1) * 128], QT[:, q0 + c0:q0 + QB])
                            B.mm(sB[:, c0:QB], KZ2[:, kt * 128:(kt + 1) * 128], QT[:, q0 + c0:q0 + QB])
                            B.act(e1[:, c0:QB], sA[:, c0:QB], AF.Exp, scale=0.125)
                            B.act(e2[:, c0:QB], sB[:, c0:QB], AF.Exp, scale=0.125)
                            if r >= 0:
                                B.tt("dve", e1[:, c0:c0 + 128], e1[:, c0:c0 + 128], cmask[:], ALU.mult)
                                B.tt("pool", e2[:, c0:c0 + 128], e2[:, c0:c0 + 128], cmask[:], ALU.mult)
                            if kt == nkt - 1 and lastn < 128:
                                B.ts("pool", e1[:, c0:QB], e1[:, c0:QB], kval[:, 0:1], ALU.mult)
                                B.ts("pool", e2[:, c0:QB], e2[:, c0:QB], kval[:, 0:1], ALU.mult)
                            B.mm(ps[4][:, c0:QB], Vb[:, kt, :], e1[:, c0:QB], start=(kt == 0), stop=(kt == kend - 1))
                            B.mm(ps[5][:, c0:QB], Vb[:, kt, :], e2[:, c0:QB], start=(kt == 0), stop=(kt == kend - 1))
                            if kt == 0:
                                B.copy("dve", AC1[:, 0:QB], e1[:, 0:QB]); B.copy("pool", AC2[:, 0:QB], e2[:, 0:QB])
                            else:
                                B.tt("dve", AC1[:, c0:QB], AC1[:, c0:QB], e1[:, c0:QB], ALU.add)
                                B.tt("pool", AC2[:, c0:QB], AC2[:, c0:QB], e2[:, c0:QB], ALU.add)
                        B.mm(ps[6][:, 0:QB], ones[:], AC1[:, 0:QB]); B.mm(ps[7][:, 0:QB], ones[:], AC2[:, 0:QB])
                        B.recip(R1[:, 0:QB], ps[6][:, 0:QB]); B.recip(R2[:, 0:QB], ps[7][:, 0:QB])
                        B.tt("dve", R1[:, 0:QB], ps[4][:, 0:QB], R1[:, 0:QB], ALU.mult)
                        B.tt("dve", R2[:, 0:QB], ps[5][:, 0:QB], R2[:, 0:QB], ALU.mult)
                        B.stt(OT[:, 0:QB], R2[:, 0:QB], nlam[:, 0:1], R1[:, 0:QB], ALU.mult, ALU.add)
                        B.tt("pool", SQ2[:, 0:QB], OT[:, 0:QB], OT[:, 0:QB], ALU.mult)
                        B.mm(ps[6][:, 0:QB], ones[:], SQ2[:, 0:QB])
                        B.ts("dve", SQ2[:, 0:QB], ps[6][:, 0:QB], 1.0 / 128, ALU.mult, EPS, ALU.add)
                        B.act(SQ2[:, 0:QB], SQ2[:, 0:QB], AF.Sqrt)
                        B.recip(SQ2[:, 0:QB], SQ2[:, 0:QB])
                        B.tt("dve", OT[:, 0:QB], OT[:, 0:QB], SQ2[:, 0:QB], ALU.mult)
                        B.ts("dve", OT[:, 0:QB], OT[:, 0:QB], subln[:, 0:1], ALU.mult, 1.0 - lam_init, ALU.mult)
                        B.dma(sq["oT"][h, :, q0:q0 + QB], OT[:, 0:QB])

        def phase3(sq):
            T = sq["T"]; TT = sq["C"]
            with contextlib.ExitStack() as st3:
                sb3 = lambda n, s, dt=F32: st3.enter_context(nc.sbuf_tensor(n + sq["name"], list(s), dt))
                WO = sb3("WO", [128, 12, D], BF16); GTb = sb3("GTb3", [128, 12, 128], BF16)
                fnw = sb3("fnw", [128, D])
                B.dma(fnw[:], I["final_norm_w"].partition_broadcast(128))
                YT = sb3("YT3", [128, 12, 128]); GT = sb3("GT3", [128, 12, 128]); x1t = sb3("x1t3", [128, D])
                xt = sb3("xt3", [128, D]); xn = sb3("xn3", [128, D]); rstat = sb3("rstat3", [128, 4])
                for c4 in range(0, 12, 4):
                    B.dma(WO[:, c4:c4 + 4, :], WB["w_out"][BR + c4 * 128:BR + (c4 + 4) * 128, :].rearrange("(c p) f -> p c f", p=128))
                for ti in range(T // TT):
                    t0 = ti * TT
                    B.dma(YT[:, :, 0:TT], sq["oT"][:, :, t0:t0 + TT].rearrange("c p t -> p c t"), rk=[sq["oT"]])
                    B.dma(GT[:, 0:12, 0:TT], sq["sgT"][:, :, t0:t0 + TT].rearrange("c p t -> p c t"), rk=[sq["sgT"]])
                    B.dma(x1t[0:TT, :], sq["x1"][t0:t0 + TT, :], rk=[sq["x1"]])
                    B.tt("pool", GTb[:, 0:12, 0:TT], YT[:, :, 0:TT], GT[:, 0:12, 0:TT], ALU.mult)
                    for c in range(12):
                        for hf in range(2):
                            B.mm(ps[4 + hf][0:TT, :], GTb[:, c, 0:TT], WO[:, c, hf * 512:(hf + 1) * 512], start=(c == 0), stop=(c == 11))
                    for hf in range(2):
                        B.tt("dve", xt[0:TT, hf * 512:(hf + 1) * 512], ps[4 + hf][0:TT, :], x1t[0:TT, hf * 512:(hf + 1) * 512], ALU.add)
                    B.act(xn[0:TT, :], xt[0:TT, :], AF.Square, accum_out=rstat[0:TT, 0:1])
                    rstd_from_ss(rstat[0:TT, 0:1], D, TT, EPS)
                    B.stt(xn[0:TT, :], xt[0:TT, :], rstat[0:TT, 0:1], fnw[0:TT, :], ALU.mult, ALU.mult)
                    B.dma(sq["y"][t0:t0 + TT, :], xn[0:TT, :])

        for sq in seqs:
            if "1" in PHASES:
                phase1_seq(sq)
        P.barrier()
        st1.close()
        for sq in seqs:
            if "2" in PHASES:
                phase2(sq)
                P.barrier()
        for sq in seqs:
            if "3" in PHASES:
                phase3(sq)
                P.barrier()
        P.emit()
    return nc, P


_IN_ORDER = ["x_prompt", "mem_prompt", "x_sample", "state_rwkv", "state_shift", "cache_k", "cache_v", "cache_mem_k",
             "cache_mem_v"]


def shard_inputs(inputs, n_cores, NB):
    c = host_consts()
    maps = []
    f = lambda a: np.ascontiguousarray(np.asarray(a, dtype=np.float32))
    for ci in range(n_cores):
        sl = slice(ci * NB, (ci + 1) * NB)
        m = {}
        m["x_prompt"] = f(inputs["x_prompt"][sl]); m["mem_prompt"] = f(inputs["mem_prompt"][sl])
        m["x_sample"] = f(inputs["x_sample"][sl])
        m["state_rwkv"] = f(inputs["state_rwkv"][0, sl]); m["state_shift"] = f(inputs["state_shift"][0, sl])
        ck = np.asarray(inputs["cache_k"]); cv = np.asarray(inputs["cache_v"])
        m["cache_k"] = f(ck[0, sl].reshape(NB, ck.shape[2], MIXW)); m["cache_v"] = f(cv[0, sl].reshape(NB, cv.shape[2], MIXW))
        m["cache_mem_k"] = f(np.asarray(inputs["cache_mem_k"])[:, sl].reshape(2, NB, NMEM, MEMW))
        m["cache_mem_v"] = f(np.asarray(inputs["cache_mem_v"])[:, sl].reshape(2, NB, NMEM, MEMW))
        for n in ["norm_w", "mem_norm_w", "w_mem_k", "w_mem_v", "w_out", "final_norm_w"]:
            m[n] = f(inputs[n])
        for n in ["rw_in", "rw_mu", "rw_w0", "rw_w_up", "rw_a0", "rw_a_up", "rw_k_k", "rw_k_a", "rw_ln_w", "rw_ln_b",
                  "df_in", "df_lq1", "df_lk1", "df_lq2", "df_lk2", "df_subln"]:
            m[n] = f(np.asarray(inputs[n])[0])
        m["rw_r_k"] = f(np.asarray(inputs["rw_r_k"])[0].reshape(MIXW))
        m.update({"c_" + k: v for k, v in c.items()})
        maps.append(m)
    return maps


def gather_outputs(results, NB, TP, TS):
    cat = lambda n, ax: np.concatenate([np.asarray(r[n]) for r in results], axis=ax)
    nb = NB * len(results)
    y_p = cat("y_prompt", 0); y_s = cat("y_sample", 0)
    p_S = cat("p_S", 0)[None]; p_sh = cat("p_shift", 0)[None]
    p_k = cat("p_k", 0).reshape(1, nb, TP, 12, 128); p_v = cat("p_v", 0).reshape(1, nb, TP, 12, 128)
    p_mk = cat("p_mk", 1).reshape(2, nb, NMEM, 4, 128); p_mv = cat("p_mv", 1).reshape(2, nb, NMEM, 4, 128)
    s_S = cat("s_S", 0)[None]; s_sh = cat("s_shift", 0)[None]
    s_k = cat("s_k", 0).reshape(1, nb, TS, 12, 128); s_v = cat("s_v", 0).reshape(1, nb, TS, 12, 128)
    return (y_p, y_s, p_S, p_sh, p_k, p_v, p_mk, p_mv, s_S, s_sh, s_k, s_v)


def kernel(**inputs):
    n_cores = 8
    xp = np.asarray(inputs["x_prompt"]); xs = np.asarray(inputs["x_sample"])
    NB = xp.shape[0] // n_cores
    TP, TS = xp.shape[1], xs.shape[1]
    PAST = np.asarray(inputs["cache_k"]).shape[2]
    nc, _ = build(TP, TS, PAST, NB)
    maps = shard_inputs(inputs, n_cores, NB)
    res = run_bass_kernel_spmd(nc, maps, core_ids=list(range(n_cores)))
    return gather_outputs(res.results, NB, TP, TS)
```

```python
import contextlib
import math
import numpy as np
import concourse.bass as bass
import concourse.mybir as mybir
from concourse.bass_utils import run_bass_kernel_spmd

F32 = mybir.dt.float32
BF16 = mybir.dt.bfloat16
AF = mybir.ActivationFunctionType
ALU = mybir.AluOpType

ENGS = ["pe", "dve", "act", "pool", "sp"]
NDMA = 48
import os
POOL_ENG = os.environ.get("POOL_ENG", "pool")

D = 1024
MIXW = 1536
MEMW = 512
BR = 2048
RWP = 4736
RWIN = RWP + MEMW + BR
DFIN = 3 * MIXW + MEMW + BR
NMEM = 256
EPS = 1e-6
GN_EPS = 64e-5
DECAY_C = math.exp(-0.5)


def _key(x):
    if isinstance(x, (str, tuple)):
        return x
    if hasattr(x, "tensor"):
        return x.tensor.name
    return x.name


class Prog:
    def __init__(self, nc, same_engine_sync=True):
        self.nc = nc
        self.ops = {e: [] for e in ENGS}
        self.n = {e: 0 for e in ENGS}
        self.seen = {e: {} for e in ENGS}
        self.lastw = {}
        self.readers = {}
        self.dma_val = [0] * NDMA
        self.dma_next = 0
        self.same = same_engine_sync
        self.nwaits = 0

    def _add(self, eng, waits, tok, is_war=False):
        if tok is None:
            return
        sk, val = tok
        if sk == ("e", eng):
            if eng == "pe" or eng == "sp":
                return
            if not self.same:
                return
        if self.seen[eng].get(sk, 0) >= val:
            return
        if waits.get(sk, 0) < val:
            waits[sk] = val

    def op(self, eng, fn, reads=(), writes=(), dma=False, inc=True, attach=None):
        if eng == "pool" and not dma:
            eng = POOL_ENG
        rk = [_key(r) for r in reads if r is not None]
        wk = [_key(w) for w in writes if w is not None]
        wk += [k for k in rk if isinstance(k, str) and k[:2] == "ps" and k[2:].isdigit() and k not in wk]
        waits = {}
        for k in rk:
            self._add(eng, waits, self.lastw.get(k))
        for k in wk:
            self._add(eng, waits, self.lastw.get(k))
            for sk, val in self.readers.get(k, {}).items():
                self._add(eng, waits, (sk, val), is_war=True)
        if dma:
            slot = self.dma_next
            self.dma_next = (self.dma_next + 1) % NDMA
            prev = self.dma_val[slot]
            if prev > 0:
                self._add(eng, waits, (("d", slot), prev))
            self.dma_val[slot] = prev + 16
            tok = (("d", slot), prev + 16)
            incinfo = ("d", slot)
        else:
            if inc:
                self.n[eng] += 1
                tok = (("e", eng), self.n[eng])
                incinfo = ("e", eng)
            else:
                tok = (("e", eng), self.n[eng] + 1)
                incinfo = None
        for sk, val in waits.items():
            self.seen[eng][sk] = val
        self.nwaits += len(waits)
        self.lastw_prev = {_key(attach): self.lastw.get(_key(attach))} if attach is not None else {}
        for k in wk:
            self.lastw[k] = tok
            self.readers[k] = {}
        for k in rk:
            d = self.readers.setdefault(k, {})
            if d.get(tok[0], 0) < tok[1]:
                d[tok[0]] = tok[1]
        wl = list(waits.items())
        if attach is not None and len(wl) > 1:
            t = self.lastw_prev.get(_key(attach))
            if t is not None:
                wl.sort(key=lambda w: w[0] == t[0])
        self.ops[eng].append((fn, wl, incinfo))
        return tok

    def barrier(self):
        for eng in ENGS:
            waits = {}
            for o in ENGS:
                if o != "sp" and self.n[o] > self.seen[eng].get(("e", o), 0):
                    if not (o == eng and eng == "pe"):
                        waits[("e", o)] = self.n[o]
            for i, v in enumerate(self.dma_val):
                if v > self.seen[eng].get(("d", i), 0):
                    waits[("d", i)] = v
            for sk, val in waits.items():
                self.seen[eng][sk] = val
            self.nwaits += len(waits)
            self.ops[eng].append((None, list(waits.items()), None))

    def emit(self):
        nc = self.nc
        with contextlib.ExitStack() as st:
            esem = {e: st.enter_context(nc.semaphore("s_" + e)) for e in ENGS}
            dsem = [st.enter_context(nc.semaphore("d_%d" % i)) for i in range(NDMA)]
            block = st.enter_context(nc.Block())

            def sem_of(sk):
                return esem[sk[1]] if sk[0] == "e" else dsem[sk[1]]

            fin = [(("e", e), self.n[e]) for e in ENGS if e != "sp" and self.n[e] > 0]
            fin += [(("d", i), v) for i, v in enumerate(self.dma_val) if v > 0]

            def run(e, name):
                for fn, waits, incinfo in self.ops[name]:
                    if fn is None:
                        for sk, val in waits:
                            e.wait_ge(sem_of(sk), val)
                        continue
                    for sk, val in waits:
                        e.wait_ge(sem_of(sk), val)
                    ins = fn(e)
                    if incinfo is not None:
                        ins.then_inc(sem_of(incinfo), 16 if incinfo[0] == "d" else 1)
                if name == "sp":
                    for sk, val in fin:
                        e.wait_ge(sem_of(sk), val)

            @block.tensor
            def _(e):
                run(e, "pe")

            @block.vector
            def _(e):
                run(e, "dve")

            @block.scalar
            def _(e):
                run(e, "act")

            @block.gpsimd
            def _(e):
                run(e, "pool")

            @block.sync
            def _(e):
                run(e, "sp")


class Bld:
    def __init__(self, nc, P, ident):
        self.nc, self.P, self.ident = nc, P, ident
        self.rr = 0

    def mm(self, out, lhsT, rhs, start=True, stop=True, rk=(), wk=None):
        self.P.op("pe", lambda e: e.matmul(out, lhsT, rhs, start=start, stop=stop),
                  [lhsT, rhs] + list(rk), [out] if wk is None else wk, inc=True, attach=lhsT)

    def tr(self, out, in_):
        k = in_.shape[0]
        idn = self.ident[0:k, 0:k]
        self.P.op("pe", lambda e: e.transpose(out, in_, idn), [in_, idn], [out], attach=in_)

    def act(self, out, in_, func, scale=1.0, bias=0.0, accum_out=None, eng="act"):
        r = [in_] + [x for x in (scale, bias) if not isinstance(x, (int, float))]
        w = [out] + ([accum_out] if accum_out is not None else [])
        if accum_out is None:
            fn = lambda e: e.activation(out=out, in_=in_, func=func, bias=bias, scale=scale)
        else:
            fn = lambda e: e.activation(out=out, in_=in_, func=func, bias=bias, scale=scale, accum_out=accum_out)
        self.P.op("act", fn, r, w)

    def tt(self, eng, out, in0, in1, op):
        self.P.op(eng, lambda e: e.tensor_tensor(out=out, in0=in0, in1=in1, op=op), [in0, in1], [out])

    def ts(self, eng, out, in0, s1, op0, s2=None, op1=None):
        r = [in0] + [x for x in (s1, s2) if x is not None and not isinstance(x, (int, float))]
        if op1 is None:
            fn = lambda e: e.tensor_scalar(out=out, in0=in0, scalar1=s1, scalar2=None, op0=op0)
        else:
            fn = lambda e: e.tensor_scalar(out=out, in0=in0, scalar1=s1, scalar2=s2, op0=op0, op1=op1)
        self.P.op(eng, fn, r, [out])

    def stt(self, out, in0, scalar, in1, op0, op1):
        r = [in0, in1] + ([] if isinstance(scalar, (int, float)) else [scalar])
        self.P.op("dve", lambda e: e.scalar_tensor_tensor(out=out, in0=in0, scalar=scalar, in1=in1, op0=op0, op1=op1), r, [out])

    def copy(self, eng, out, in_):
        if eng == "act":
            self.P.op("act", lambda e: e.copy(out=out, in_=in_), [in_], [out])
        else:
            self.P.op(eng, lambda e: e.tensor_copy(out=out, in_=in_), [in_], [out])

    def evac(self, out, in_):
        self.rr ^= 1
        self.copy("act" if self.rr else "dve", out, in_)

    def memset(self, eng, ap, val):
        self.P.op(eng, lambda e: e.memset(ap, val), [], [ap])

    def recip(self, out, in_):
        self.P.op("dve", lambda e: e.reciprocal(out=out, in_=in_), [in_], [out])

    def scan_add(self, out, ones, data, init=0.0):
        self.P.op("dve", lambda e: e.tensor_tensor_scan(out=out, data0=ones, data1=data, initial=init,
                                                        op0=ALU.mult, op1=ALU.add), [ones, data], [out])

    def dma(self, out, in_, eng="sp", rk=None, wk=None):
        self.P.op(eng, lambda e: e.dma_start(out=out, in_=in_), [in_] if rk is None else rk,
                  [out] if wk is None else wk, dma=True)


def host_consts():
    i = np.arange(128)
    c = {}
    c["ident"] = np.eye(128, dtype=np.float32)
    su = (i[None, :] > i[:, None]).astype(np.float32)
    iu = (i[None, :] >= i[:, None]).astype(np.float32)
    c["mask2"] = np.concatenate([su, iu], axis=1)
    c["masksl"] = (i[None, :] < i[:, None]).astype(np.float32)
    c["bones"] = (i[None, :] // 64 == i[:, None] // 64).astype(np.float32)
    c["ones"] = np.ones((128, 128), np.float32)
    c["cmask"] = (i[:, None] < (i[None, :] // 64 + 1) * 64).astype(np.float32)
    return c


def build(TP, TS, PAST, NB=2, PHASES="123"):
    nc = bass.Bass("TRN2", target_bir_lowering=False)
    P = Prog(nc)
    din = lambda n, s: nc.dram_tensor(n, list(s), F32, kind="ExternalInput").ap()
    dout = lambda n, s: nc.dram_tensor(n, list(s), F32, kind="ExternalOutput").ap()
    dscr = lambda n, s, dt=F32: nc.dram_tensor(n, list(s), dt, kind="Internal").ap()
    WB = {"rw_in": dscr("rw_in_b", [RWIN // 128, 128, D], BF16), "df_in": dscr("df_in_b", [DFIN // 128, 128, D], BF16),
          "w_out": dscr("w_out_b", [2 * BR, D], BF16)}

    I = {}
    I["x_prompt"] = din("x_prompt", [NB, TP, D])
    I["mem_prompt"] = din("mem_prompt", [NB, NMEM, D])
    I["x_sample"] = din("x_sample", [NB, TS, D])
    I["state_rwkv"] = din("state_rwkv", [NB, 24, 64, 64])
    I["state_shift"] = din("state_shift", [NB, RWP])
    I["cache_k"] = din("cache_k", [NB, PAST, MIXW])
    I["cache_v"] = din("cache_v", [NB, PAST, MIXW])
    I["cache_mem_k"] = din("cache_mem_k", [2, NB, NMEM, MEMW])
    I["cache_mem_v"] = din("cache_mem_v", [2, NB, NMEM, MEMW])
    for n, s in [("norm_w", [2, D]), ("mem_norm_w", [2, D]), ("w_mem_k", [2, D, MEMW]), ("w_mem_v", [2, D, MEMW]),
                 ("w_out", [2, BR, D]), ("final_norm_w", [D]), ("rw_in", [D, RWIN]), ("rw_mu", [RWP]),
                 ("rw_w0", [MIXW]), ("rw_w_up", [64, MIXW]), ("rw_a0", [MIXW]), ("rw_a_up", [64, MIXW]),
                 ("rw_k_k", [MIXW]), ("rw_k_a", [MIXW]), ("rw_r_k", [MIXW]), ("rw_ln_w", [MIXW]),
                 ("rw_ln_b", [MIXW]), ("df_in", [D, DFIN]), ("df_lq1", [64]), ("df_lk1", [64]),
                 ("df_lq2", [64]), ("df_lk2", [64]), ("df_subln", [128])]:
        I[n] = din(n, s)
    for n, s in [("ident", [128, 128]), ("mask2", [128, 256]), ("masksl", [128, 128]), ("bones", [128, 128]),
                 ("ones", [128, 128]), ("cmask", [128, 128])]:
        I[n] = din("c_" + n, s)
    O = {}
    O["y_prompt"] = dout("y_prompt", [NB, TP, D])
    O["y_sample"] = dout("y_sample", [NB, TS, D])
    O["p_S"] = dout("p_S", [NB, 24, 64, 64])
    O["p_shift"] = dout("p_shift", [NB, RWP])
    O["p_k"] = dout("p_k", [NB, TP, MIXW])
    O["p_v"] = dout("p_v", [NB, TP, MIXW])
    O["p_mk"] = dout("p_mk", [2, NB, NMEM, MEMW])
    O["p_mv"] = dout("p_mv", [2, NB, NMEM, MEMW])
    O["s_S"] = dout("s_S", [NB, 24, 64, 64])
    O["s_shift"] = dout("s_shift", [NB, RWP])
    O["s_k"] = dout("s_k", [NB, TS, MIXW])
    O["s_v"] = dout("s_v", [NB, TS, MIXW])

    seqs = []
    for b in range(NB):
        seqs.append(dict(kind="p", b=b, T=TP, C=128, x=I["x_prompt"][b], past=0, name="p%d" % b,
                         y=O["y_prompt"][b], oS=O["p_S"][b], oshift=O["p_shift"][b], ok=O["p_k"][b], ov=O["p_v"][b]))
    for b in range(NB):
        seqs.append(dict(kind="s", b=b, T=TS, C=TS, x=I["x_sample"][b], past=PAST, name="s%d" % b,
                         y=O["y_sample"][b], oS=O["s_S"][b], oshift=O["s_shift"][b], ok=O["s_k"][b], ov=O["s_v"][b]))
    for sq in seqs:
        T = sq["T"]
        sq["qT"] = dscr("qT_" + sq["name"], [12, 128, T], BF16)
        sq["sgT"] = dscr("sgT_" + sq["name"], [12, 128, T])
        sq["oT"] = dscr("oT_" + sq["name"], [12, 128, T])
        sq["x1"] = dscr("x1_" + sq["name"], [T, D])

    with contextlib.ExitStack() as st:
        sb = lambda n, s: st.enter_context(nc.sbuf_tensor(n, list(s), F32))
        ps = [st.enter_context(nc.psum_tensor("ps%d" % i, [128, 512], F32)) for i in range(8)]
        ident = sb("ident", [128, 128]); mask2 = sb("mask2", [128, 256]); masksl = sb("masksl", [128, 128])
        bones = sb("bones", [128, 128]); ones = sb("ones", [128, 128]); cmask = sb("cmask", [128, 128])
        B = Bld(nc, P, ident)
        for t, n in [(ident, "ident"), (mask2, "mask2"), (masksl, "masksl"), (bones, "bones"), (ones, "ones"), (cmask, "cmask")]:
            B.dma(t[:], I[n])
        stgA = sb("stgA", [128, 128]); stgB = sb("stgB", [128, 128]); colA = sb("colA", [128, 128]); colB = sb("colB", [128, 128])
        B.memset("pool", stgA[:], 0.0); B.memset("pool", stgB[:], 0.0)
        rowsA = [("mu", I["rw_mu"], 37)] + [(n, I["rw_" + n], 12) for n in ["w0", "a0", "k_k", "k_a", "r_k", "ln_w", "ln_b"]]
        rowsB = [("nw0", I["norm_w"][0], 8), ("nw1", I["norm_w"][1], 8), ("mnw0", I["mem_norm_w"][0], 8),
                 ("mnw1", I["mem_norm_w"][1], 8), ("subln", I["df_subln"], 1)]
        cols = {}
        for stg, col, rows in ((stgA, colA, rowsA), (stgB, colB, rowsB)):
            r0 = 0
            for n, src, k in rows:
                B.dma(stg[r0:r0 + k, :], src.rearrange("(c p) -> c p", p=128))
                cols[n] = col[:, r0:r0 + k]
                r0 += k
            B.tr(ps[0][:, 0:128], stg[:])
            B.copy("dve", col[:], ps[0][:, 0:128])
        mu, w0, a0, k_k, k_a, r_k, ln_w, ln_b = [cols[n] for n in ["mu", "w0", "a0", "k_k", "k_a", "r_k", "ln_w", "ln_b"]]
        nw0, nw1, mnw0, mnw1, subln = [cols[n] for n in ["nw0", "nw1", "mnw0", "mnw1", "subln"]]
        WUP = sb("WUP", [128, MIXW]); AUP = sb("AUP", [128, MIXW])
        B.memset("pool", WUP[64:128, :], 0.0); B.memset("pool", AUP[0:64, :], 0.0)
        B.dma(WUP[0:64, :], I["rw_w_up"]); B.dma(AUP[64:128, :], I["rw_a_up"])
        lam_init = 0.8 - 0.6 * math.exp(-0.3 * 1)
        lq = sb("lq", [128, 4, 64]); lsum = sb("lsum", [128, 2]); nlam = sb("nlam", [128, 1]); lprod = sb("lprod", [128, 2, 64])
        for i, n in enumerate(["df_lq1", "df_lk1", "df_lq2", "df_lk2"]):
            B.dma(lq[:, i, :], I[n].partition_broadcast(128))
        B.tt("dve", lprod[:, 0, :], lq[:, 0, :], lq[:, 1, :], ALU.mult)
        B.tt("dve", lprod[:, 1, :], lq[:, 2, :], lq[:, 3, :], ALU.mult)
        P.op("dve", lambda e: e.tensor_reduce(out=lsum[:], in_=lprod[:], axis=mybir.AxisListType.X, op=ALU.add), [lprod], [lsum])
        B.act(lsum[:], lsum[:], AF.Exp)
        B.tt("dve", nlam[:], lsum[:, 1:2], lsum[:, 0:1], ALU.subtract)
        B.ts("dve", nlam[:], nlam[:], -lam_init, ALU.add)

        with contextlib.ExitStack() as st0:
            cin = [st0.enter_context(nc.sbuf_tensor("cin%d" % i, [128, 2048], F32)) for i in range(2)]
            cout = [st0.enter_context(nc.sbuf_tensor("cout%d" % i, [128, 2048], BF16)) for i in range(2)]
            ci = 0
            for src, dst, C_ in ((I["rw_in"], WB["rw_in"], RWIN), (I["df_in"], WB["df_in"], DFIN)):
                nch = C_ // 128
                for r0 in range(0, D, 128):
                    d = r0 // 128
                    for c0 in range(0, nch, 16):
                        n = min(16, nch - c0)
                        a_, b_ = cin[ci % 2], cout[ci % 2]; ci += 1
                        B.dma(a_[:, 0:n * 128], src[r0:r0 + 128, c0 * 128:(c0 + n) * 128])
                        B.copy("act" if ci % 2 else "dve", b_[:, 0:n * 128], a_[:, 0:n * 128])
                        B.dma(dst[c0:c0 + n, :, d * 128:(d + 1) * 128].rearrange("c p x -> p c x"),
                              b_[:, 0:n * 128].rearrange("p (c x) -> p c x", x=128))
            wsrc2 = I["w_out"].rearrange("l r c -> (l r) c")
            for r0 in range(0, 2 * BR, 256):
                a_, b_ = cin[ci % 2], cout[ci % 2]; ci += 1
                B.dma(a_[:].rearrange("p (r c) -> p r c", r=2), wsrc2[r0:r0 + 256, :].rearrange("(r p) c -> p r c", p=128))
                B.copy("act" if ci % 2 else "dve", b_[:], a_[:])
                B.dma(WB["w_out"][r0:r0 + 256, :].rearrange("(r p) c -> p r c", p=128), b_[:].rearrange("p (r c) -> p r c", r=2))
        P.barrier()
        st1 = contextlib.ExitStack()
        sb = lambda n, s, dt=F32: st1.enter_context(nc.sbuf_tensor(n, list(s), dt))
        xt = sb("xt", [128, D]); xn = sb("xn", [128, D]); hT = sb("hT", [128, 8, 128], BF16)
        rstat = sb("rstat", [128, 4])
        NWB = 4
        Wb = [sb("W%d" % i, [128, 4, 1024], BF16) for i in range(NWB)]
        GTb = sb("GTb", [128, 16, 128], BF16); YTb = sb("YTb", [128, 12, 128], BF16)
        raw = [sb("raw0", [128, 130]), sb("raw1", [128, 130])]
        dif = [sb("dif0", [128, 128]), sb("dif1", [128, 128])]
        PS = sb("PS", [128, 37, 128])
        plast = sb("plast", [128, 37]); plstg = stgA
        SG = sb("SG", [128, 12, 128]); AA = sb("AA", [128, 12, 128]); KK = sb("KK", [128, 12, 128])
        CS = sb("CS", [128, 12, 128]); EP = sb("EP", [128, 12, 128]); EM = sb("EM", [128, 12, 128]); EQ = sb("EQ", [128, 12, 128])
        TMP = sb("TMP", [128, 12, 128]); BON = sb("BON", [128, 12, 128]); YT = SG
        onesrow = sb("onesrow", [128, 128])
        ARs = [sb("AR_%d" % i, [128, 256]) for i in range(2)]
        AZs = [[sb("AZ0_%d" % i, [128, 128]), sb("AZ1_%d" % i, [128, 128])] for i in range(2)]
        BZs = [[sb("BZ0_%d" % i, [128, 128]), sb("BZ1_%d" % i, [128, 128])] for i in range(2)]
        KZs = [[sb("KZ0_%d" % i, [128, 128]), sb("KZ1_%d" % i, [128, 128])] for i in range(2)]
        BGs = [sb("BG_%d" % i, [128, 256]) for i in range(2)]
        TMXs = [sb("TMX_%d" % i, [128, 3, 128]) for i in range(2)]
        AZ = AZs[0] + AZs[1]; BZ = BZs[0] + BZs[1]; KZ = KZs[0] + KZs[1]
        MA = [sb("MA0", [128, 256]), sb("MA1", [128, 256])]
        MB = [sb("MB0", [128, 256]), sb("MB1", [128, 256])]
        PXa = [[sb("PX%d%d" % (h, i), [128, 192]) for i in range(2)] for h in range(2)]
        PTa = [[sb("PT%d%d" % (h, i), [128, 128]) for i in range(2)] for h in range(2)]
        UP = sb("UP", [128, 128]); STMP = sb("STMP", [128, 128])
        Hone = sb("Hstate", [128, 12, 128])
        Hbd = {sq["name"]: Hone for sq in seqs}
        GT = sb("GT", [128, 16, 128]); QM = sb("QM", [128, 4, 128]); EE = sb("EE", [128, 2, 128])
        YM = sb("YM", [128, 4, 128]); RD = sb("RD", [128, 128])
        x1t = sb("x1t", [128, D]); KVT = sb("KVT", [128, 512])
        mkT = [sb("mkT0", [128, 4, NMEM]), sb("mkT1", [128, 4, NMEM])]
        mvS = [sb("mv0", [128, 2, MEMW]), sb("mv1", [128, 2, MEMW])]
        memx = GT[:].rearrange("p a b -> p (a b)").rearrange("p (m d) -> p m d", m=2)
        memh = PS[:, 0:16, :].rearrange("p a b -> p (a b)").rearrange("p (d m) -> p d m", d=8)
        memo = TMP[:, 0:8, :].rearrange("p a b -> p (a b)").rearrange("p (m f) -> p m f", m=2)
        Wf = [BON[:, 0:8, :], CS[:, 0:8, :]]
        B.memset("pool", onesrow[:], 1.0)
        for t in AZ + BZ + KZ:
            B.memset("pool", t[:], 0.0)

        def rstd_from_ss(ss, n, TT, eps):
            B.ts("dve", ss, ss, 1.0 / n, ALU.mult, eps, ALU.add)
            B.act(ss, ss, AF.Ln)
            B.act(ss, ss, AF.Exp, scale=-0.5)

        def rms_to_hT(xsrc, TT, nwc, hdst, junk):
            B.act(junk[:TT, :], xsrc[:TT, :], AF.Square, accum_out=rstat[:TT, 0:1])
            rstd_from_ss(rstat[:TT, 0:1], D, TT, EPS)
            B.ts("dve", junk[:TT, :], xsrc[:TT, :], rstat[:TT, 0:1], ALU.mult)
            for g in range(2):
                for d4 in range(4):
                    d = g * 4 + d4
                    B.tr(ps[g][:, d4 * 128:d4 * 128 + TT], junk[:TT, d * 128:(d + 1) * 128])
                for d4 in range(4):
                    d = g * 4 + d4
                    B.ts("dve", hdst[:, d, 0:TT], ps[g][:, d4 * 128:d4 * 128 + TT], nwc[:, d:d + 1], ALU.mult)

        wctr = [0]

        def proj_chunks(Wsrc, f0, nf, hsrc, TT, consume):
            f = f0
            while f < f0 + nf:
                n = min(4, f0 + nf - f)
                wb = Wb[wctr[0] % NWB]; pb = ps[wctr[0] % 2]; wctr[0] += 1
                B.dma(wb[:, 0:n, :], Wsrc[f:f + n].rearrange("c p x -> p c x"))
                for j in range(n):
                    for d in range(8):
                        B.mm(pb[:, j * 128:j * 128 + TT], wb[:, j, d * 128:(d + 1) * 128], hsrc[:, d, 0:TT],
                             start=(d == 0), stop=(d == 7))
                    consume(f + j, pb[:, j * 128:j * 128 + TT])
                f += n

        def mem_attn(qm, TT, li, ymdst):
            for h in range(4):
                for m in range(2):
                    B.mm(ps[2][:, m * 128:m * 128 + TT], mkT[li][:, h, m * 128:(m + 1) * 128], qm[:, h, 0:TT])
                    B.act(EE[:, m, 0:TT], ps[2][:, m * 128:m * 128 + TT], AF.Exp, scale=128 ** -0.5)
                for m in range(2):
                    B.mm(ps[3][:, 0:TT], mvS[li][:, m, h * 128:(h + 1) * 128], EE[:, m, 0:TT], start=(m == 0), stop=(m == 1))
                for m in range(2):
                    B.mm(ps[3][:, 128:128 + TT], ones[:], EE[:, m, 0:TT], start=(m == 0), stop=(m == 1))
                B.recip(RD[:, 0:TT], ps[3][:, 128:128 + TT])
                B.tt("dve", ymdst[:, h, 0:TT], ps[3][:, 0:TT], RD[:, 0:TT], ALU.mult)

        def out_proj(li, c0, nchunks, TT, xres, xdst):
            c = 0
            first = True
            while c < nchunks:
                n = min(4, nchunks - c)
                wb = Wb[wctr[0] % NWB]; wctr[0] += 1
                wv = wb[:].rearrange("p a b -> p (a b)")
                r0 = li * BR + (c0 + c) * 128
                B.dma(wv[:, 0:n * D].rearrange("p (c f) -> p c f", f=D),
                      WB["w_out"][r0:r0 + n * 128, :].rearrange("(c p) f -> p c f", p=128))
                for j in range(n):
                    for hf in range(2):
                        B.mm(ps[4 + hf][0:TT, :], GTb[:, c0 + c + j, 0:TT], wv[:, j * D + hf * 512:j * D + hf * 512 + 512],
                             start=first, stop=(c + j == nchunks - 1))
                    first = False
                c += n
            for hf in range(2):
                B.tt("dve", xdst[0:TT, hf * 512:(hf + 1) * 512], ps[4 + hf][0:TT, :], xres[0:TT, hf * 512:(hf + 1) * 512], ALU.add)

        def phase_mem(sq):
            b = sq["b"]
            for li in range(2):
                if sq["kind"] == "p":
                    B.dma(memx[:], I["mem_prompt"][b].rearrange("(m p) d -> p m d", p=128))
                    mnw = mnw0 if li == 0 else mnw1
                    for m in range(2):
                        rms_to_hT(memx[:, m, :], 128, mnw, memh[:, :, m * 128:(m + 1) * 128], xn)
                    for h in range(4):
                        wb = Wf[wctr[0] % 2]; wctr[0] += 1
                        B.dma(wb, I["w_mem_k"][li][:, h * 128:(h + 1) * 128].rearrange("(d p) f -> p d f", p=128))
                        for d in range(8):
                            B.mm(ps[2][:, 0:NMEM], wb[:, d, :], memh[:, d, :], start=(d == 0), stop=(d == 7))
                        B.evac(mkT[li][:, h, :], ps[2][:, 0:NMEM])
                    for which, wsrc, odst in ((0, I["w_mem_k"][li], O["p_mk"][li, b]), (1, I["w_mem_v"][li], O["p_mv"][li, b])):
                        for fh in range(4):
                            wb = Wf[wctr[0] % 2]; wctr[0] += 1
                            B.dma(wb, wsrc[:, fh * 128:(fh + 1) * 128].rearrange("(d p) f -> p d f", p=128))
                            for m in range(2):
                                for d in range(8):
                                    B.mm(ps[3][:, m * 128:(m + 1) * 128], memh[:, d, m * 128:(m + 1) * 128], wb[:, d, :],
                                         start=(d == 0), stop=(d == 7))
                                dst = (mvS[li] if which == 1 else memo)
                                B.evac(dst[:, m, fh * 128:(fh + 1) * 128], ps[3][:, m * 128:(m + 1) * 128])
                        src = mvS[li] if which == 1 else memo
                        B.dma(odst.rearrange("(m p) f -> p m f", p=128), src[:])
                else:
                    B.dma(memo[:], I["cache_mem_k"][li, b].rearrange("(m p) f -> p m f", p=128))
                    B.dma(mvS[li][:], I["cache_mem_v"][li, b].rearrange("(m p) f -> p m f", p=128))
                    for h in range(4):
                        for m in range(2):
                            B.tr(ps[2][:, m * 128:(m + 1) * 128], memo[:, m, h * 128:(h + 1) * 128])
                        B.evac(mkT[li][:, h, :], ps[2][:, 0:NMEM])

        def rwkv_scan(sq, TT):
            C = TT
            H = Hbd[sq["name"]]
            nlev = int(math.ceil(math.log2(C)))
            m2v = mask2[:].rearrange("p (a c) -> p a c", a=2)
            for pr in range(12):
                AR, AZ, BZ, KZ, BG, TMX = ARs[pr % 2], AZs[pr % 2], BZs[pr % 2], KZs[pr % 2], BGs[pr % 2], TMXs[pr % 2]
                rP, kP, vP = PS[:, pr, 0:C], PS[:, 12 + pr, 0:C], PS[:, 24 + pr, 0:C]
                B.stt(AR[:, 0:C], KK[:, pr, 0:C], -1.0, EQ[:, pr, 0:C], ALU.mult, ALU.mult)
                B.tt("pool", AR[:, 128:128 + C], rP, EP[:, pr, 0:C], ALU.mult)
                for h in range(2):
                    hs = slice(h * 64, h * 64 + 64)
                    B.copy("pool", AZ[h][hs, 0:C], AR[hs, 0:C])
                    B.tt("dve" if h else "pool", BZ[h][hs, 0:C], AA[hs, pr, 0:C], EM[hs, pr, 0:C], ALU.mult)
                    B.tt("pool" if h else "dve", KZ[h][hs, 0:C], kP[hs, :], EM[hs, pr, 0:C], ALU.mult)
                gC = EP[:, pr, C - 1:C]
                B.stt(BG[:, 0:C], AA[:, pr, 0:C], gC, EM[:, pr, 0:C], ALU.mult, ALU.mult)
                B.stt(BG[:, 128:128 + C], kP, gC, EM[:, pr, 0:C], ALU.mult, ALU.mult)
                B.tr(ps[3][0:C, 0:128], BG[:, 0:C]); B.tr(ps[3][0:C, 128:256], BG[:, 128:128 + C]); B.tr(ps[3][0:C, 256:384], vP)
                B.evac(TMX[0:C, :, :].rearrange("p a b -> p (a b)"), ps[3][0:C, 0:384])
                Vp = TMX[0:C, 2, :]
                for h in range(2):
                    B.mm(ps[4][0:C, h * 256:h * 256 + 128 + C], BZ[h][:, 0:C], AR[:, 0:128 + C])
                    B.mm(ps[5][0:C, h * 256:h * 256 + 128 + C], KZ[h][:, 0:C], AR[:, 0:128 + C])
                for h in range(2):
                    B.tt("dve", MA[h][0:C, :].rearrange("p (a c) -> p a c", a=2)[:, :, 0:C],
                         ps[4][0:C, h * 256:(h + 1) * 256].rearrange("p (a c) -> p a c", a=2)[:, :, 0:C], m2v[0:C, :, 0:C], ALU.mult)
                    B.tt("dve", MB[h][0:C, :].rearrange("p (a c) -> p a c", a=2)[:, :, 0:C],
                         ps[5][0:C, h * 256:(h + 1) * 256].rearrange("p (a c) -> p a c", a=2)[:, :, 0:C], m2v[0:C, :, 0:C], ALU.mult)
                for h in range(2):
                    B.mm(ps[6][0:C, h * 128:h * 128 + C], AZ[h][:, 0:C], BZ[h][:, 0:C])
                    B.tt("dve", PXa[h][0][0:C, 0:C], ps[6][0:C, h * 128:h * 128 + C], masksl[0:C, 0:C], ALU.mult)
                B.mm(ps[7][0:C, 0:128], AR[:, 0:C], H[:, pr, :], start=True, stop=False)
                for h in range(2):
                    B.mm(ps[7][0:C, h * 64:h * 64 + 64], MB[h][0:C, 0:C], Vp[:, h * 64:h * 64 + 64], start=False, stop=(h == 1))
                for h in range(2):
                    B.evac(PXa[h][0][0:C, 128:192], ps[7][0:C, h * 64:h * 64 + 64])
                cur = [0, 0]
                for lv in range(nlev):
                    last = lv == nlev - 1
                    for h in range(2):
                        Pm = PXa[h][cur[h]]
                        PT = MA[h][0:C, 0:C] if lv == 0 else PTa[h][cur[h]][0:C, 0:C]
                        pa_, pb_ = ps[4 + 2 * h], ps[5 + 2 * h]
                        if last:
                            B.mm(pa_[0:C, 128:192], PT, Pm[0:C, 128:192])
                            B.tt("dve", UP[0:C, h * 64:h * 64 + 64], pa_[0:C, 128:192], Pm[0:C, 128:192], ALU.add)
                        else:
                            nxt = 1 - cur[h]
                            if C == 128:
                                B.mm(pa_[0:C, 0:192], PT, Pm[0:C, 0:192])
                            else:
                                B.mm(pa_[0:C, 0:C], PT, Pm[0:C, 0:C])
                                B.mm(pa_[0:C, 128:192], PT, Pm[0:C, 128:192])
                            B.mm(pb_[0:C, 0:C], Pm[0:C, 0:C], PT)
                            B.copy("act", PXa[h][nxt][0:C, 0:C], pa_[0:C, 0:C])
                            B.tt("dve", PXa[h][nxt][0:C, 128:192], pa_[0:C, 128:192], Pm[0:C, 128:192], ALU.add)
                            B.copy("act" if h else "dve", PTa[h][nxt][0:C, 0:C], pb_[0:C, 0:C])
                            cur[h] = nxt
                for h in range(2):
                    hs = slice(h * 64, h * 64 + 64)
                    B.mm(ps[2][:, h * 128:h * 128 + C], H[:, pr, :], AR[:, 128:128 + C], start=True, stop=False)
                    B.mm(ps[2][:, h * 128:h * 128 + C], UP[0:C, :], MA[h][0:C, 128:128 + C], start=False, stop=False)
                    B.mm(ps[2][:, h * 128:h * 128 + C], Vp, MB[h][0:C, 128:128 + C], start=False, stop=True)
                    B.copy("act", YT[hs, pr, 0:C], ps[2][hs, h * 128:h * 128 + C])
                B.mm(ps[3][:, 384:512], TMX[0:C, 0, :], UP[0:C, :], start=True, stop=False)
                B.mm(ps[3][:, 384:512], TMX[0:C, 1, :], Vp, start=False, stop=True)
                B.tt("dve", STMP[:], ps[3][:, 384:512], bones[:], ALU.mult)
                B.stt(H[:, pr, :], H[:, pr, :], gC, STMP[:], ALU.mult, ALU.add)

        def phase1_tile(sq, ti):
            TT = sq["C"]; t0 = ti * TT; nm = sq["name"]
            B.dma(xt[0:TT, :], sq["x"][t0:t0 + TT, :])
            rms_to_hT(xt, TT, nw0, hT, xn)
            cnt = [0]

            def cons_p(f, pap):
                rw = raw[cnt[0] % 2]; df = dif[cnt[0] % 2]; cnt[0] += 1
                B.copy("pool", rw[:, 0:1], plast[:, f:f + 1])
                B.copy("act", rw[:, 1:TT + 1], pap)
                B.copy("pool", plast[:, f:f + 1], rw[:, TT:TT + 1])
                B.tt("pool", df[:, 0:TT], rw[:, 0:TT], rw[:, 1:TT + 1], ALU.subtract)
                B.stt(PS[:, f, 0:TT], df[:, 0:TT], mu[:, f:f + 1], rw[:, 1:TT + 1], ALU.mult, ALU.add)
            proj_chunks(WB["rw_in"], 0, 37, hT, TT, cons_p)
            B.act(PS[0:64, 36, 0:TT], PS[0:64, 36, 0:TT], AF.Tanh)
            for f in range(12):
                B.mm(ps[2][:, (f % 4) * 128:(f % 4) * 128 + TT], WUP[:, f * 128:(f + 1) * 128], PS[:, 36, 0:TT])
                B.act(SG[:, f, 0:TT], ps[2][:, (f % 4) * 128:(f % 4) * 128 + TT], AF.Sigmoid, bias=w0[:, f:f + 1])
                B.mm(ps[3][:, (f % 4) * 128:(f % 4) * 128 + TT], AUP[:, f * 128:(f + 1) * 128], PS[:, 36, 0:TT])
                B.act(AA[:, f, 0:TT], ps[3][:, (f % 4) * 128:(f % 4) * 128 + TT], AF.Sigmoid, bias=a0[:, f:f + 1])
            for f in range(12):
                kP = PS[:, 12 + f, 0:TT]
                B.ts("pool", KK[:, f, 0:TT], kP, k_k[:, f:f + 1], ALU.mult)
                B.tt("pool", TMP[:, f, 0:TT], KK[:, f, 0:TT], KK[:, f, 0:TT], ALU.mult)
                B.mm(ps[2][:, (f % 4) * 128:(f % 4) * 128 + TT], bones[:], TMP[:, f, 0:TT])
                B.act(TMP[:, f, 0:TT], ps[2][:, (f % 4) * 128:(f % 4) * 128 + TT], AF.Sqrt)
                B.ts("dve", TMP[:, f, 0:TT], TMP[:, f, 0:TT], 1e-12, ALU.max)
                B.recip(TMP[:, f, 0:TT], TMP[:, f, 0:TT])
                B.tt("dve", KK[:, f, 0:TT], KK[:, f, 0:TT], TMP[:, f, 0:TT], ALU.mult)
                B.ts("dve", TMP[:, f, 0:TT], AA[:, f, 0:TT], -1.0, ALU.add, k_a[:, f:f + 1], ALU.mult)
                B.stt(kP, TMP[:, f, 0:TT], 1.0, kP, ALU.add, ALU.mult)
                B.tt("pool", AA[:, f, 0:TT], KK[:, f, 0:TT], AA[:, f, 0:TT], ALU.mult)
                B.stt(TMP[:, f, 0:TT], PS[:, f, 0:TT], r_k[:, f:f + 1], kP, ALU.mult, ALU.mult)
                B.mm(ps[3][:, (f % 4) * 128:(f % 4) * 128 + TT], bones[:], TMP[:, f, 0:TT])
                B.tt("dve", BON[:, f, 0:TT], ps[3][:, (f % 4) * 128:(f % 4) * 128 + TT], PS[:, 24 + f, 0:TT], ALU.mult)
                B.scan_add(CS[:, f, 0:TT], onesrow[:, 0:TT], SG[:, f, 0:TT])
            B.tt("pool", TMP[:, :, 0:TT], CS[:, :, 0:TT], SG[:, :, 0:TT], ALU.subtract)
            B.act(EP[:, :, 0:TT], CS[:, :, 0:TT], AF.Exp, scale=-DECAY_C)
            B.act(EM[:, :, 0:TT], CS[:, :, 0:TT], AF.Exp, scale=DECAY_C)
            B.act(EQ[:, :, 0:TT], TMP[:, :, 0:TT], AF.Exp, scale=-DECAY_C)
            rwkv_scan(sq, TT)
            for f in range(12):
                pq = ps[2][:, (f % 4) * 128:(f % 4) * 128 + TT]
                B.mm(pq, bones[:], YT[:, f, 0:TT])
                B.stt(YT[:, f, 0:TT], pq, -1.0 / 64, YT[:, f, 0:TT], ALU.mult, ALU.add)
                B.tt("pool", TMP[:, f, 0:TT], YT[:, f, 0:TT], YT[:, f, 0:TT], ALU.mult)
                pq2 = ps[3][:, (f % 4) * 128:(f % 4) * 128 + TT]
                B.mm(pq2, bones[:], TMP[:, f, 0:TT])
                B.ts("dve", TMP[:, f, 0:TT], pq2, 1.0 / 64, ALU.mult, GN_EPS, ALU.add)
                B.act(TMP[:, f, 0:TT], TMP[:, f, 0:TT], AF.Sqrt)
                B.recip(TMP[:, f, 0:TT], TMP[:, f, 0:TT])
                B.tt("dve", YT[:, f, 0:TT], YT[:, f, 0:TT], TMP[:, f, 0:TT], ALU.mult)
                B.ts("dve", YT[:, f, 0:TT], YT[:, f, 0:TT], ln_w[:, f:f + 1], ALU.mult, ln_b[:, f:f + 1], ALU.add)
                B.tt("pool", YT[:, f, 0:TT], YT[:, f, 0:TT], BON[:, f, 0:TT], ALU.add)
            def cons_qg(f, pap):
                if f < 41:
                    B.evac(QM[:, f - 37, 0:TT], pap)
                else:
                    B.act(GT[:, f - 41, 0:TT], pap, AF.Silu)
            proj_chunks(WB["rw_in"], 37, 20, hT, TT, cons_qg)
            mem_attn(QM, TT, 0, YM)
            B.tt("pool", GTb[:, 0:12, 0:TT], YT[:, :, 0:TT], GT[:, 0:12, 0:TT], ALU.mult)
            B.tt("pool", GTb[:, 12:16, 0:TT], YM[:, :, 0:TT], GT[:, 12:16, 0:TT], ALU.mult)
            out_proj(0, 0, 16, TT, xt, x1t)
            rms_to_hT(x1t, TT, nw1, hT, xn)

            def cons_l1(f, pap):
                if f < 12:
                    B.copy("act", YTb[:, f, 0:TT], pap)
                    B.dma(sq["qT"][f, :, t0:t0 + TT], YTb[:, f, 0:TT], eng="act")
                elif f < 36:
                    g = (f - 12) % 4
                    B.evac(TMP[:, g, 0:TT], pap)
                    B.tr(ps[3][0:TT, g * 128:(g + 1) * 128], TMP[:, g, 0:TT])
                    if g == 3:
                        B.copy("act", KVT[0:TT, :], ps[3][0:TT, :])
                        dst = sq["ok"] if f < 24 else sq["ov"]
                        c0 = ((f - 12) % 12 - 3) * 128
                        B.dma(dst[t0:t0 + TT, c0:c0 + 512], KVT[0:TT, :], eng="act")
                elif f < 40:
                    B.evac(QM[:, f - 36, 0:TT], pap)
                else:
                    B.act(GT[:, f - 40, 0:TT], pap, AF.Silu)
            proj_chunks(WB["df_in"], 0, 56, hT, TT, cons_l1)
            B.dma(sq["sgT"][:, :, t0:t0 + TT].rearrange("c p t -> p c t"), GT[:, 0:12, 0:TT], eng="act")
            mem_attn(QM, TT, 1, YM)
            B.tt("pool", GTb[:, 12:16, 0:TT], YM[:, :, 0:TT], GT[:, 12:16, 0:TT], ALU.mult)
            out_proj(1, 12, 4, TT, x1t, xt)
            B.dma(sq["x1"][t0:t0 + TT, :], xt[0:TT, :], eng="act")

        def phase1_seq(sq):
            nm = sq["name"]; b = sq["b"]; H = Hbd[nm]
            phase_mem(sq)
            if sq["kind"] == "p":
                B.memset("pool", plast[:], 0.0)
                B.memset("pool", H[:], 0.0)
            else:
                B.memset("pool", plstg[:], 0.0)
                B.dma(plstg[0:37, :], I["state_shift"][b].rearrange("(c p) -> c p", p=128))
                B.tr(ps[0][:, 0:128], plstg[:])
                B.copy("dve", plast[:], ps[0][:, 0:37])
                B.memset("pool", TMP[:], 0.0)
                for pr in range(12):
                    for h in range(2):
                        B.dma(TMP[h * 64:h * 64 + 64, pr, h * 64:h * 64 + 64], I["state_rwkv"][b, 2 * pr + h])
                for pr in range(12):
                    B.tr(ps[pr % 2][:, 0:128], TMP[:, pr, :])
                    B.evac(H[:, pr, :], ps[pr % 2][:, 0:128])
            for ti in range(sq["T"] // sq["C"]):
                phase1_tile(sq, ti)
            B.tr(ps[0][0:37, 0:128], plast[:])
            B.copy("dve", plstg[0:37, :], ps[0][0:37, 0:128])
            B.dma(sq["oshift"].rearrange("(c p) -> c p", p=128), plstg[0:37, :])
            for pr in range(12):
                B.tr(ps[pr % 2][:, 0:128], H[:, pr, :])
                B.evac(TMP[:, pr, :], ps[pr % 2][:, 0:128])
            for pr in range(12):
                for h in range(2):
                    B.dma(sq["oS"][2 * pr + h], TMP[h * 64:h * 64 + 64, pr, h * 64:h * 64 + 64])

        def phase2(sq):
            T = sq["T"]; past = sq["past"]; b = sq["b"]
            L = past + T
            nkt = (L + 127) // 128
            Lp = nkt * 128
            with contextlib.ExitStack() as st2:
                sb2 = lambda n, s, dt=F32: st2.enter_context(nc.sbuf_tensor(n + sq["name"], list(s), dt))
                Ktm = sb2("Ktm", [128, nkt, 128]); Vtm = sb2("Vtm", [128, nkt, 128]); Vb = sb2("Vb", [128, nkt, 128], BF16)
                KZ1 = sb2("KZ1", [128, Lp], BF16); KZ2 = sb2("KZ2", [128, Lp], BF16); QT = sb2("QT", [128, T], BF16)
                E1 = [sb2("E1a", [128, 512], BF16), sb2("E1b", [128, 512], BF16)]
                E2 = [sb2("E2a", [128, 512], BF16), sb2("E2b", [128, 512], BF16)]
                AC1 = sb2("AC1", [128, 512]); AC2 = sb2("AC2", [128, 512]); R1 = sb2("R1", [128, 512]); R2 = sb2("R2", [128, 512])
                OT = sb2("OT", [128, 512]); SQ2 = sb2("SQ2", [128, 512]); kval = sb2("kval", [128, 1])
                B.memset("pool", KZ1[64:128, :], 0.0); B.memset("pool", KZ2[0:64, :], 0.0)
                lastn = L - (nkt - 1) * 128
                if lastn < 128:
                    B.memset("pool", kval[:], 0.0); B.memset("pool", kval[0:lastn, :], 1.0)
                for h in range(12):
                    if lastn < 128:
                        B.memset("pool", Ktm[:, nkt - 1, :], 0.0); B.memset("pool", Vtm[:, nkt - 1, :], 0.0)
                    def ld(dst, src, n0, ntl):
                        for g0 in range(0, ntl, 8):
                            g = min(8, ntl - g0)
                            B.dma(dst[:, n0 + g0:n0 + g0 + g, :],
                                  src[g0 * 128:(g0 + g) * 128, h * 128:(h + 1) * 128].rearrange("(n p) f -> p n f", p=128))
                    if past:
                        ld(Ktm, I["cache_k"][b], 0, past // 128)
                        ld(Vtm, I["cache_v"][b], 0, past // 128)
                    nk0 = past // 128
                    if T >= 128:
                        ld(Ktm, sq["ok"], nk0, T // 128)
                        ld(Vtm, sq["ov"], nk0, T // 128)
                    else:
                        B.dma(Ktm[0:T, nk0, :], sq["ok"][:, h * 128:(h + 1) * 128], rk=[sq["ok"]])
                        B.dma(Vtm[0:T, nk0, :], sq["ov"][:, h * 128:(h + 1) * 128], rk=[sq["ov"]])
                    B.dma(QT[:], sq["qT"][h], rk=[sq["qT"]])
                    hk = nkt // 2
                    if hk:
                        B.copy("act", Vb[:, 0:hk, :], Vtm[:, 0:hk, :])
                    B.copy("dve", Vb[:, hk:nkt, :], Vtm[:, hk:nkt, :])
                    for kt in range(nkt):
                        B.tr(ps[kt % 2][:, 0:128], Ktm[:, kt, :])
                        B.copy("act", KZ1[0:64, kt * 128:(kt + 1) * 128], ps[kt % 2][0:64, 0:128])
                        B.copy("dve", KZ2[64:128, kt * 128:(kt + 1) * 128], ps[kt % 2][64:128, 0:128])
                    QB = min(512, T)
                    for q0 in range(0, T, QB):
                        kend = (past + q0 + QB + 127) // 128
                        pvA, pvB = (ps[4], ps[5]) if (q0 // QB) % 2 == 0 else (ps[2], ps[3])
                        ectr = 0
                        for kt in range(kend):
                            if sq["kind"] == "p":
                                r = kt - q0 // 128
                                c0 = max(0, r) * 128
                            else:
                                r = -1; c0 = 0
                            N = QB - c0
                            e1, e2 = E1[ectr % 2], E2[ectr % 2]; ectr += 1
                            sA, sB = ps[0], ps[1]
                            B.mm(sA[:, c0:QB], KZ1[:, kt * 128:(kt + 1) * 128], QT[:, q0 + c0:q0 + QB])
                            B.mm(sB[:, c0:QB], KZ2[:, kt * 128:(kt + 1) * 128], QT[:, q0 + c0:q0 + QB])
                            B.act(e1[:, c0:QB], sA[:, c0:QB], AF.Exp, scale=0.125)
                            B.act(e2[:, c0:QB], sB[:, c0:QB], AF.Exp, scale=0.125)
                            if r >= 0:
                                B.tt("dve", e1[:, c0:c0 + 128], e1[:, c0:c0 + 128], cmask[:], ALU.mult)
                                B.tt("pool", e2[:, c0:c0 + 128], e2[:, c0:c0 + 128], cmask[:], ALU.mult)
                            if kt == nkt - 1 and lastn < 128:
                                B.ts("pool", e1[:, c0:QB], e1[:, c0:QB], kval[:, 0:1], ALU.mult)
                                B.ts("pool", e2[:, c0:QB], e2[:, c0:QB], kval[:, 0:1], ALU.mult)
                            B.mm(pvA[:, c0:QB], Vb[:, kt, :], e1[:, c0:QB], start=(kt == 0), stop=(kt == kend - 1))
                            B.mm(pvB[:, c0:QB], Vb[:, kt, :], e2[:, c0:QB], start=(kt == 0), stop=(kt == kend - 1))
                            if kt == 0:
                                B.copy("dve", AC1[:, 0:QB], e1[:, 0:QB]); B.copy("pool", AC2[:, 0:QB], e2[:, 0:QB])
                            else:
                                B.tt("dve", AC1[:, c0:QB], AC1[:, c0:QB], e1[:, c0:QB], ALU.add)
                                B.tt("pool", AC2[:, c0:QB], AC2[:, c0:QB], e2[:, c0:QB], ALU.add)
                        B.mm(ps[6][:, 0:QB], ones[:], AC1[:, 0:QB]); B.mm(ps[7][:, 0:QB], ones[:], AC2[:, 0:QB])
                        B.recip(R1[:, 0:QB], ps[6][:, 0:QB]); B.recip(R2[:, 0:QB], ps[7][:, 0:QB])
                        B.tt("dve", R1[:, 0:QB], pvA[:, 0:QB], R1[:, 0:QB], ALU.mult)
                        B.tt("dve", R2[:, 0:QB], pvB[:, 0:QB], R2[:, 0:QB], ALU.mult)
                        B.stt(OT[:, 0:QB], R2[:, 0:QB], nlam[:, 0:1], R1[:, 0:QB], ALU.mult, ALU.add)
                        B.tt("pool", SQ2[:, 0:QB], OT[:, 0:QB], OT[:, 0:QB], ALU.mult)
                        B.mm(ps[6][:, 0:QB], ones[:], SQ2[:, 0:QB])
                        B.ts("dve", SQ2[:, 0:QB], ps[6][:, 0:QB], 1.0 / 128, ALU.mult, EPS, ALU.add)
                        B.act(SQ2[:, 0:QB], SQ2[:, 0:QB], AF.Sqrt)
                        B.recip(SQ2[:, 0:QB], SQ2[:, 0:QB])
                        B.tt("dve", OT[:, 0:QB], OT[:, 0:QB], SQ2[:, 0:QB], ALU.mult)
                        B.ts("dve", OT[:, 0:QB], OT[:, 0:QB], subln[:, 0:1], ALU.mult, 1.0 - lam_init, ALU.mult)
                        B.dma(sq["oT"][h, :, q0:q0 + QB], OT[:, 0:QB], eng="act")

        def phase3(sq):
            T = sq["T"]; TT = sq["C"]
            with contextlib.ExitStack() as st3:
                sb3 = lambda n, s, dt=F32: st3.enter_context(nc.sbuf_tensor(n + sq["name"], list(s), dt))
                WO = sb3("WO", [128, 12, D], BF16); GTb = sb3("GTb3", [128, 12, 128], BF16)
                fnw = sb3("fnw", [128, D])
                B.dma(fnw[:], I["final_norm_w"].partition_broadcast(128))
                YT = sb3("YT3", [128, 12, 128]); GT = sb3("GT3", [128, 12, 128]); x1t = sb3("x1t3", [128, D])
                xt = sb3("xt3", [128, D]); xn = sb3("xn3", [128, D]); rstat = sb3("rstat3", [128, 4])
                for c4 in range(0, 12, 4):
                    B.dma(WO[:, c4:c4 + 4, :], WB["w_out"][BR + c4 * 128:BR + (c4 + 4) * 128, :].rearrange("(c p) f -> p c f", p=128))
                for ti in range(T // TT):
                    t0 = ti * TT
                    B.dma(YT[:, :, 0:TT], sq["oT"][:, :, t0:t0 + TT].rearrange("c p t -> p c t"), rk=[sq["oT"]])
                    B.dma(GT[:, 0:12, 0:TT], sq["sgT"][:, :, t0:t0 + TT].rearrange("c p t -> p c t"), rk=[sq["sgT"]])
                    B.dma(x1t[0:TT, :], sq["x1"][t0:t0 + TT, :], rk=[sq["x1"]])
                    B.tt("pool", GTb[:, 0:12, 0:TT], YT[:, :, 0:TT], GT[:, 0:12, 0:TT], ALU.mult)
                    for c in range(12):
                        for hf in range(2):
                            B.mm(ps[4 + hf][0:TT, :], GTb[:, c, 0:TT], WO[:, c, hf * 512:(hf + 1) * 512], start=(c == 0), stop=(c == 11))
                    for hf in range(2):
                        B.tt("dve", xt[0:TT, hf * 512:(hf + 1) * 512], ps[4 + hf][0:TT, :], x1t[0:TT, hf * 512:(hf + 1) * 512], ALU.add)
                    B.act(xn[0:TT, :], xt[0:TT, :], AF.Square, accum_out=rstat[0:TT, 0:1])
                    rstd_from_ss(rstat[0:TT, 0:1], D, TT, EPS)
                    B.stt(xn[0:TT, :], xt[0:TT, :], rstat[0:TT, 0:1], fnw[0:TT, :], ALU.mult, ALU.mult)
                    B.dma(sq["y"][t0:t0 + TT, :], xn[0:TT, :], eng="act")

        for sq in seqs:
            if "1" in PHASES:
                phase1_seq(sq)
        P.barrier()
        st1.close()
        for sq in seqs:
            if "2" in PHASES:
                phase2(sq)
                P.barrier()
        for sq in seqs:
            if "3" in PHASES:
                phase3(sq)
                P.barrier()
        P.emit()
    return nc, P


_IN_ORDER = ["x_prompt", "mem_prompt", "x_sample", "state_rwkv", "state_shift", "cache_k", "cache_v", "cache_mem_k",
             "cache_mem_v"]


def shard_inputs(inputs, n_cores, NB):
    c = host_consts()
    maps = []
    f = lambda a: np.ascontiguousarray(np.asarray(a, dtype=np.float32))
    for ci in range(n_cores):
        sl = slice(ci * NB, (ci + 1) * NB)
        m = {}
        m["x_prompt"] = f(inputs["x_prompt"][sl]); m["mem_prompt"] = f(inputs["mem_prompt"][sl])
        m["x_sample"] = f(inputs["x_sample"][sl])
        m["state_rwkv"] = f(inputs["state_rwkv"][0, sl]); m["state_shift"] = f(inputs["state_shift"][0, sl])
        ck = np.asarray(inputs["cache_k"]); cv = np.asarray(inputs["cache_v"])
        m["cache_k"] = f(ck[0, sl].reshape(NB, ck.shape[2], MIXW)); m["cache_v"] = f(cv[0, sl].reshape(NB, cv.shape[2], MIXW))
        m["cache_mem_k"] = f(np.asarray(inputs["cache_mem_k"])[:, sl].reshape(2, NB, NMEM, MEMW))
        m["cache_mem_v"] = f(np.asarray(inputs["cache_mem_v"])[:, sl].reshape(2, NB, NMEM, MEMW))
        for n in ["norm_w", "mem_norm_w", "w_mem_k", "w_mem_v", "w_out", "final_norm_w"]:
            m[n] = f(inputs[n])
        for n in ["rw_in", "rw_mu", "rw_w0", "rw_w_up", "rw_a0", "rw_a_up", "rw_k_k", "rw_k_a", "rw_ln_w", "rw_ln_b",
                  "df_in", "df_lq1", "df_lk1", "df_lq2", "df_lk2", "df_subln"]:
            m[n] = f(np.asarray(inputs[n])[0])
        m["rw_r_k"] = f(np.asarray(inputs["rw_r_k"])[0].reshape(MIXW))
        m.update({"c_" + k: v for k, v in c.items()})
        maps.append(m)
    return maps


def gather_outputs(results, NB, TP, TS):
    cat = lambda n, ax: np.concatenate([np.asarray(r[n]) for r in results], axis=ax)
    nb = NB * len(results)
    y_p = cat("y_prompt", 0); y_s = cat("y_sample", 0)
    p_S = cat("p_S", 0)[None]; p_sh = cat("p_shift", 0)[None]
    p_k = cat("p_k", 0).reshape(1, nb, TP, 12, 128); p_v = cat("p_v", 0).reshape(1, nb, TP, 12, 128)
    p_mk = cat("p_mk", 1).reshape(2, nb, NMEM, 4, 128); p_mv = cat("p_mv", 1).reshape(2, nb, NMEM, 4, 128)
    s_S = cat("s_S", 0)[None]; s_sh = cat("s_shift", 0)[None]
    s_k = cat("s_k", 0).reshape(1, nb, TS, 12, 128); s_v = cat("s_v", 0).reshape(1, nb, TS, 12, 128)
    return (y_p, y_s, p_S, p_sh, p_k, p_v, p_mk, p_mv, s_S, s_sh, s_k, s_v)


def kernel(**inputs):
    n_cores = 8
    xp = np.asarray(inputs["x_prompt"]); xs = np.asarray(inputs["x_sample"])
    NB = xp.shape[0] // n_cores
    TP, TS = xp.shape[1], xs.shape[1]
    PAST = np.asarray(inputs["cache_k"]).shape[2]
    nc, _ = build(TP, TS, PAST, NB)
    maps = shard_inputs(inputs, n_cores, NB)
    res = run_bass_kernel_spmd(nc, maps, core_ids=list(range(n_cores)))
    return gather_outputs(res.results, NB, TP, TS)
```

```python
import contextlib
import math
import numpy as np
import concourse.bass as bass
import concourse.mybir as mybir
from concourse.bass_utils import run_bass_kernel_spmd

F32 = mybir.dt.float32
BF16 = mybir.dt.bfloat16
AF = mybir.ActivationFunctionType
ALU = mybir.AluOpType

ENGS = ["pe", "dve", "act", "pool", "sp"]
NDMA = 48
import os
POOL_ENG = os.environ.get("POOL_ENG", "pool")

D = 1024
MIXW = 1536
MEMW = 512
BR = 2048
RWP = 4736
RWIN = RWP + MEMW + BR
DFIN = 3 * MIXW + MEMW + BR
NMEM = 256
EPS = 1e-6
GN_EPS = 64e-5
DECAY_C = math.exp(-0.5)


def _key(x):
    if isinstance(x, (str, tuple)):
        return x
    if hasattr(x, "tensor"):
        return x.tensor.name
    return x.name


class Prog:
    def __init__(self, nc, same_engine_sync=True):
        self.nc = nc
        self.ops = {e: [] for e in ENGS}
        self.n = {e: 0 for e in ENGS}
        self.seen = {e: {} for e in ENGS}
        self.lastw = {}
        self.readers = {}
        self.dma_val = [0] * NDMA
        self.dma_next = 0
        self.same = same_engine_sync
        self.nwaits = 0

    def _add(self, eng, waits, tok, is_war=False):
        if tok is None:
            return
        sk, val = tok
        if sk == ("e", eng):
            if eng == "pe" or eng == "sp":
                return
            if not self.same:
                return
        if self.seen[eng].get(sk, 0) >= val:
            return
        if waits.get(sk, 0) < val:
            waits[sk] = val

    def op(self, eng, fn, reads=(), writes=(), dma=False, inc=True, attach=None):
        if eng == "pool" and not dma:
            eng = POOL_ENG
        rk = [_key(r) for r in reads if r is not None]
        wk = [_key(w) for w in writes if w is not None]
        wk += [k for k in rk if isinstance(k, str) and k[:2] == "ps" and k[2:].isdigit() and k not in wk]
        waits = {}
        for k in rk:
            self._add(eng, waits, self.lastw.get(k))
        for k in wk:
            self._add(eng, waits, self.lastw.get(k))
            for sk, val in self.readers.get(k, {}).items():
                self._add(eng, waits, (sk, val), is_war=True)
        if dma:
            slot = self.dma_next
            self.dma_next = (self.dma_next + 1) % NDMA
            prev = self.dma_val[slot]
            if prev > 0:
                self._add(eng, waits, (("d", slot), prev))
            self.dma_val[slot] = prev + 16
            tok = (("d", slot), prev + 16)
            incinfo = ("d", slot)
        else:
            if inc:
                self.n[eng] += 1
                tok = (("e", eng), self.n[eng])
                incinfo = ("e", eng)
            else:
                tok = (("e", eng), self.n[eng] + 1)
                incinfo = None
        for sk, val in waits.items():
            self.seen[eng][sk] = val
        self.nwaits += len(waits)
        self.lastw_prev = {_key(attach): self.lastw.get(_key(attach))} if attach is not None else {}
        for k in wk:
            self.lastw[k] = tok
            self.readers[k] = {}
        for k in rk:
            d = self.readers.setdefault(k, {})
            if d.get(tok[0], 0) < tok[1]:
                d[tok[0]] = tok[1]
        wl = list(waits.items())
        if attach is not None and len(wl) > 1:
            t = self.lastw_prev.get(_key(attach))
            if t is not None:
                wl.sort(key=lambda w: w[0] == t[0])
        self.ops[eng].append((fn, wl, incinfo))
        return tok

    def barrier(self):
        for eng in ENGS:
            waits = {}
            for o in ENGS:
                if o != "sp" and self.n[o] > self.seen[eng].get(("e", o), 0):
                    if not (o == eng and eng == "pe"):
                        waits[("e", o)] = self.n[o]
            for i, v in enumerate(self.dma_val):
                if v > self.seen[eng].get(("d", i), 0):
                    waits[("d", i)] = v
            for sk, val in waits.items():
                self.seen[eng][sk] = val
            self.nwaits += len(waits)
            self.ops[eng].append((None, list(waits.items()), None))

    def emit(self):
        nc = self.nc
        with contextlib.ExitStack() as st:
            esem = {e: st.enter_context(nc.semaphore("s_" + e)) for e in ENGS}
            dsem = [st.enter_context(nc.semaphore("d_%d" % i)) for i in range(NDMA)]
            block = st.enter_context(nc.Block())

            def sem_of(sk):
                return esem[sk[1]] if sk[0] == "e" else dsem[sk[1]]

            fin = [(("e", e), self.n[e]) for e in ENGS if e != "sp" and self.n[e] > 0]
            fin += [(("d", i), v) for i, v in enumerate(self.dma_val) if v > 0]

            def run(e, name):
                for fn, waits, incinfo in self.ops[name]:
                    if fn is None:
                        for sk, val in waits:
                            e.wait_ge(sem_of(sk), val)
                        continue
                    for sk, val in waits:
                        e.wait_ge(sem_of(sk), val)
                    ins = fn(e)
                    if incinfo is not None:
                        ins.then_inc(sem_of(incinfo), 16 if incinfo[0] == "d" else 1)
                if name == "sp":
                    for sk, val in fin:
                        e.wait_ge(sem_of(sk), val)

            @block.tensor
            def _(e):
                run(e, "pe")

            @block.vector
            def _(e):
                run(e, "dve")

            @block.scalar
            def _(e):
                run(e, "act")

            @block.gpsimd
            def _(e):
                run(e, "pool")

            @block.sync
            def _(e):
                run(e, "sp")


class Bld:
    def __init__(self, nc, P, ident):
        self.nc, self.P, self.ident = nc, P, ident
        self.rr = 0

    def mm(self, out, lhsT, rhs, start=True, stop=True, rk=(), wk=None):
        self.P.op("pe", lambda e: e.matmul(out, lhsT, rhs, start=start, stop=stop),
                  [lhsT, rhs] + list(rk), [out] if wk is None else wk, inc=True, attach=lhsT)

    def tr(self, out, in_):
        k = in_.shape[0]
        idn = self.ident[0:k, 0:k]
        self.P.op("pe", lambda e: e.transpose(out, in_, idn), [in_, idn], [out], attach=in_)

    def act(self, out, in_, func, scale=1.0, bias=0.0, accum_out=None, eng="act"):
        r = [in_] + [x for x in (scale, bias) if not isinstance(x, (int, float))]
        w = [out] + ([accum_out] if accum_out is not None else [])
        if accum_out is None:
            fn = lambda e: e.activation(out=out, in_=in_, func=func, bias=bias, scale=scale)
        else:
            fn = lambda e: e.activation(out=out, in_=in_, func=func, bias=bias, scale=scale, accum_out=accum_out)
        self.P.op("act", fn, r, w)

    def tt(self, eng, out, in0, in1, op):
        self.P.op(eng, lambda e: e.tensor_tensor(out=out, in0=in0, in1=in1, op=op), [in0, in1], [out])

    def ts(self, eng, out, in0, s1, op0, s2=None, op1=None):
        r = [in0] + [x for x in (s1, s2) if x is not None and not isinstance(x, (int, float))]
        if op1 is None:
            fn = lambda e: e.tensor_scalar(out=out, in0=in0, scalar1=s1, scalar2=None, op0=op0)
        else:
            fn = lambda e: e.tensor_scalar(out=out, in0=in0, scalar1=s1, scalar2=s2, op0=op0, op1=op1)
        self.P.op(eng, fn, r, [out])

    def stt(self, out, in0, scalar, in1, op0, op1):
        r = [in0, in1] + ([] if isinstance(scalar, (int, float)) else [scalar])
        self.P.op("dve", lambda e: e.scalar_tensor_tensor(out=out, in0=in0, scalar=scalar, in1=in1, op0=op0, op1=op1), r, [out])

    def copy(self, eng, out, in_):
        if eng == "act":
            self.P.op("act", lambda e: e.copy(out=out, in_=in_), [in_], [out])
        else:
            self.P.op(eng, lambda e: e.tensor_copy(out=out, in_=in_), [in_], [out])

    def evac(self, out, in_):
        self.rr ^= 1
        self.copy("act" if self.rr else "dve", out, in_)

    def memset(self, eng, ap, val):
        self.P.op(eng, lambda e: e.memset(ap, val), [], [ap])

    def recip(self, out, in_):
        self.P.op("dve", lambda e: e.reciprocal(out=out, in_=in_), [in_], [out])

    def scan_add(self, out, ones, data, init=0.0):
        self.P.op("dve", lambda e: e.tensor_tensor_scan(out=out, data0=ones, data1=data, initial=init,
                                                        op0=ALU.mult, op1=ALU.add), [ones, data], [out])

    def dma(self, out, in_, eng="sp", rk=None, wk=None):
        self.P.op(eng, lambda e: e.dma_start(out=out, in_=in_), [in_] if rk is None else rk,
                  [out] if wk is None else wk, dma=True)


def host_consts():
    i = np.arange(128)
    c = {}
    c["ident"] = np.eye(128, dtype=np.float32)
    su = (i[None, :] > i[:, None]).astype(np.float32)
    iu = (i[None, :] >= i[:, None]).astype(np.float32)
    c["mask2"] = np.concatenate([su, iu], axis=1)
    c["masksl"] = (i[None, :] < i[:, None]).astype(np.float32)
    c["bones"] = (i[None, :] // 64 == i[:, None] // 64).astype(np.float32)
    c["ones"] = np.ones((128, 128), np.float32)
    c["cmask"] = (i[:, None] < (i[None, :] // 64 + 1) * 64).astype(np.float32)
    return c


def build(TP, TS, PAST, NB=2, PHASES="123"):
    nc = bass.Bass("TRN2", target_bir_lowering=False)
    P = Prog(nc)
    din = lambda n, s: nc.dram_tensor(n, list(s), F32, kind="ExternalInput").ap()
    dout = lambda n, s: nc.dram_tensor(n, list(s), F32, kind="ExternalOutput").ap()
    dscr = lambda n, s, dt=F32: nc.dram_tensor(n, list(s), dt, kind="Internal").ap()
    WB = {"rw_in": dscr("rw_in_b", [RWIN // 128, 128, D], BF16), "df_in": dscr("df_in_b", [DFIN // 128, 128, D], BF16),
          "w_out": dscr("w_out_b", [2 * BR, D], BF16)}

    I = {}
    I["x_prompt"] = din("x_prompt", [NB, TP, D])
    I["mem_prompt"] = din("mem_prompt", [NB, NMEM, D])
    I["x_sample"] = din("x_sample", [NB, TS, D])
    I["state_rwkv"] = din("state_rwkv", [NB, 24, 64, 64])
    I["state_shift"] = din("state_shift", [NB, RWP])
    I["cache_k"] = din("cache_k", [NB, PAST, MIXW])
    I["cache_v"] = din("cache_v", [NB, PAST, MIXW])
    I["cache_mem_k"] = din("cache_mem_k", [2, NB, NMEM, MEMW])
    I["cache_mem_v"] = din("cache_mem_v", [2, NB, NMEM, MEMW])
    for n, s in [("norm_w", [2, D]), ("mem_norm_w", [2, D]), ("w_mem_k", [2, D, MEMW]), ("w_mem_v", [2, D, MEMW]),
                 ("w_out", [2, BR, D]), ("final_norm_w", [D]), ("rw_in", [D, RWIN]), ("rw_mu", [RWP]),
                 ("rw_w0", [MIXW]), ("rw_w_up", [64, MIXW]), ("rw_a0", [MIXW]), ("rw_a_up", [64, MIXW]),
                 ("rw_k_k", [MIXW]), ("rw_k_a", [MIXW]), ("rw_r_k", [MIXW]), ("rw_ln_w", [MIXW]),
                 ("rw_ln_b", [MIXW]), ("df_in", [D, DFIN]), ("df_lq1", [64]), ("df_lk1", [64]),
                 ("df_lq2", [64]), ("df_lk2", [64]), ("df_subln", [128])]:
        I[n] = din(n, s)
    for n, s in [("ident", [128, 128]), ("mask2", [128, 256]), ("masksl", [128, 128]), ("bones", [128, 128]),
                 ("ones", [128, 128]), ("cmask", [128, 128])]:
        I[n] = din("c_" + n, s)
    O = {}
    O["y_prompt"] = dout("y_prompt", [NB, TP, D])
    O["y_sample"] = dout("y_sample", [NB, TS, D])
    O["p_S"] = dout("p_S", [NB, 24, 64, 64])
    O["p_shift"] = dout("p_shift", [NB, RWP])
    O["p_k"] = dout("p_k", [NB, TP, MIXW])
    O["p_v"] = dout("p_v", [NB, TP, MIXW])
    O["p_mk"] = dout("p_mk", [2, NB, NMEM, MEMW])
    O["p_mv"] = dout("p_mv", [2, NB, NMEM, MEMW])
    O["s_S"] = dout("s_S", [NB, 24, 64, 64])
    O["s_shift"] = dout("s_shift", [NB, RWP])
    O["s_k"] = dout("s_k", [NB, TS, MIXW])
    O["s_v"] = dout("s_v", [NB, TS, MIXW])

    seqs = []
    for b in range(NB):
        seqs.append(dict(kind="p", b=b, T=TP, C=128, x=I["x_prompt"][b], past=0, name="p%d" % b,
                         y=O["y_prompt"][b], oS=O["p_S"][b], oshift=O["p_shift"][b], ok=O["p_k"][b], ov=O["p_v"][b]))
    for b in range(NB):
        seqs.append(dict(kind="s", b=b, T=TS, C=TS, x=I["x_sample"][b], past=PAST, name="s%d" % b,
                         y=O["y_sample"][b], oS=O["s_S"][b], oshift=O["s_shift"][b], ok=O["s_k"][b], ov=O["s_v"][b]))
    for sq in seqs:
        T = sq["T"]
        sq["qT"] = dscr("qT_" + sq["name"], [12, 128, T], BF16)
        sq["sgT"] = dscr("sgT_" + sq["name"], [12, 128, T])
        sq["oT"] = dscr("oT_" + sq["name"], [12, 128, T])
        sq["x1"] = dscr("x1_" + sq["name"], [T, D])

    with contextlib.ExitStack() as st:
        sb = lambda n, s: st.enter_context(nc.sbuf_tensor(n, list(s), F32))
        ps = [st.enter_context(nc.psum_tensor("ps%d" % i, [128, 512], F32)) for i in range(8)]
        ident = sb("ident", [128, 128]); mask2 = sb("mask2", [128, 256]); masksl = sb("masksl", [128, 128])
        bones = sb("bones", [128, 128]); ones = sb("ones", [128, 128]); cmask = sb("cmask", [128, 128])
        B = Bld(nc, P, ident)
        for t, n in [(ident, "ident"), (mask2, "mask2"), (masksl, "masksl"), (bones, "bones"), (ones, "ones"), (cmask, "cmask")]:
            B.dma(t[:], I[n])
        stgA = sb("stgA", [128, 128]); stgB = sb("stgB", [128, 128]); colA = sb("colA", [128, 128]); colB = sb("colB", [128, 128])
        B.memset("pool", stgA[:], 0.0); B.memset("pool", stgB[:], 0.0)
        rowsA = [("mu", I["rw_mu"], 37)] + [(n, I["rw_" + n], 12) for n in ["w0", "a0", "k_k", "k_a", "r_k", "ln_w", "ln_b"]]
        rowsB = [("nw0", I["norm_w"][0], 8), ("nw1", I["norm_w"][1], 8), ("mnw0", I["mem_norm_w"][0], 8),
                 ("mnw1", I["mem_norm_w"][1], 8), ("subln", I["df_subln"], 1)]
        cols = {}
        for stg, col, rows in ((stgA, colA, rowsA), (stgB, colB, rowsB)):
            r0 = 0
            for n, src, k in rows:
                B.dma(stg[r0:r0 + k, :], src.rearrange("(c p) -> c p", p=128))
                cols[n] = col[:, r0:r0 + k]
                r0 += k
            B.tr(ps[0][:, 0:128], stg[:])
            B.copy("dve", col[:], ps[0][:, 0:128])
        mu, w0, a0, k_k, k_a, r_k, ln_w, ln_b = [cols[n] for n in ["mu", "w0", "a0", "k_k", "k_a", "r_k", "ln_w", "ln_b"]]
        nw0, nw1, mnw0, mnw1, subln = [cols[n] for n in ["nw0", "nw1", "mnw0", "mnw1", "subln"]]
        WUP = sb("WUP", [128, MIXW]); AUP = sb("AUP", [128, MIXW])
        B.memset("pool", WUP[64:128, :], 0.0); B.memset("pool", AUP[0:64, :], 0.0)
        B.dma(WUP[0:64, :], I["rw_w_up"]); B.dma(AUP[64:128, :], I["rw_a_up"])
        lam_init = 0.8 - 0.6 * math.exp(-0.3 * 1)
        lq = sb("lq", [128, 4, 64]); lsum = sb("lsum", [128, 2]); nlam = sb("nlam", [128, 1]); lprod = sb("lprod", [128, 2, 64])
        for i, n in enumerate(["df_lq1", "df_lk1", "df_lq2", "df_lk2"]):
            B.dma(lq[:, i, :], I[n].partition_broadcast(128))
        B.tt("dve", lprod[:, 0, :], lq[:, 0, :], lq[:, 1, :], ALU.mult)
        B.tt("dve", lprod[:, 1, :], lq[:, 2, :], lq[:, 3, :], ALU.mult)
        P.op("dve", lambda e: e.tensor_reduce(out=lsum[:], in_=lprod[:], axis=mybir.AxisListType.X, op=ALU.add), [lprod], [lsum])
        B.act(lsum[:], lsum[:], AF.Exp)
        B.tt("dve", nlam[:], lsum[:, 1:2], lsum[:, 0:1], ALU.subtract)
        B.ts("dve", nlam[:], nlam[:], -lam_init, ALU.add)

        with contextlib.ExitStack() as st0:
            cin = [st0.enter_context(nc.sbuf_tensor("cin%d" % i, [128, 2048], F32)) for i in range(2)]
            cout = [st0.enter_context(nc.sbuf_tensor("cout%d" % i, [128, 2048], BF16)) for i in range(2)]
            ci = 0
            for src, dst, C_ in ((I["rw_in"], WB["rw_in"], RWIN), (I["df_in"], WB["df_in"], DFIN)):
                nch = C_ // 128
                for r0 in range(0, D, 128):
                    d = r0 // 128
                    for c0 in range(0, nch, 16):
                        n = min(16, nch - c0)
                        a_, b_ = cin[ci % 2], cout[ci % 2]; ci += 1
                        B.dma(a_[:, 0:n * 128], src[r0:r0 + 128, c0 * 128:(c0 + n) * 128])
                        B.copy("act" if ci % 2 else "dve", b_[:, 0:n * 128], a_[:, 0:n * 128])
                        B.dma(dst[c0:c0 + n, :, d * 128:(d + 1) * 128].rearrange("c p x -> p c x"),
                              b_[:, 0:n * 128].rearrange("p (c x) -> p c x", x=128))
            wsrc2 = I["w_out"].rearrange("l r c -> (l r) c")
            for r0 in range(0, 2 * BR, 256):
                a_, b_ = cin[ci % 2], cout[ci % 2]; ci += 1
                B.dma(a_[:].rearrange("p (r c) -> p r c", r=2), wsrc2[r0:r0 + 256, :].rearrange("(r p) c -> p r c", p=128))
                B.copy("act" if ci % 2 else "dve", b_[:], a_[:])
                B.dma(WB["w_out"][r0:r0 + 256, :].rearrange("(r p) c -> p r c", p=128), b_[:].rearrange("p (r c) -> p r c", r=2))
        P.barrier()
        st1 = contextlib.ExitStack()
        sb = lambda n, s, dt=F32: st1.enter_context(nc.sbuf_tensor(n, list(s), dt))
        xt = sb("xt", [128, D]); xn = sb("xn", [128, D]); hT = sb("hT", [128, 8, 128], BF16)
        rstat = sb("rstat", [128, 4])
        NWB = 4
        Wb = [sb("W%d" % i, [128, 4, 1024], BF16) for i in range(NWB)]
        GTb = sb("GTb", [128, 16, 128], BF16); YTb = sb("YTb", [128, 12, 128], BF16)
        raw = [sb("raw0", [128, 130]), sb("raw1", [128, 130])]
        dif = [sb("dif0", [128, 128]), sb("dif1", [128, 128])]
        PS = sb("PS", [128, 37, 128])
        plast = sb("plast", [128, 37]); plstg = stgA
        SG = sb("SG", [128, 12, 128]); AA = sb("AA", [128, 12, 128]); KK = sb("KK", [128, 12, 128])
        CS = sb("CS", [128, 12, 128]); EP = sb("EP", [128, 12, 128]); EM = sb("EM", [128, 12, 128]); EQ = sb("EQ", [128, 12, 128])
        TMP = sb("TMP", [128, 12, 128]); BON = sb("BON", [128, 12, 128]); YT = SG
        onesrow = sb("onesrow", [128, 128])
        ARs = [sb("AR_%d" % i, [128, 256]) for i in range(2)]
        AZs = [[sb("AZ0_%d" % i, [128, 128]), sb("AZ1_%d" % i, [128, 128])] for i in range(2)]
        BZs = [[sb("BZ0_%d" % i, [128, 128]), sb("BZ1_%d" % i, [128, 128])] for i in range(2)]
        KZs = [[sb("KZ0_%d" % i, [128, 128]), sb("KZ1_%d" % i, [128, 128])] for i in range(2)]
        BGs = [sb("BG_%d" % i, [128, 256]) for i in range(2)]
        TMXs = [sb("TMX_%d" % i, [128, 3, 128]) for i in range(2)]
        AZ = AZs[0] + AZs[1]; BZ = BZs[0] + BZs[1]; KZ = KZs[0] + KZs[1]
        MA = [sb("MA0", [128, 256]), sb("MA1", [128, 256])]
        MB = [sb("MB0", [128, 256]), sb("MB1", [128, 256])]
        PXa = [[sb("PX%d%d" % (h, i), [128, 192]) for i in range(2)] for h in range(2)]
        PTa = [[sb("PT%d%d" % (h, i), [128, 128]) for i in range(2)] for h in range(2)]
        UP = sb("UP", [128, 128]); STMP = sb("STMP", [128, 128])
        Hone = sb("Hstate", [128, 12, 128])
        Hbd = {sq["name"]: Hone for sq in seqs}
        GT = sb("GT", [128, 16, 128]); QM = sb("QM", [128, 4, 128]); EE = sb("EE", [128, 2, 128])
        YM = sb("YM", [128, 4, 128]); RD = sb("RD", [128, 128])
        x1t = sb("x1t", [128, D]); KVT = sb("KVT", [128, 512])
        mkT = [sb("mkT0", [128, 4, NMEM]), sb("mkT1", [128, 4, NMEM])]
        mvS = [sb("mv0", [128, 2, MEMW]), sb("mv1", [128, 2, MEMW])]
        memx = GT[:].rearrange("p a b -> p (a b)").rearrange("p (m d) -> p m d", m=2)
        memh = PS[:, 0:16, :].rearrange("p a b -> p (a b)").rearrange("p (d m) -> p d m", d=8)
        memo = TMP[:, 0:8, :].rearrange("p a b -> p (a b)").rearrange("p (m f) -> p m f", m=2)
        Wf = [BON[:, 0:8, :], CS[:, 0:8, :]]
        B.memset("pool", onesrow[:], 1.0)
        for t in AZ + BZ + KZ:
            B.memset("pool", t[:], 0.0)

        def rstd_from_ss(ss, n, TT, eps):
            B.ts("dve", ss, ss, 1.0 / n, ALU.mult, eps, ALU.add)
            B.act(ss, ss, AF.Ln)
            B.act(ss, ss, AF.Exp, scale=-0.5)

        def rms_to_hT(xsrc, TT, nwc, hdst, junk):
            B.act(junk[:TT, :], xsrc[:TT, :], AF.Square, accum_out=rstat[:TT, 0:1])
            rstd_from_ss(rstat[:TT, 0:1], D, TT, EPS)
            B.ts("dve", junk[:TT, :], xsrc[:TT, :], rstat[:TT, 0:1], ALU.mult)
            for g in range(2):
                for d4 in range(4):
                    d = g * 4 + d4
                    B.tr(ps[g][:, d4 * 128:d4 * 128 + TT], junk[:TT, d * 128:(d + 1) * 128])
                for d4 in range(4):
                    d = g * 4 + d4
                    B.ts("dve", hdst[:, d, 0:TT], ps[g][:, d4 * 128:d4 * 128 + TT], nwc[:, d:d + 1], ALU.mult)

        wctr = [0]

        def proj_chunks(Wsrc, f0, nf, hsrc, TT, consume):
            f = f0
            while f < f0 + nf:
                n = min(4, f0 + nf - f)
                wb = Wb[wctr[0] % NWB]; pb = ps[wctr[0] % 2]; wctr[0] += 1
                B.dma(wb[:, 0:n, :], Wsrc[f:f + n].rearrange("c p x -> p c x"))
                for j in range(n):
                    for d in range(8):
                        B.mm(pb[:, j * 128:j * 128 + TT], wb[:, j, d * 128:(d + 1) * 128], hsrc[:, d, 0:TT],
                             start=(d == 0), stop=(d == 7))
                    consume(f + j, pb[:, j * 128:j * 128 + TT])
                f += n

        def mem_attn(qm, TT, li, ymdst):
            for h in range(4):
                for m in range(2):
                    B.mm(ps[2][:, m * 128:m * 128 + TT], mkT[li][:, h, m * 128:(m + 1) * 128], qm[:, h, 0:TT])
                    B.act(EE[:, m, 0:TT], ps[2][:, m * 128:m * 128 + TT], AF.Exp, scale=128 ** -0.5)
                for m in range(2):
                    B.mm(ps[3][:, 0:TT], mvS[li][:, m, h * 128:(h + 1) * 128], EE[:, m, 0:TT], start=(m == 0), stop=(m == 1))
                for m in range(2):
                    B.mm(ps[3][:, 128:128 + TT], ones[:], EE[:, m, 0:TT], start=(m == 0), stop=(m == 1))
                B.recip(RD[:, 0:TT], ps[3][:, 128:128 + TT])
                B.tt("dve", ymdst[:, h, 0:TT], ps[3][:, 0:TT], RD[:, 0:TT], ALU.mult)

        def out_proj(li, c0, nchunks, TT, xres, xdst):
            c = 0
            first = True
            while c < nchunks:
                n = min(4, nchunks - c)
                wb = Wb[wctr[0] % NWB]; wctr[0] += 1
                wv = wb[:].rearrange("p a b -> p (a b)")
                r0 = li * BR + (c0 + c) * 128
                B.dma(wv[:, 0:n * D].rearrange("p (c f) -> p c f", f=D),
                      WB["w_out"][r0:r0 + n * 128, :].rearrange("(c p) f -> p c f", p=128))
                for j in range(n):
                    for hf in range(2):
                        B.mm(ps[4 + hf][0:TT, :], GTb[:, c0 + c + j, 0:TT], wv[:, j * D + hf * 512:j * D + hf * 512 + 512],
                             start=first, stop=(c + j == nchunks - 1))
                    first = False
                c += n
            for hf in range(2):
                B.tt("dve", xdst[0:TT, hf * 512:(hf + 1) * 512], ps[4 + hf][0:TT, :], xres[0:TT, hf * 512:(hf + 1) * 512], ALU.add)

        def phase_mem(sq):
            b = sq["b"]
            for li in range(2):
                if sq["kind"] == "p":
                    B.dma(memx[:], I["mem_prompt"][b].rearrange("(m p) d -> p m d", p=128))
                    mnw = mnw0 if li == 0 else mnw1
                    for m in range(2):
                        rms_to_hT(memx[:, m, :], 128, mnw, memh[:, :, m * 128:(m + 1) * 128], xn)
                    for h in range(4):
                        wb = Wf[wctr[0] % 2]; wctr[0] += 1
                        B.dma(wb, I["w_mem_k"][li][:, h * 128:(h + 1) * 128].rearrange("(d p) f -> p d f", p=128))
                        for d in range(8):
                            B.mm(ps[2][:, 0:NMEM], wb[:, d, :], memh[:, d, :], start=(d == 0), stop=(d == 7))
                        B.evac(mkT[li][:, h, :], ps[2][:, 0:NMEM])
                    for which, wsrc, odst in ((0, I["w_mem_k"][li], O["p_mk"][li, b]), (1, I["w_mem_v"][li], O["p_mv"][li, b])):
                        for fh in range(4):
                            wb = Wf[wctr[0] % 2]; wctr[0] += 1
                            B.dma(wb, wsrc[:, fh * 128:(fh + 1) * 128].rearrange("(d p) f -> p d f", p=128))
                            for m in range(2):
                                for d in range(8):
                                    B.mm(ps[3][:, m * 128:(m + 1) * 128], memh[:, d, m * 128:(m + 1) * 128], wb[:, d, :],
                                         start=(d == 0), stop=(d == 7))
                                dst = (mvS[li] if which == 1 else memo)
                                B.evac(dst[:, m, fh * 128:(fh + 1) * 128], ps[3][:, m * 128:(m + 1) * 128])
                        src = mvS[li] if which == 1 else memo
                        B.dma(odst.rearrange("(m p) f -> p m f", p=128), src[:])
                else:
                    B.dma(memo[:], I["cache_mem_k"][li, b].rearrange("(m p) f -> p m f", p=128))
                    B.dma(mvS[li][:], I["cache_mem_v"][li, b].rearrange("(m p) f -> p m f", p=128))
                    for h in range(4):
                        for m in range(2):
                            B.tr(ps[2][:, m * 128:(m + 1) * 128], memo[:, m, h * 128:(h + 1) * 128])
                        B.evac(mkT[li][:, h, :], ps[2][:, 0:NMEM])

        def rwkv_scan(sq, TT):
            C = TT
            H = Hbd[sq["name"]]
            nlev = int(math.ceil(math.log2(C)))
            m2v = mask2[:].rearrange("p (a c) -> p a c", a=2)
            for pr in range(12):
                AR, AZ, BZ, KZ, BG, TMX = ARs[pr % 2], AZs[pr % 2], BZs[pr % 2], KZs[pr % 2], BGs[pr % 2], TMXs[pr % 2]
                rP, kP, vP = PS[:, pr, 0:C], PS[:, 12 + pr, 0:C], PS[:, 24 + pr, 0:C]
                B.stt(AR[:, 0:C], KK[:, pr, 0:C], -1.0, EQ[:, pr, 0:C], ALU.mult, ALU.mult)
                B.tt("pool", AR[:, 128:128 + C], rP, EP[:, pr, 0:C], ALU.mult)
                for h in range(2):
                    hs = slice(h * 64, h * 64 + 64)
                    B.copy("pool", AZ[h][hs, 0:C], AR[hs, 0:C])
                    B.tt("dve" if h else "pool", BZ[h][hs, 0:C], AA[hs, pr, 0:C], EM[hs, pr, 0:C], ALU.mult)
                    B.tt("pool" if h else "dve", KZ[h][hs, 0:C], kP[hs, :], EM[hs, pr, 0:C], ALU.mult)
                gC = EP[:, pr, C - 1:C]
                B.stt(BG[:, 0:C], AA[:, pr, 0:C], gC, EM[:, pr, 0:C], ALU.mult, ALU.mult)
                B.stt(BG[:, 128:128 + C], kP, gC, EM[:, pr, 0:C], ALU.mult, ALU.mult)
                B.tr(ps[3][0:C, 0:128], BG[:, 0:C]); B.tr(ps[3][0:C, 128:256], BG[:, 128:128 + C]); B.tr(ps[3][0:C, 256:384], vP)
                B.evac(TMX[0:C, :, :].rearrange("p a b -> p (a b)"), ps[3][0:C, 0:384])
                Vp = TMX[0:C, 2, :]
                for h in range(2):
                    B.mm(ps[4][0:C, h * 256:h * 256 + 128 + C], BZ[h][:, 0:C], AR[:, 0:128 + C])
                    B.mm(ps[5][0:C, h * 256:h * 256 + 128 + C], KZ[h][:, 0:C], AR[:, 0:128 + C])
                for h in range(2):
                    B.tt("dve", MA[h][0:C, :].rearrange("p (a c) -> p a c", a=2)[:, :, 0:C],
                         ps[4][0:C, h * 256:(h + 1) * 256].rearrange("p (a c) -> p a c", a=2)[:, :, 0:C], m2v[0:C, :, 0:C], ALU.mult)
                    B.tt("dve", MB[h][0:C, :].rearrange("p (a c) -> p a c", a=2)[:, :, 0:C],
                         ps[5][0:C, h * 256:(h + 1) * 256].rearrange("p (a c) -> p a c", a=2)[:, :, 0:C], m2v[0:C, :, 0:C], ALU.mult)
                for h in range(2):
                    B.mm(ps[6][0:C, h * 128:h * 128 + C], AZ[h][:, 0:C], BZ[h][:, 0:C])
                    B.tt("dve", PXa[h][0][0:C, 0:C], ps[6][0:C, h * 128:h * 128 + C], masksl[0:C, 0:C], ALU.mult)
                B.mm(ps[7][0:C, 0:128], AR[:, 0:C], H[:, pr, :], start=True, stop=False)
                for h in range(2):
                    B.mm(ps[7][0:C, h * 64:h * 64 + 64], MB[h][0:C, 0:C], Vp[:, h * 64:h * 64 + 64], start=False, stop=(h == 1))
                for h in range(2):
                    B.evac(PXa[h][0][0:C, 128:192], ps[7][0:C, h * 64:h * 64 + 64])
                cur = [0, 0]
                for lv in range(nlev):
                    last = lv == nlev - 1
                    for h in range(2):
                        Pm = PXa[h][cur[h]]
                        PT = MA[h][0:C, 0:C] if lv == 0 else PTa[h][cur[h]][0:C, 0:C]
                        pa_, pb_ = ps[4 + 2 * h], ps[5 + 2 * h]
                        if last:
                            B.mm(pa_[0:C, 128:192], PT, Pm[0:C, 128:192])
                            B.tt("dve", UP[0:C, h * 64:h * 64 + 64], pa_[0:C, 128:192], Pm[0:C, 128:192], ALU.add)
                        else:
                            nxt = 1 - cur[h]
                            if C == 128:
                                B.mm(pa_[0:C, 0:192], PT, Pm[0:C, 0:192])
                            else:
                                B.mm(pa_[0:C, 0:C], PT, Pm[0:C, 0:C])
                                B.mm(pa_[0:C, 128:192], PT, Pm[0:C, 128:192])
                            B.mm(pb_[0:C, 0:C], Pm[0:C, 0:C], PT)
                            B.copy("act", PXa[h][nxt][0:C, 0:C], pa_[0:C, 0:C])
                            B.tt("dve", PXa[h][nxt][0:C, 128:192], pa_[0:C, 128:192], Pm[0:C, 128:192], ALU.add)
                            B.copy("act" if h else "dve", PTa[h][nxt][0:C, 0:C], pb_[0:C, 0:C])
                            cur[h] = nxt
                for h in range(2):
                    hs = slice(h * 64, h * 64 + 64)
                    B.mm(ps[2][:, h * 128:h * 128 + C], H[:, pr, :], AR[:, 128:128 + C], start=True, stop=False)
                    B.mm(ps[2][:, h * 128:h * 128 + C], UP[0:C, :], MA[h][0:C, 128:128 + C], start=False, stop=False)
                    B.mm(ps[2][:, h * 128:h * 128 + C], Vp, MB[h][0:C, 128:128 + C], start=False, stop=True)
                    B.copy("act", YT[hs, pr, 0:C], ps[2][hs, h * 128:h * 128 + C])
                B.mm(ps[3][:, 384:512], TMX[0:C, 0, :], UP[0:C, :], start=True, stop=False)
                B.mm(ps[3][:, 384:512], TMX[0:C, 1, :], Vp, start=False, stop=True)
                B.tt("dve", STMP[:], ps[3][:, 384:512], bones[:], ALU.mult)
                B.stt(H[:, pr, :], H[:, pr, :], gC, STMP[:], ALU.mult, ALU.add)

        def phase1_tile(sq, ti):
            TT = sq["C"]; t0 = ti * TT; nm = sq["name"]
            B.dma(xt[0:TT, :], sq["x"][t0:t0 + TT, :])
            rms_to_hT(xt, TT, nw0, hT, xn)
            cnt = [0]

            def cons_p(f, pap):
                rw = raw[cnt[0] % 2]; df = dif[cnt[0] % 2]; cnt[0] += 1
                B.copy("pool", rw[:, 0:1], plast[:, f:f + 1])
                B.copy("act", rw[:, 1:TT + 1], pap)
                B.copy("pool", plast[:, f:f + 1], rw[:, TT:TT + 1])
                B.tt("pool", df[:, 0:TT], rw[:, 0:TT], rw[:, 1:TT + 1], ALU.subtract)
                B.stt(PS[:, f, 0:TT], df[:, 0:TT], mu[:, f:f + 1], rw[:, 1:TT + 1], ALU.mult, ALU.add)
            proj_chunks(WB["rw_in"], 0, 37, hT, TT, cons_p)
            B.act(PS[0:64, 36, 0:TT], PS[0:64, 36, 0:TT], AF.Tanh)
            for f in range(12):
                B.mm(ps[2][:, (f % 4) * 128:(f % 4) * 128 + TT], WUP[:, f * 128:(f + 1) * 128], PS[:, 36, 0:TT])
                B.act(SG[:, f, 0:TT], ps[2][:, (f % 4) * 128:(f % 4) * 128 + TT], AF.Sigmoid, bias=w0[:, f:f + 1])
                B.mm(ps[3][:, (f % 4) * 128:(f % 4) * 128 + TT], AUP[:, f * 128:(f + 1) * 128], PS[:, 36, 0:TT])
                B.act(AA[:, f, 0:TT], ps[3][:, (f % 4) * 128:(f % 4) * 128 + TT], AF.Sigmoid, bias=a0[:, f:f + 1])
            for f in range(12):
                kP = PS[:, 12 + f, 0:TT]
                B.ts("pool", KK[:, f, 0:TT], kP, k_k[:, f:f + 1], ALU.mult)
                B.tt("pool", TMP[:, f, 0:TT], KK[:, f, 0:TT], KK[:, f, 0:TT], ALU.mult)
                B.mm(ps[2][:, (f % 4) * 128:(f % 4) * 128 + TT], bones[:], TMP[:, f, 0:TT])
                B.act(TMP[:, f, 0:TT], ps[2][:, (f % 4) * 128:(f % 4) * 128 + TT], AF.Sqrt)
                B.ts("dve", TMP[:, f, 0:TT], TMP[:, f, 0:TT], 1e-12, ALU.max)
                B.recip(TMP[:, f, 0:TT], TMP[:, f, 0:TT])
                B.tt("dve", KK[:, f, 0:TT], KK[:, f, 0:TT], TMP[:, f, 0:TT], ALU.mult)
                B.ts("dve", TMP[:, f, 0:TT], AA[:, f, 0:TT], -1.0, ALU.add, k_a[:, f:f + 1], ALU.mult)
                B.stt(kP, TMP[:, f, 0:TT], 1.0, kP, ALU.add, ALU.mult)
                B.tt("pool", AA[:, f, 0:TT], KK[:, f, 0:TT], AA[:, f, 0:TT], ALU.mult)
                B.stt(TMP[:, f, 0:TT], PS[:, f, 0:TT], r_k[:, f:f + 1], kP, ALU.mult, ALU.mult)
                B.mm(ps[3][:, (f % 4) * 128:(f % 4) * 128 + TT], bones[:], TMP[:, f, 0:TT])
                B.tt("dve", BON[:, f, 0:TT], ps[3][:, (f % 4) * 128:(f % 4) * 128 + TT], PS[:, 24 + f, 0:TT], ALU.mult)
                B.scan_add(CS[:, f, 0:TT], onesrow[:, 0:TT], SG[:, f, 0:TT])
            B.tt("pool", TMP[:, :, 0:TT], CS[:, :, 0:TT], SG[:, :, 0:TT], ALU.subtract)
            B.act(EP[:, :, 0:TT], CS[:, :, 0:TT], AF.Exp, scale=-DECAY_C)
            B.act(EM[:, :, 0:TT], CS[:, :, 0:TT], AF.Exp, scale=DECAY_C)
            B.act(EQ[:, :, 0:TT], TMP[:, :, 0:TT], AF.Exp, scale=-DECAY_C)
            rwkv_scan(sq, TT)
            for g in range(3):
                sl = slice(4 * g, 4 * g + 4)
                yv, tv, bv = YT[:, sl, 0:TT], TMP[:, sl, 0:TT], BON[:, sl, 0:TT]
                p2 = ps[2][:].rearrange("p (a c) -> p a c", a=4)[:, :, 0:TT]
                p3 = ps[3][:].rearrange("p (a c) -> p a c", a=4)[:, :, 0:TT]
                bc = lambda col: col[:, sl].unsqueeze(2).broadcast_to([128, 4, TT])
                for j in range(4):
                    B.mm(ps[2][:, j * 128:j * 128 + TT], bones[:], YT[:, 4 * g + j, 0:TT])
                B.stt(yv, p2, -1.0 / 64, yv, ALU.mult, ALU.add)
                B.tt("pool", tv, yv, yv, ALU.mult)
                for j in range(4):
                    B.mm(ps[3][:, j * 128:j * 128 + TT], bones[:], TMP[:, 4 * g + j, 0:TT])
                B.ts("dve", tv, p3, 1.0 / 64, ALU.mult, GN_EPS, ALU.add)
                B.act(tv, tv, AF.Sqrt)
                B.recip(tv, tv)
                B.tt("dve", yv, yv, tv, ALU.mult)
                B.tt("pool", yv, yv, bc(ln_w), ALU.mult)
                B.tt("dve", yv, yv, bc(ln_b), ALU.add)
                B.tt("pool", yv, yv, bv, ALU.add)
            def cons_qg(f, pap):
                if f < 41:
                    B.evac(QM[:, f - 37, 0:TT], pap)
                else:
                    B.act(GT[:, f - 41, 0:TT], pap, AF.Silu)
            proj_chunks(WB["rw_in"], 37, 20, hT, TT, cons_qg)
            mem_attn(QM, TT, 0, YM)
            B.tt("pool", GTb[:, 0:12, 0:TT], YT[:, :, 0:TT], GT[:, 0:12, 0:TT], ALU.mult)
            B.tt("pool", GTb[:, 12:16, 0:TT], YM[:, :, 0:TT], GT[:, 12:16, 0:TT], ALU.mult)
            out_proj(0, 0, 16, TT, xt, x1t)
            rms_to_hT(x1t, TT, nw1, hT, xn)

            def cons_l1(f, pap):
                if f < 12:
                    B.copy("act", YTb[:, f, 0:TT], pap)
                    B.dma(sq["qT"][f, :, t0:t0 + TT], YTb[:, f, 0:TT], eng="act")
                elif f < 36:
                    g = (f - 12) % 4
                    B.evac(TMP[:, g, 0:TT], pap)
                    B.tr(ps[3][0:TT, g * 128:(g + 1) * 128], TMP[:, g, 0:TT])
                    if g == 3:
                        B.copy("act", KVT[0:TT, :], ps[3][0:TT, :])
                        dst = sq["ok"] if f < 24 else sq["ov"]
                        c0 = ((f - 12) % 12 - 3) * 128
                        B.dma(dst[t0:t0 + TT, c0:c0 + 512], KVT[0:TT, :], eng="act")
                elif f < 40:
                    B.evac(QM[:, f - 36, 0:TT], pap)
                else:
                    B.act(GT[:, f - 40, 0:TT], pap, AF.Silu)
            proj_chunks(WB["df_in"], 0, 56, hT, TT, cons_l1)
            B.dma(sq["sgT"][:, :, t0:t0 + TT].rearrange("c p t -> p c t"), GT[:, 0:12, 0:TT], eng="act")
            mem_attn(QM, TT, 1, YM)
            B.tt("pool", GTb[:, 12:16, 0:TT], YM[:, :, 0:TT], GT[:, 12:16, 0:TT], ALU.mult)
            out_proj(1, 12, 4, TT, x1t, xt)
            B.dma(sq["x1"][t0:t0 + TT, :], xt[0:TT, :], eng="act")

        def phase1_seq(sq):
            nm = sq["name"]; b = sq["b"]; H = Hbd[nm]
            phase_mem(sq)
            if sq["kind"] == "p":
                B.memset("pool", plast[:], 0.0)
                B.memset("pool", H[:], 0.0)
            else:
                B.memset("pool", plstg[:], 0.0)
                B.dma(plstg[0:37, :], I["state_shift"][b].rearrange("(c p) -> c p", p=128))
                B.tr(ps[0][:, 0:128], plstg[:])
                B.copy("dve", plast[:], ps[0][:, 0:37])
                B.memset("pool", TMP[:], 0.0)
                for pr in range(12):
                    for h in range(2):
                        B.dma(TMP[h * 64:h * 64 + 64, pr, h * 64:h * 64 + 64], I["state_rwkv"][b, 2 * pr + h])
                for pr in range(12):
                    B.tr(ps[pr % 2][:, 0:128], TMP[:, pr, :])
                    B.evac(H[:, pr, :], ps[pr % 2][:, 0:128])
            for ti in range(sq["T"] // sq["C"]):
                phase1_tile(sq, ti)
            B.tr(ps[0][0:37, 0:128], plast[:])
            B.copy("dve", plstg[0:37, :], ps[0][0:37, 0:128])
            B.dma(sq["oshift"].rearrange("(c p) -> c p", p=128), plstg[0:37, :])
            for pr in range(12):
                B.tr(ps[pr % 2][:, 0:128], H[:, pr, :])
                B.evac(TMP[:, pr, :], ps[pr % 2][:, 0:128])
            for pr in range(12):
                for h in range(2):
                    B.dma(sq["oS"][2 * pr + h], TMP[h * 64:h * 64 + 64, pr, h * 64:h * 64 + 64])

        def phase2(sq):
            T = sq["T"]; past = sq["past"]; b = sq["b"]
            L = past + T
            nkt = (L + 127) // 128
            Lp = nkt * 128
            with contextlib.ExitStack() as st2:
                sb2 = lambda n, s, dt=F32: st2.enter_context(nc.sbuf_tensor(n + sq["name"], list(s), dt))
                Ktm = sb2("Ktm", [128, nkt, 128]); Vtm = sb2("Vtm", [128, nkt, 128]); Vb = sb2("Vb", [128, nkt, 128], BF16)
                KZ1 = sb2("KZ1", [128, Lp], BF16); KZ2 = sb2("KZ2", [128, Lp], BF16); QT = sb2("QT", [128, T], BF16)
                E1 = [sb2("E1a", [128, 512], BF16), sb2("E1b", [128, 512], BF16)]
                E2 = [sb2("E2a", [128, 512], BF16), sb2("E2b", [128, 512], BF16)]
                AC1 = sb2("AC1", [128, 512]); AC2 = sb2("AC2", [128, 512]); R1 = sb2("R1", [128, 512]); R2 = sb2("R2", [128, 512])
                OT = sb2("OT", [128, 512]); SQ2 = sb2("SQ2", [128, 512]); kval = sb2("kval", [128, 1])
                B.memset("pool", KZ1[64:128, :], 0.0); B.memset("pool", KZ2[0:64, :], 0.0)
                lastn = L - (nkt - 1) * 128
                if lastn < 128:
                    B.memset("pool", kval[:], 0.0); B.memset("pool", kval[0:lastn, :], 1.0)
                for h in range(12):
                    if lastn < 128:
                        B.memset("pool", Ktm[:, nkt - 1, :], 0.0); B.memset("pool", Vtm[:, nkt - 1, :], 0.0)
                    def ld(dst, src, n0, ntl):
                        for g0 in range(0, ntl, 8):
                            g = min(8, ntl - g0)
                            B.dma(dst[:, n0 + g0:n0 + g0 + g, :],
                                  src[g0 * 128:(g0 + g) * 128, h * 128:(h + 1) * 128].rearrange("(n p) f -> p n f", p=128))
                    if past:
                        ld(Ktm, I["cache_k"][b], 0, past // 128)
                        ld(Vtm, I["cache_v"][b], 0, past // 128)
                    nk0 = past // 128
                    if T >= 128:
                        ld(Ktm, sq["ok"], nk0, T // 128)
                        ld(Vtm, sq["ov"], nk0, T // 128)
                    else:
                        B.dma(Ktm[0:T, nk0, :], sq["ok"][:, h * 128:(h + 1) * 128], rk=[sq["ok"]])
                        B.dma(Vtm[0:T, nk0, :], sq["ov"][:, h * 128:(h + 1) * 128], rk=[sq["ov"]])
                    B.dma(QT[:], sq["qT"][h], rk=[sq["qT"]])
                    hk = nkt // 2
                    if hk:
                        B.copy("act", Vb[:, 0:hk, :], Vtm[:, 0:hk, :])
                    B.copy("dve", Vb[:, hk:nkt, :], Vtm[:, hk:nkt, :])
                    for kt in range(nkt):
                        B.tr(ps[kt % 2][:, 0:128], Ktm[:, kt, :])
                        B.copy("act", KZ1[0:64, kt * 128:(kt + 1) * 128], ps[kt % 2][0:64, 0:128])
                        B.copy("dve", KZ2[64:128, kt * 128:(kt + 1) * 128], ps[kt % 2][64:128, 0:128])
                    QB = min(512, T)
                    for q0 in range(0, T, QB):
                        kend = (past + q0 + QB + 127) // 128
                        pvA, pvB = ps[4], ps[5]

                        def geom(kt):
                            if sq["kind"] == "p":
                                r = kt - q0 // 128
                                return r, max(0, r) * 128
                            return -1, 0

                        def qk(kt):
                            r, c0 = geom(kt)
                            sA, sB = ps[2 * (kt % 2)], ps[2 * (kt % 2) + 1]
                            B.mm(sA[:, c0:QB], KZ1[:, kt * 128:(kt + 1) * 128], QT[:, q0 + c0:q0 + QB])
                            B.mm(sB[:, c0:QB], KZ2[:, kt * 128:(kt + 1) * 128], QT[:, q0 + c0:q0 + QB])

                        qk(0)
                        for kt in range(kend):
                            r, c0 = geom(kt)
                            sA, sB = ps[2 * (kt % 2)], ps[2 * (kt % 2) + 1]
                            e1, e2 = E1[kt % 2], E2[kt % 2]
                            B.act(e1[:, c0:QB], sA[:, c0:QB], AF.Exp, scale=0.125)
                            B.act(e2[:, c0:QB], sB[:, c0:QB], AF.Exp, scale=0.125)
                            if kt + 1 < kend:
                                qk(kt + 1)
                            if r >= 0:
                                B.tt("dve", e1[:, c0:c0 + 128], e1[:, c0:c0 + 128], cmask[:], ALU.mult)
                                B.tt("pool", e2[:, c0:c0 + 128], e2[:, c0:c0 + 128], cmask[:], ALU.mult)
                            if kt == nkt - 1 and lastn < 128:
                                B.ts("pool", e1[:, c0:QB], e1[:, c0:QB], kval[:, 0:1], ALU.mult)
                                B.ts("pool", e2[:, c0:QB], e2[:, c0:QB], kval[:, 0:1], ALU.mult)
                            B.mm(pvA[:, c0:QB], Vb[:, kt, :], e1[:, c0:QB], start=(kt == 0), stop=(kt == kend - 1))
                            B.mm(pvB[:, c0:QB], Vb[:, kt, :], e2[:, c0:QB], start=(kt == 0), stop=(kt == kend - 1))
                            if kt == 0:
                                B.copy("dve", AC1[:, 0:QB], e1[:, 0:QB]); B.copy("pool", AC2[:, 0:QB], e2[:, 0:QB])
                            else:
                                B.tt("dve", AC1[:, c0:QB], AC1[:, c0:QB], e1[:, c0:QB], ALU.add)
                                B.tt("pool", AC2[:, c0:QB], AC2[:, c0:QB], e2[:, c0:QB], ALU.add)
                        B.mm(ps[6][:, 0:QB], ones[:], AC1[:, 0:QB]); B.mm(ps[7][:, 0:QB], ones[:], AC2[:, 0:QB])
                        B.recip(R1[:, 0:QB], ps[6][:, 0:QB]); B.recip(R2[:, 0:QB], ps[7][:, 0:QB])
                        B.tt("dve", R1[:, 0:QB], pvA[:, 0:QB], R1[:, 0:QB], ALU.mult)
                        B.tt("dve", R2[:, 0:QB], pvB[:, 0:QB], R2[:, 0:QB], ALU.mult)
                        B.stt(OT[:, 0:QB], R2[:, 0:QB], nlam[:, 0:1], R1[:, 0:QB], ALU.mult, ALU.add)
                        B.tt("pool", SQ2[:, 0:QB], OT[:, 0:QB], OT[:, 0:QB], ALU.mult)
                        B.mm(ps[6][:, 0:QB], ones[:], SQ2[:, 0:QB])
                        B.ts("dve", SQ2[:, 0:QB], ps[6][:, 0:QB], 1.0 / 128, ALU.mult, EPS, ALU.add)
                        B.act(SQ2[:, 0:QB], SQ2[:, 0:QB], AF.Sqrt)
                        B.recip(SQ2[:, 0:QB], SQ2[:, 0:QB])
                        B.tt("dve", OT[:, 0:QB], OT[:, 0:QB], SQ2[:, 0:QB], ALU.mult)
                        B.ts("dve", OT[:, 0:QB], OT[:, 0:QB], subln[:, 0:1], ALU.mult, 1.0 - lam_init, ALU.mult)
                        B.dma(sq["oT"][h, :, q0:q0 + QB], OT[:, 0:QB], eng="act")

        def phase3(sq):
            T = sq["T"]; TT = sq["C"]
            with contextlib.ExitStack() as st3:
                sb3 = lambda n, s, dt=F32: st3.enter_context(nc.sbuf_tensor(n + sq["name"], list(s), dt))
                WO = sb3("WO", [128, 12, D], BF16); GTb = sb3("GTb3", [128, 12, 128], BF16)
                fnw = sb3("fnw", [128, D])
                B.dma(fnw[:], I["final_norm_w"].partition_broadcast(128))
                YT = sb3("YT3", [128, 12, 128]); GT = sb3("GT3", [128, 12, 128]); x1t = sb3("x1t3", [128, D])
                xt = sb3("xt3", [128, D]); xn = sb3("xn3", [128, D]); rstat = sb3("rstat3", [128, 4])
                for c4 in range(0, 12, 4):
                    B.dma(WO[:, c4:c4 + 4, :], WB["w_out"][BR + c4 * 128:BR + (c4 + 4) * 128, :].rearrange("(c p) f -> p c f", p=128))
                for ti in range(T // TT):
                    t0 = ti * TT
                    B.dma(YT[:, :, 0:TT], sq["oT"][:, :, t0:t0 + TT].rearrange("c p t -> p c t"), rk=[sq["oT"]])
                    B.dma(GT[:, 0:12, 0:TT], sq["sgT"][:, :, t0:t0 + TT].rearrange("c p t -> p c t"), rk=[sq["sgT"]])
                    B.dma(x1t[0:TT, :], sq["x1"][t0:t0 + TT, :], rk=[sq["x1"]])
                    B.tt("pool", GTb[:, 0:12, 0:TT], YT[:, :, 0:TT], GT[:, 0:12, 0:TT], ALU.mult)
                    for c in range(12):
                        for hf in range(2):
                            B.mm(ps[4 + hf][0:TT, :], GTb[:, c, 0:TT], WO[:, c, hf * 512:(hf + 1) * 512], start=(c == 0), stop=(c == 11))
                    for hf in range(2):
                        B.tt("dve", xt[0:TT, hf * 512:(hf + 1) * 512], ps[4 + hf][0:TT, :], x1t[0:TT, hf * 512:(hf + 1) * 512], ALU.add)
                    B.act(xn[0:TT, :], xt[0:TT, :], AF.Square, accum_out=rstat[0:TT, 0:1])
                    rstd_from_ss(rstat[0:TT, 0:1], D, TT, EPS)
                    B.stt(xn[0:TT, :], xt[0:TT, :], rstat[0:TT, 0:1], fnw[0:TT, :], ALU.mult, ALU.mult)
                    B.dma(sq["y"][t0:t0 + TT, :], xn[0:TT, :], eng="act")

        for sq in seqs:
            if "1" in PHASES:
                phase1_seq(sq)
        P.barrier()
        st1.close()
        for sq in seqs:
            if "2" in PHASES:
                phase2(sq)
                P.barrier()
        for sq in seqs:
            if "3" in PHASES:
                phase3(sq)
                P.barrier()
        P.emit()
    return nc, P


_IN_ORDER = ["x_prompt", "mem_prompt", "x_sample", "state_rwkv", "state_shift", "cache_k", "cache_v", "cache_mem_k",
             "cache_mem_v"]


def shard_inputs(inputs, n_cores, NB):
    c = host_consts()
    maps = []
    f = lambda a: np.ascontiguousarray(np.asarray(a, dtype=np.float32))
    for ci in range(n_cores):
        sl = slice(ci * NB, (ci + 1) * NB)
        m = {}
        m["x_prompt"] = f(inputs["x_prompt"][sl]); m["mem_prompt"] = f(inputs["mem_prompt"][sl])
        m["x_sample"] = f(inputs["x_sample"][sl])
        m["state_rwkv"] = f(inputs["state_rwkv"][0, sl]); m["state_shift"] = f(inputs["state_shift"][0, sl])
        ck = np.asarray(inputs["cache_k"]); cv = np.asarray(inputs["cache_v"])
        m["cache_k"] = f(ck[0, sl].reshape(NB, ck.shape[2], MIXW)); m["cache_v"] = f(cv[0, sl].reshape(NB, cv.shape[2], MIXW))
        m["cache_mem_k"] = f(np.asarray(inputs["cache_mem_k"])[:, sl].reshape(2, NB, NMEM, MEMW))
        m["cache_mem_v"] = f(np.asarray(inputs["cache_mem_v"])[:, sl].reshape(2, NB, NMEM, MEMW))
        for n in ["norm_w", "mem_norm_w", "w_mem_k", "w_mem_v", "w_out", "final_norm_w"]:
            m[n] = f(inputs[n])
        for n in ["rw_in", "rw_mu", "rw_w0", "rw_w_up", "rw_a0", "rw_a_up", "rw_k_k", "rw_k_a", "rw_ln_w", "rw_ln_b",
                  "df_in", "df_lq1", "df_lk1", "df_lq2", "df_lk2", "df_subln"]:
            m[n] = f(np.asarray(inputs[n])[0])
        m["rw_r_k"] = f(np.asarray(inputs["rw_r_k"])[0].reshape(MIXW))
        m.update({"c_" + k: v for k, v in c.items()})
        maps.append(m)
    return maps


def gather_outputs(results, NB, TP, TS):
    cat = lambda n, ax: np.concatenate([np.asarray(r[n]) for r in results], axis=ax)
    nb = NB * len(results)
    y_p = cat("y_prompt", 0); y_s = cat("y_sample", 0)
    p_S = cat("p_S", 0)[None]; p_sh = cat("p_shift", 0)[None]
    p_k = cat("p_k", 0).reshape(1, nb, TP, 12, 128); p_v = cat("p_v", 0).reshape(1, nb, TP, 12, 128)
    p_mk = cat("p_mk", 1).reshape(2, nb, NMEM, 4, 128); p_mv = cat("p_mv", 1).reshape(2, nb, NMEM, 4, 128)
    s_S = cat("s_S", 0)[None]; s_sh = cat("s_shift", 0)[None]
    s_k = cat("s_k", 0).reshape(1, nb, TS, 12, 128); s_v = cat("s_v", 0).reshape(1, nb, TS, 12, 128)
    return (y_p, y_s, p_S, p_sh, p_k, p_v, p_mk, p_mv, s_S, s_sh, s_k, s_v)


def kernel(**inputs):
    n_cores = 8
    xp = np.asarray(inputs["x_prompt"]); xs = np.asarray(inputs["x_sample"])
    NB = xp.shape[0] // n_cores
    TP, TS = xp.shape[1], xs.shape[1]
    PAST = np.asarray(inputs["cache_k"]).shape[2]
    nc, _ = build(TP, TS, PAST, NB)
    maps = shard_inputs(inputs, n_cores, NB)
    res = run_bass_kernel_spmd(nc, maps, core_ids=list(range(n_cores)))
    return gather_outputs(res.results, NB, TP, TS)
```

```python
import contextlib
import math
import numpy as np
import concourse.bass as bass
import concourse.mybir as mybir
from concourse.bass_utils import run_bass_kernel_spmd

F32 = mybir.dt.float32
BF16 = mybir.dt.bfloat16
AF = mybir.ActivationFunctionType
ALU = mybir.AluOpType

ENGS = ["pe", "dve", "act", "pool", "sp"]
NDMA = 48
import os
POOL_ENG = os.environ.get("POOL_ENG", "pool")

D = 1024
MIXW = 1536
MEMW = 512
BR = 2048
RWP = 4736
RWIN = RWP + MEMW + BR
DFIN = 3 * MIXW + MEMW + BR
NMEM = 256
EPS = 1e-6
GN_EPS = 64e-5
DECAY_C = math.exp(-0.5)


def _key(x):
    if isinstance(x, (str, tuple)):
        return x
    if hasattr(x, "tensor"):
        return x.tensor.name
    return x.name


class Prog:
    def __init__(self, nc, same_engine_sync=True):
        self.nc = nc
        self.ops = {e: [] for e in ENGS}
        self.n = {e: 0 for e in ENGS}
        self.seen = {e: {} for e in ENGS}
        self.lastw = {}
        self.readers = {}
        self.dma_val = [0] * NDMA
        self.dma_next = 0
        self.same = same_engine_sync
        self.nwaits = 0

    def _add(self, eng, waits, tok, is_war=False):
        if tok is None:
            return
        sk, val = tok
        if sk == ("e", eng):
            if eng == "pe" or eng == "sp":
                return
            if not self.same:
                return
        if self.seen[eng].get(sk, 0) >= val:
            return
        if waits.get(sk, 0) < val:
            waits[sk] = val

    def op(self, eng, fn, reads=(), writes=(), dma=False, inc=True, attach=None):
        if eng == "pool" and not dma:
            eng = POOL_ENG
        rk = [_key(r) for r in reads if r is not None]
        wk = [_key(w) for w in writes if w is not None]
        wk += [k for k in rk if isinstance(k, str) and k[:2] == "ps" and k[2:].isdigit() and k not in wk]
        waits = {}
        for k in rk:
            self._add(eng, waits, self.lastw.get(k))
        for k in wk:
            self._add(eng, waits, self.lastw.get(k))
            for sk, val in self.readers.get(k, {}).items():
                self._add(eng, waits, (sk, val), is_war=True)
        if dma:
            slot = self.dma_next
            self.dma_next = (self.dma_next + 1) % NDMA
            prev = self.dma_val[slot]
            if prev > 0:
                self._add(eng, waits, (("d", slot), prev))
            self.dma_val[slot] = prev + 16
            tok = (("d", slot), prev + 16)
            incinfo = ("d", slot)
        else:
            if inc:
                self.n[eng] += 1
                tok = (("e", eng), self.n[eng])
                incinfo = ("e", eng)
            else:
                tok = (("e", eng), self.n[eng] + 1)
                incinfo = None
        for sk, val in waits.items():
            self.seen[eng][sk] = val
        self.nwaits += len(waits)
        self.lastw_prev = {_key(attach): self.lastw.get(_key(attach))} if attach is not None else {}
        for k in wk:
            self.lastw[k] = tok
            self.readers[k] = {}
        for k in rk:
            d = self.readers.setdefault(k, {})
            if d.get(tok[0], 0) < tok[1]:
                d[tok[0]] = tok[1]
        wl = list(waits.items())
        if attach is not None and len(wl) > 1:
            t = self.lastw_prev.get(_key(attach))
            if t is not None:
                wl.sort(key=lambda w: w[0] == t[0])
        self.ops[eng].append((fn, wl, incinfo))
        return tok

    def barrier(self):
        for eng in ENGS:
            waits = {}
            for o in ENGS:
                if o != "sp" and self.n[o] > self.seen[eng].get(("e", o), 0):
                    if not (o == eng and eng == "pe"):
                        waits[("e", o)] = self.n[o]
            for i, v in enumerate(self.dma_val):
                if v > self.seen[eng].get(("d", i), 0):
                    waits[("d", i)] = v
            for sk, val in waits.items():
                self.seen[eng][sk] = val
            self.nwaits += len(waits)
            self.ops[eng].append((None, list(waits.items()), None))

    def emit(self):
        nc = self.nc
        with contextlib.ExitStack() as st:
            esem = {e: st.enter_context(nc.semaphore("s_" + e)) for e in ENGS}
            dsem = [st.enter_context(nc.semaphore("d_%d" % i)) for i in range(NDMA)]
            block = st.enter_context(nc.Block())

            def sem_of(sk):
                return esem[sk[1]] if sk[0] == "e" else dsem[sk[1]]

            fin = [(("e", e), self.n[e]) for e in ENGS if e != "sp" and self.n[e] > 0]
            fin += [(("d", i), v) for i, v in enumerate(self.dma_val) if v > 0]

            def run(e, name):
                for fn, waits, incinfo in self.ops[name]:
                    if fn is None:
                        for sk, val in waits:
                            e.wait_ge(sem_of(sk), val)
                        continue
                    for sk, val in waits:
                        e.wait_ge(sem_of(sk), val)
                    ins = fn(e)
                    if incinfo is not None:
                        ins.then_inc(sem_of(incinfo), 16 if incinfo[0] == "d" else 1)
                if name == "sp":
                    for sk, val in fin:
                        e.wait_ge(sem_of(sk), val)

            @block.tensor
            def _(e):
                run(e, "pe")

            @block.vector
            def _(e):
                run(e, "dve")

            @block.scalar
            def _(e):
                run(e, "act")

            @block.gpsimd
            def _(e):
                run(e, "pool")

            @block.sync
            def _(e):
                run(e, "sp")


class Bld:
    def __init__(self, nc, P, ident):
        self.nc, self.P, self.ident = nc, P, ident
        self.rr = 0

    def mm(self, out, lhsT, rhs, start=True, stop=True, rk=(), wk=None):
        self.P.op("pe", lambda e: e.matmul(out, lhsT, rhs, start=start, stop=stop),
                  [lhsT, rhs] + list(rk), [out] if wk is None else wk, inc=True, attach=lhsT)

    def tr(self, out, in_):
        k = in_.shape[0]
        idn = self.ident[0:k, 0:k]
        self.P.op("pe", lambda e: e.transpose(out, in_, idn), [in_, idn], [out], attach=in_)

    def act(self, out, in_, func, scale=1.0, bias=0.0, accum_out=None, eng="act"):
        r = [in_] + [x for x in (scale, bias) if not isinstance(x, (int, float))]
        w = [out] + ([accum_out] if accum_out is not None else [])
        if accum_out is None:
            fn = lambda e: e.activation(out=out, in_=in_, func=func, bias=bias, scale=scale)
        else:
            fn = lambda e: e.activation(out=out, in_=in_, func=func, bias=bias, scale=scale, accum_out=accum_out)
        self.P.op("act", fn, r, w)

    def tt(self, eng, out, in0, in1, op):
        self.P.op(eng, lambda e: e.tensor_tensor(out=out, in0=in0, in1=in1, op=op), [in0, in1], [out])

    def ts(self, eng, out, in0, s1, op0, s2=None, op1=None):
        r = [in0] + [x for x in (s1, s2) if x is not None and not isinstance(x, (int, float))]
        if op1 is None:
            fn = lambda e: e.tensor_scalar(out=out, in0=in0, scalar1=s1, scalar2=None, op0=op0)
        else:
            fn = lambda e: e.tensor_scalar(out=out, in0=in0, scalar1=s1, scalar2=s2, op0=op0, op1=op1)
        self.P.op(eng, fn, r, [out])

    def stt(self, out, in0, scalar, in1, op0, op1):
        r = [in0, in1] + ([] if isinstance(scalar, (int, float)) else [scalar])
        self.P.op("dve", lambda e: e.scalar_tensor_tensor(out=out, in0=in0, scalar=scalar, in1=in1, op0=op0, op1=op1), r, [out])

    def copy(self, eng, out, in_):
        if eng == "act":
            self.P.op("act", lambda e: e.copy(out=out, in_=in_), [in_], [out])
        else:
            self.P.op(eng, lambda e: e.tensor_copy(out=out, in_=in_), [in_], [out])

    def evac(self, out, in_):
        self.rr ^= 1
        self.copy("act" if self.rr else "dve", out, in_)

    def memset(self, eng, ap, val):
        self.P.op(eng, lambda e: e.memset(ap, val), [], [ap])

    def recip(self, out, in_):
        self.P.op("dve", lambda e: e.reciprocal(out=out, in_=in_), [in_], [out])

    def scan_add(self, out, ones, data, init=0.0):
        self.P.op("dve", lambda e: e.tensor_tensor_scan(out=out, data0=ones, data1=data, initial=init,
                                                        op0=ALU.mult, op1=ALU.add), [ones, data], [out])

    def dma(self, out, in_, eng="sp", rk=None, wk=None):
        self.P.op(eng, lambda e: e.dma_start(out=out, in_=in_), [in_] if rk is None else rk,
                  [out] if wk is None else wk, dma=True)


def host_consts():
    i = np.arange(128)
    c = {}
    c["ident"] = np.eye(128, dtype=np.float32)
    su = (i[None, :] > i[:, None]).astype(np.float32)
    iu = (i[None, :] >= i[:, None]).astype(np.float32)
    c["mask2"] = np.concatenate([su, iu], axis=1)
    c["masksl"] = (i[None, :] < i[:, None]).astype(np.float32)
    c["bones"] = (i[None, :] // 64 == i[:, None] // 64).astype(np.float32)
    c["ones"] = np.ones((128, 128), np.float32)
    c["cmask"] = (i[:, None] < (i[None, :] // 64 + 1) * 64).astype(np.float32)
    return c


def build(TP, TS, PAST, NB=2, PHASES="123"):
    nc = bass.Bass("TRN2", target_bir_lowering=False)
    P = Prog(nc)
    din = lambda n, s: nc.dram_tensor(n, list(s), F32, kind="ExternalInput").ap()
    dout = lambda n, s: nc.dram_tensor(n, list(s), F32, kind="ExternalOutput").ap()
    dscr = lambda n, s, dt=F32: nc.dram_tensor(n, list(s), dt, kind="Internal").ap()
    WB = {"rw_in": dscr("rw_in_b", [RWIN // 128, 128, D], BF16), "df_in": dscr("df_in_b", [DFIN // 128, 128, D], BF16),
          "w_out": dscr("w_out_b", [2 * BR, D], BF16)}

    I = {}
    I["x_prompt"] = din("x_prompt", [NB, TP, D])
    I["mem_prompt"] = din("mem_prompt", [NB, NMEM, D])
    I["x_sample"] = din("x_sample", [NB, TS, D])
    I["state_rwkv"] = din("state_rwkv", [NB, 24, 64, 64])
    I["state_shift"] = din("state_shift", [NB, RWP])
    I["cache_k"] = din("cache_k", [NB, PAST, MIXW])
    I["cache_v"] = din("cache_v", [NB, PAST, MIXW])
    I["cache_mem_k"] = din("cache_mem_k", [2, NB, NMEM, MEMW])
    I["cache_mem_v"] = din("cache_mem_v", [2, NB, NMEM, MEMW])
    for n, s in [("norm_w", [2, D]), ("mem_norm_w", [2, D]), ("w_mem_k", [2, D, MEMW]), ("w_mem_v", [2, D, MEMW]),
                 ("w_out", [2, BR, D]), ("final_norm_w", [D]), ("rw_in", [D, RWIN]), ("rw_mu", [RWP]),
                 ("rw_w0", [MIXW]), ("rw_w_up", [64, MIXW]), ("rw_a0", [MIXW]), ("rw_a_up", [64, MIXW]),
                 ("rw_k_k", [MIXW]), ("rw_k_a", [MIXW]), ("rw_r_k", [MIXW]), ("rw_ln_w", [MIXW]),
                 ("rw_ln_b", [MIXW]), ("df_in", [D, DFIN]), ("df_lq1", [64]), ("df_lk1", [64]),
                 ("df_lq2", [64]), ("df_lk2", [64]), ("df_subln", [128])]:
        I[n] = din(n, s)
    for n, s in [("ident", [128, 128]), ("mask2", [128, 256]), ("masksl", [128, 128]), ("bones", [128, 128]),
                 ("ones", [128, 128]), ("cmask", [128, 128])]:
        I[n] = din("c_" + n, s)
    O = {}
    O["y_prompt"] = dout("y_prompt", [NB, TP, D])
    O["y_sample"] = dout("y_sample", [NB, TS, D])
    O["p_S"] = dout("p_S", [NB, 24, 64, 64])
    O["p_shift"] = dout("p_shift", [NB, RWP])
    O["p_k"] = dout("p_k", [NB, TP, MIXW])
    O["p_v"] = dout("p_v", [NB, TP, MIXW])
    O["p_mk"] = dout("p_mk", [2, NB, NMEM, MEMW])
    O["p_mv"] = dout("p_mv", [2, NB, NMEM, MEMW])
    O["s_S"] = dout("s_S", [NB, 24, 64, 64])
    O["s_shift"] = dout("s_shift", [NB, RWP])
    O["s_k"] = dout("s_k", [NB, TS, MIXW])
    O["s_v"] = dout("s_v", [NB, TS, MIXW])

    seqs = []
    for b in range(NB):
        seqs.append(dict(kind="p", b=b, T=TP, C=128, x=I["x_prompt"][b], past=0, name="p%d" % b,
                         y=O["y_prompt"][b], oS=O["p_S"][b], oshift=O["p_shift"][b], ok=O["p_k"][b], ov=O["p_v"][b]))
    for b in range(NB):
        seqs.append(dict(kind="s", b=b, T=TS, C=TS, x=I["x_sample"][b], past=PAST, name="s%d" % b,
                         y=O["y_sample"][b], oS=O["s_S"][b], oshift=O["s_shift"][b], ok=O["s_k"][b], ov=O["s_v"][b]))
    for sq in seqs:
        T = sq["T"]
        sq["qT"] = dscr("qT_" + sq["name"], [12, 128, T], BF16)
        sq["sgT"] = dscr("sgT_" + sq["name"], [12, 128, T])
        sq["oT"] = dscr("oT_" + sq["name"], [12, 128, T])
        sq["x1"] = dscr("x1_" + sq["name"], [T, D])

    with contextlib.ExitStack() as st:
        sb = lambda n, s: st.enter_context(nc.sbuf_tensor(n, list(s), F32))
        ps = [st.enter_context(nc.psum_tensor("ps%d" % i, [128, 512], F32)) for i in range(8)]
        ident = sb("ident", [128, 128]); mask2 = sb("mask2", [128, 256]); masksl = sb("masksl", [128, 128])
        bones = sb("bones", [128, 128]); ones = sb("ones", [128, 128]); cmask = sb("cmask", [128, 128])
        B = Bld(nc, P, ident)
        for t, n in [(ident, "ident"), (mask2, "mask2"), (masksl, "masksl"), (bones, "bones"), (ones, "ones"), (cmask, "cmask")]:
            B.dma(t[:], I[n])
        stgA = sb("stgA", [128, 128]); stgB = sb("stgB", [128, 128]); colA = sb("colA", [128, 128]); colB = sb("colB", [128, 128])
        B.memset("pool", stgA[:], 0.0); B.memset("pool", stgB[:], 0.0)
        rowsA = [("mu", I["rw_mu"], 37)] + [(n, I["rw_" + n], 12) for n in ["w0", "a0", "k_k", "k_a", "r_k", "ln_w", "ln_b"]]
        rowsB = [("nw0", I["norm_w"][0], 8), ("nw1", I["norm_w"][1], 8), ("mnw0", I["mem_norm_w"][0], 8),
                 ("mnw1", I["mem_norm_w"][1], 8), ("subln", I["df_subln"], 1)]
        cols = {}
        for stg, col, rows in ((stgA, colA, rowsA), (stgB, colB, rowsB)):
            r0 = 0
            for n, src, k in rows:
                B.dma(stg[r0:r0 + k, :], src.rearrange("(c p) -> c p", p=128))
                cols[n] = col[:, r0:r0 + k]
                r0 += k
            B.tr(ps[0][:, 0:128], stg[:])
            B.copy("dve", col[:], ps[0][:, 0:128])
        mu, w0, a0, k_k, k_a, r_k, ln_w, ln_b = [cols[n] for n in ["mu", "w0", "a0", "k_k", "k_a", "r_k", "ln_w", "ln_b"]]
        nw0, nw1, mnw0, mnw1, subln = [cols[n] for n in ["nw0", "nw1", "mnw0", "mnw1", "subln"]]
        WUP = sb("WUP", [128, MIXW]); AUP = sb("AUP", [128, MIXW])
        B.memset("pool", WUP[64:128, :], 0.0); B.memset("pool", AUP[0:64, :], 0.0)
        B.dma(WUP[0:64, :], I["rw_w_up"]); B.dma(AUP[64:128, :], I["rw_a_up"])
        lam_init = 0.8 - 0.6 * math.exp(-0.3 * 1)
        lq = sb("lq", [128, 4, 64]); lsum = sb("lsum", [128, 2]); nlam = sb("nlam", [128, 1]); lprod = sb("lprod", [128, 2, 64])
        for i, n in enumerate(["df_lq1", "df_lk1", "df_lq2", "df_lk2"]):
            B.dma(lq[:, i, :], I[n].partition_broadcast(128))
        B.tt("dve", lprod[:, 0, :], lq[:, 0, :], lq[:, 1, :], ALU.mult)
        B.tt("dve", lprod[:, 1, :], lq[:, 2, :], lq[:, 3, :], ALU.mult)
        P.op("dve", lambda e: e.tensor_reduce(out=lsum[:], in_=lprod[:], axis=mybir.AxisListType.X, op=ALU.add), [lprod], [lsum])
        B.act(lsum[:], lsum[:], AF.Exp)
        B.tt("dve", nlam[:], lsum[:, 1:2], lsum[:, 0:1], ALU.subtract)
        B.ts("dve", nlam[:], nlam[:], -lam_init, ALU.add)

        with contextlib.ExitStack() as st0:
            cin = [st0.enter_context(nc.sbuf_tensor("cin%d" % i, [128, 2048], F32)) for i in range(2)]
            cout = [st0.enter_context(nc.sbuf_tensor("cout%d" % i, [128, 2048], BF16)) for i in range(2)]
            ci = 0
            for src, dst, C_ in ((I["rw_in"], WB["rw_in"], RWIN), (I["df_in"], WB["df_in"], DFIN)):
                nch = C_ // 128
                for r0 in range(0, D, 128):
                    d = r0 // 128
                    for c0 in range(0, nch, 16):
                        n = min(16, nch - c0)
                        a_, b_ = cin[ci % 2], cout[ci % 2]; ci += 1
                        B.dma(a_[:, 0:n * 128], src[r0:r0 + 128, c0 * 128:(c0 + n) * 128])
                        B.copy("act" if ci % 2 else "dve", b_[:, 0:n * 128], a_[:, 0:n * 128])
                        B.dma(dst[c0:c0 + n, :, d * 128:(d + 1) * 128].rearrange("c p x -> p c x"),
                              b_[:, 0:n * 128].rearrange("p (c x) -> p c x", x=128))
            wsrc2 = I["w_out"].rearrange("l r c -> (l r) c")
            for r0 in range(0, 2 * BR, 256):
                a_, b_ = cin[ci % 2], cout[ci % 2]; ci += 1
                B.dma(a_[:].rearrange("p (r c) -> p r c", r=2), wsrc2[r0:r0 + 256, :].rearrange("(r p) c -> p r c", p=128))
                B.copy("act" if ci % 2 else "dve", b_[:], a_[:])
                B.dma(WB["w_out"][r0:r0 + 256, :].rearrange("(r p) c -> p r c", p=128), b_[:].rearrange("p (r c) -> p r c", r=2))
        P.barrier()
        st1 = contextlib.ExitStack()
        sb = lambda n, s, dt=F32: st1.enter_context(nc.sbuf_tensor(n, list(s), dt))
        xt = sb("xt", [128, D]); xn = sb("xn", [128, D]); hT = sb("hT", [128, 8, 128], BF16)
        rstat = sb("rstat", [128, 4])
        NWB = 4
        Wb = [sb("W%d" % i, [128, 4, 1024], BF16) for i in range(NWB)]
        GTb = sb("GTb", [128, 16, 128], BF16); YTb = sb("YTb", [128, 12, 128], BF16)
        raw = [sb("raw0", [128, 130]), sb("raw1", [128, 130])]
        dif = [sb("dif0", [128, 128]), sb("dif1", [128, 128])]
        PS = sb("PS", [128, 37, 128])
        plast = sb("plast", [128, 37]); plstg = stgA
        SG = sb("SG", [128, 12, 128]); AA = sb("AA", [128, 12, 128]); KK = sb("KK", [128, 12, 128])
        CS = sb("CS", [128, 12, 128]); EP = sb("EP", [128, 12, 128]); EM = sb("EM", [128, 12, 128]); EQ = sb("EQ", [128, 12, 128])
        TMP = sb("TMP", [128, 12, 128]); BON = sb("BON", [128, 12, 128]); YT = SG
        onesrow = sb("onesrow", [128, 128])
        ARs = [sb("AR_%d" % i, [128, 256]) for i in range(2)]
        AZs = [[sb("AZ0_%d" % i, [128, 128]), sb("AZ1_%d" % i, [128, 128])] for i in range(2)]
        BZs = [[sb("BZ0_%d" % i, [128, 128]), sb("BZ1_%d" % i, [128, 128])] for i in range(2)]
        KZs = [[sb("KZ0_%d" % i, [128, 128]), sb("KZ1_%d" % i, [128, 128])] for i in range(2)]
        BGs = [sb("BG_%d" % i, [128, 256]) for i in range(2)]
        TMXs = [sb("TMX_%d" % i, [128, 3, 128]) for i in range(2)]
        AZ = AZs[0] + AZs[1]; BZ = BZs[0] + BZs[1]; KZ = KZs[0] + KZs[1]
        MA = [sb("MA0", [128, 256]), sb("MA1", [128, 256])]
        MB = [sb("MB0", [128, 256]), sb("MB1", [128, 256])]
        PXa = [[sb("PX%d%d" % (h, i), [128, 192]) for i in range(2)] for h in range(2)]
        PTa = [[sb("PT%d%d" % (h, i), [128, 128]) for i in range(2)] for h in range(2)]
        UP = sb("UP", [128, 128]); STMP = sb("STMP", [128, 128])
        Hone = sb("Hstate", [128, 12, 128])
        Hbd = {sq["name"]: Hone for sq in seqs}
        GT = sb("GT", [128, 16, 128]); QM = sb("QM", [128, 4, 128]); EE = sb("EE", [128, 2, 128])
        YM = sb("YM", [128, 4, 128]); RD = sb("RD", [128, 128])
        x1t = sb("x1t", [128, D]); KVT = sb("KVT", [128, 512])
        mkT = [sb("mkT0", [128, 4, NMEM]), sb("mkT1", [128, 4, NMEM])]
        mvS = [sb("mv0", [128, 2, MEMW]), sb("mv1", [128, 2, MEMW])]
        memx = GT[:].rearrange("p a b -> p (a b)").rearrange("p (m d) -> p m d", m=2)
        memh = PS[:, 0:16, :].rearrange("p a b -> p (a b)").rearrange("p (d m) -> p d m", d=8)
        memo = TMP[:, 0:8, :].rearrange("p a b -> p (a b)").rearrange("p (m f) -> p m f", m=2)
        Wf = [BON[:, 0:8, :], CS[:, 0:8, :]]
        B.memset("pool", onesrow[:], 1.0)
        for t in AZ + BZ + KZ:
            B.memset("pool", t[:], 0.0)

        def rstd_from_ss(ss, n, TT, eps):
            B.ts("dve", ss, ss, 1.0 / n, ALU.mult, eps, ALU.add)
            B.act(ss, ss, AF.Ln)
            B.act(ss, ss, AF.Exp, scale=-0.5)

        def rms_to_hT(xsrc, TT, nwc, hdst, junk):
            B.act(junk[:TT, :], xsrc[:TT, :], AF.Square, accum_out=rstat[:TT, 0:1])
            rstd_from_ss(rstat[:TT, 0:1], D, TT, EPS)
            B.ts("dve", junk[:TT, :], xsrc[:TT, :], rstat[:TT, 0:1], ALU.mult)
            for g in range(2):
                for d4 in range(4):
                    d = g * 4 + d4
                    B.tr(ps[g][:, d4 * 128:d4 * 128 + TT], junk[:TT, d * 128:(d + 1) * 128])
                for d4 in range(4):
                    d = g * 4 + d4
                    B.ts("dve", hdst[:, d, 0:TT], ps[g][:, d4 * 128:d4 * 128 + TT], nwc[:, d:d + 1], ALU.mult)

        wctr = [0]

        def proj_chunks(Wsrc, f0, nf, hsrc, TT, consume):
            f = f0
            while f < f0 + nf:
                n = min(4, f0 + nf - f)
                wb = Wb[wctr[0] % NWB]; pb = ps[wctr[0] % 2]; wctr[0] += 1
                B.dma(wb[:, 0:n, :], Wsrc[f:f + n].rearrange("c p x -> p c x"))
                for j in range(n):
                    for d in range(8):
                        B.mm(pb[:, j * 128:j * 128 + TT], wb[:, j, d * 128:(d + 1) * 128], hsrc[:, d, 0:TT],
                             start=(d == 0), stop=(d == 7))
                    consume(f + j, pb[:, j * 128:j * 128 + TT])
                f += n

        def mem_attn(qm, TT, li, ymdst):
            for h in range(4):
                for m in range(2):
                    B.mm(ps[2][:, m * 128:m * 128 + TT], mkT[li][:, h, m * 128:(m + 1) * 128], qm[:, h, 0:TT])
                    B.act(EE[:, m, 0:TT], ps[2][:, m * 128:m * 128 + TT], AF.Exp, scale=128 ** -0.5)
                for m in range(2):
                    B.mm(ps[3][:, 0:TT], mvS[li][:, m, h * 128:(h + 1) * 128], EE[:, m, 0:TT], start=(m == 0), stop=(m == 1))
                for m in range(2):
                    B.mm(ps[3][:, 128:128 + TT], ones[:], EE[:, m, 0:TT], start=(m == 0), stop=(m == 1))
                B.recip(RD[:, 0:TT], ps[3][:, 128:128 + TT])
                B.tt("dve", ymdst[:, h, 0:TT], ps[3][:, 0:TT], RD[:, 0:TT], ALU.mult)

        def out_proj(li, c0, nchunks, TT, xres, xdst):
            c = 0
            first = True
            while c < nchunks:
                n = min(4, nchunks - c)
                wb = Wb[wctr[0] % NWB]; wctr[0] += 1
                wv = wb[:].rearrange("p a b -> p (a b)")
                r0 = li * BR + (c0 + c) * 128
                B.dma(wv[:, 0:n * D].rearrange("p (c f) -> p c f", f=D),
                      WB["w_out"][r0:r0 + n * 128, :].rearrange("(c p) f -> p c f", p=128))
                for j in range(n):
                    for hf in range(2):
                        B.mm(ps[4 + hf][0:TT, :], GTb[:, c0 + c + j, 0:TT], wv[:, j * D + hf * 512:j * D + hf * 512 + 512],
                             start=first, stop=(c + j == nchunks - 1))
                    first = False
                c += n
            for hf in range(2):
                B.tt("dve", xdst[0:TT, hf * 512:(hf + 1) * 512], ps[4 + hf][0:TT, :], xres[0:TT, hf * 512:(hf + 1) * 512], ALU.add)

        def phase_mem(sq):
            b = sq["b"]
            for li in range(2):
                if sq["kind"] == "p":
                    B.dma(memx[:], I["mem_prompt"][b].rearrange("(m p) d -> p m d", p=128))
                    mnw = mnw0 if li == 0 else mnw1
                    for m in range(2):
                        rms_to_hT(memx[:, m, :], 128, mnw, memh[:, :, m * 128:(m + 1) * 128], xn)
                    for h in range(4):
                        wb = Wf[wctr[0] % 2]; wctr[0] += 1
                        B.dma(wb, I["w_mem_k"][li][:, h * 128:(h + 1) * 128].rearrange("(d p) f -> p d f", p=128))
                        for d in range(8):
                            B.mm(ps[2][:, 0:NMEM], wb[:, d, :], memh[:, d, :], start=(d == 0), stop=(d == 7))
                        B.evac(mkT[li][:, h, :], ps[2][:, 0:NMEM])
                    for which, wsrc, odst in ((0, I["w_mem_k"][li], O["p_mk"][li, b]), (1, I["w_mem_v"][li], O["p_mv"][li, b])):
                        for fh in range(4):
                            wb = Wf[wctr[0] % 2]; wctr[0] += 1
                            B.dma(wb, wsrc[:, fh * 128:(fh + 1) * 128].rearrange("(d p) f -> p d f", p=128))
                            for m in range(2):
                                for d in range(8):
                                    B.mm(ps[3][:, m * 128:(m + 1) * 128], memh[:, d, m * 128:(m + 1) * 128], wb[:, d, :],
                                         start=(d == 0), stop=(d == 7))
                                dst = (mvS[li] if which == 1 else memo)
                                B.evac(dst[:, m, fh * 128:(fh + 1) * 128], ps[3][:, m * 128:(m + 1) * 128])
                        src = mvS[li] if which == 1 else memo
                        B.dma(odst.rearrange("(m p) f -> p m f", p=128), src[:])
                else:
                    B.dma(memo[:], I["cache_mem_k"][li, b].rearrange("(m p) f -> p m f", p=128))
                    B.dma(mvS[li][:], I["cache_mem_v"][li, b].rearrange("(m p) f -> p m f", p=128))
                    for h in range(4):
                        for m in range(2):
                            B.tr(ps[2][:, m * 128:(m + 1) * 128], memo[:, m, h * 128:(h + 1) * 128])
                        B.evac(mkT[li][:, h, :], ps[2][:, 0:NMEM])

        def rwkv_scan(sq, TT):
            C = TT
            H = Hbd[sq["name"]]
            nlev = int(math.ceil(math.log2(C)))
            m2v = mask2[:].rearrange("p (a c) -> p a c", a=2)
            for pr in range(12):
                AR, AZ, BZ, KZ, BG, TMX = ARs[pr % 2], AZs[pr % 2], BZs[pr % 2], KZs[pr % 2], BGs[pr % 2], TMXs[pr % 2]
                rP, kP, vP = PS[:, pr, 0:C], PS[:, 12 + pr, 0:C], PS[:, 24 + pr, 0:C]
                B.stt(AR[:, 0:C], KK[:, pr, 0:C], -1.0, EQ[:, pr, 0:C], ALU.mult, ALU.mult)
                B.tt("pool", AR[:, 128:128 + C], rP, EP[:, pr, 0:C], ALU.mult)
                for h in range(2):
                    hs = slice(h * 64, h * 64 + 64)
                    B.copy("pool", AZ[h][hs, 0:C], AR[hs, 0:C])
                    B.tt("dve" if h else "pool", BZ[h][hs, 0:C], AA[hs, pr, 0:C], EM[hs, pr, 0:C], ALU.mult)
                    B.tt("pool" if h else "dve", KZ[h][hs, 0:C], kP[hs, :], EM[hs, pr, 0:C], ALU.mult)
                gC = EP[:, pr, C - 1:C]
                B.stt(BG[:, 0:C], AA[:, pr, 0:C], gC, EM[:, pr, 0:C], ALU.mult, ALU.mult)
                B.stt(BG[:, 128:128 + C], kP, gC, EM[:, pr, 0:C], ALU.mult, ALU.mult)
                B.tr(ps[3][0:C, 0:128], BG[:, 0:C]); B.tr(ps[3][0:C, 128:256], BG[:, 128:128 + C]); B.tr(ps[3][0:C, 256:384], vP)
                B.evac(TMX[0:C, :, :].rearrange("p a b -> p (a b)"), ps[3][0:C, 0:384])
                Vp = TMX[0:C, 2, :]
                for h in range(2):
                    B.mm(ps[4][0:C, h * 256:h * 256 + 128 + C], BZ[h][:, 0:C], AR[:, 0:128 + C])
                    B.mm(ps[5][0:C, h * 256:h * 256 + 128 + C], KZ[h][:, 0:C], AR[:, 0:128 + C])
                for h in range(2):
                    B.tt("dve", MA[h][0:C, :].rearrange("p (a c) -> p a c", a=2)[:, :, 0:C],
                         ps[4][0:C, h * 256:(h + 1) * 256].rearrange("p (a c) -> p a c", a=2)[:, :, 0:C], m2v[0:C, :, 0:C], ALU.mult)
                    B.tt("dve", MB[h][0:C, :].rearrange("p (a c) -> p a c", a=2)[:, :, 0:C],
                         ps[5][0:C, h * 256:(h + 1) * 256].rearrange("p (a c) -> p a c", a=2)[:, :, 0:C], m2v[0:C, :, 0:C], ALU.mult)
                for h in range(2):
                    B.mm(ps[6][0:C, h * 128:h * 128 + C], AZ[h][:, 0:C], BZ[h][:, 0:C])
                    B.tt("dve", PXa[h][0][0:C, 0:C], ps[6][0:C, h * 128:h * 128 + C], masksl[0:C, 0:C], ALU.mult)
                B.mm(ps[7][0:C, 0:128], AR[:, 0:C], H[:, pr, :], start=True, stop=False)
                for h in range(2):
                    B.mm(ps[7][0:C, h * 64:h * 64 + 64], MB[h][0:C, 0:C], Vp[:, h * 64:h * 64 + 64], start=False, stop=(h == 1))
                for h in range(2):
                    B.evac(PXa[h][0][0:C, 128:192], ps[7][0:C, h * 64:h * 64 + 64])
                cur = [0, 0]
                for lv in range(nlev):
                    last = lv == nlev - 1
                    for h in range(2):
                        Pm = PXa[h][cur[h]]
                        PT = MA[h][0:C, 0:C] if lv == 0 else PTa[h][cur[h]][0:C, 0:C]
                        pa_, pb_ = ps[4 + 2 * h], ps[5 + 2 * h]
                        if last:
                            B.mm(pa_[0:C, 128:192], PT, Pm[0:C, 128:192])
                            B.tt("dve", UP[0:C, h * 64:h * 64 + 64], pa_[0:C, 128:192], Pm[0:C, 128:192], ALU.add)
                        else:
                            nxt = 1 - cur[h]
                            if C == 128:
                                B.mm(pa_[0:C, 0:192], PT, Pm[0:C, 0:192])
                            else:
                                B.mm(pa_[0:C, 0:C], PT, Pm[0:C, 0:C])
                                B.mm(pa_[0:C, 128:192], PT, Pm[0:C, 128:192])
                            B.mm(pb_[0:C, 0:C], Pm[0:C, 0:C], PT)
                            B.copy("act", PXa[h][nxt][0:C, 0:C], pa_[0:C, 0:C])
                            B.tt("dve", PXa[h][nxt][0:C, 128:192], pa_[0:C, 128:192], Pm[0:C, 128:192], ALU.add)
                            B.copy("act" if h else "dve", PTa[h][nxt][0:C, 0:C], pb_[0:C, 0:C])
                            cur[h] = nxt
                for h in range(2):
                    hs = slice(h * 64, h * 64 + 64)
                    B.mm(ps[2][:, h * 128:h * 128 + C], H[:, pr, :], AR[:, 128:128 + C], start=True, stop=False)
                    B.mm(ps[2][:, h * 128:h * 128 + C], UP[0:C, :], MA[h][0:C, 128:128 + C], start=False, stop=False)
                    B.mm(ps[2][:, h * 128:h * 128 + C], Vp, MB[h][0:C, 128:128 + C], start=False, stop=True)
                    B.copy("act", YT[hs, pr, 0:C], ps[2][hs, h * 128:h * 128 + C])
                B.mm(ps[3][:, 384:512], TMX[0:C, 0, :], UP[0:C, :], start=True, stop=False)
                B.mm(ps[3][:, 384:512], TMX[0:C, 1, :], Vp, start=False, stop=True)
                B.tt("dve", STMP[:], ps[3][:, 384:512], bones[:], ALU.mult)
                B.stt(H[:, pr, :], H[:, pr, :], gC, STMP[:], ALU.mult, ALU.add)

        def phase1_tile(sq, ti):
            TT = sq["C"]; t0 = ti * TT; nm = sq["name"]
            B.dma(xt[0:TT, :], sq["x"][t0:t0 + TT, :])
            rms_to_hT(xt, TT, nw0, hT, xn)
            cnt = [0]

            def cons_p(f, pap):
                rw = raw[cnt[0] % 2]; df = dif[cnt[0] % 2]; cnt[0] += 1
                B.copy("pool", rw[:, 0:1], plast[:, f:f + 1])
                B.copy("act", rw[:, 1:TT + 1], pap)
                B.copy("pool", plast[:, f:f + 1], rw[:, TT:TT + 1])
                B.tt("pool", df[:, 0:TT], rw[:, 0:TT], rw[:, 1:TT + 1], ALU.subtract)
                B.stt(PS[:, f, 0:TT], df[:, 0:TT], mu[:, f:f + 1], rw[:, 1:TT + 1], ALU.mult, ALU.add)
            proj_chunks(WB["rw_in"], 0, 37, hT, TT, cons_p)
            B.act(PS[0:64, 36, 0:TT], PS[0:64, 36, 0:TT], AF.Tanh)
            for f in range(12):
                B.mm(ps[2][:, (f % 4) * 128:(f % 4) * 128 + TT], WUP[:, f * 128:(f + 1) * 128], PS[:, 36, 0:TT])
                B.act(SG[:, f, 0:TT], ps[2][:, (f % 4) * 128:(f % 4) * 128 + TT], AF.Sigmoid, bias=w0[:, f:f + 1])
                B.mm(ps[3][:, (f % 4) * 128:(f % 4) * 128 + TT], AUP[:, f * 128:(f + 1) * 128], PS[:, 36, 0:TT])
                B.act(AA[:, f, 0:TT], ps[3][:, (f % 4) * 128:(f % 4) * 128 + TT], AF.Sigmoid, bias=a0[:, f:f + 1])
            for f in range(12):
                kP = PS[:, 12 + f, 0:TT]
                B.ts("pool", KK[:, f, 0:TT], kP, k_k[:, f:f + 1], ALU.mult)
                B.tt("pool", TMP[:, f, 0:TT], KK[:, f, 0:TT], KK[:, f, 0:TT], ALU.mult)
                B.mm(ps[2][:, (f % 4) * 128:(f % 4) * 128 + TT], bones[:], TMP[:, f, 0:TT])
                B.act(TMP[:, f, 0:TT], ps[2][:, (f % 4) * 128:(f % 4) * 128 + TT], AF.Sqrt)
                B.ts("dve", TMP[:, f, 0:TT], TMP[:, f, 0:TT], 1e-12, ALU.max)
                B.recip(TMP[:, f, 0:TT], TMP[:, f, 0:TT])
                B.tt("dve", KK[:, f, 0:TT], KK[:, f, 0:TT], TMP[:, f, 0:TT], ALU.mult)
                B.ts("dve", TMP[:, f, 0:TT], AA[:, f, 0:TT], -1.0, ALU.add, k_a[:, f:f + 1], ALU.mult)
                B.stt(kP, TMP[:, f, 0:TT], 1.0, kP, ALU.add, ALU.mult)
                B.tt("pool", AA[:, f, 0:TT], KK[:, f, 0:TT], AA[:, f, 0:TT], ALU.mult)
                B.stt(TMP[:, f, 0:TT], PS[:, f, 0:TT], r_k[:, f:f + 1], kP, ALU.mult, ALU.mult)
                B.mm(ps[3][:, (f % 4) * 128:(f % 4) * 128 + TT], bones[:], TMP[:, f, 0:TT])
                B.tt("dve", BON[:, f, 0:TT], ps[3][:, (f % 4) * 128:(f % 4) * 128 + TT], PS[:, 24 + f, 0:TT], ALU.mult)
                B.scan_add(CS[:, f, 0:TT], onesrow[:, 0:TT], SG[:, f, 0:TT])
            B.tt("pool", TMP[:, :, 0:TT], CS[:, :, 0:TT], SG[:, :, 0:TT], ALU.subtract)
            B.act(EP[:, :, 0:TT], CS[:, :, 0:TT], AF.Exp, scale=-DECAY_C)
            B.act(EM[:, :, 0:TT], CS[:, :, 0:TT], AF.Exp, scale=DECAY_C)
            B.act(EQ[:, :, 0:TT], TMP[:, :, 0:TT], AF.Exp, scale=-DECAY_C)
            rwkv_scan(sq, TT)
            for g in range(3):
                sl = slice(4 * g, 4 * g + 4)
                yv, tv, bv = YT[:, sl, 0:TT], TMP[:, sl, 0:TT], BON[:, sl, 0:TT]
                p2 = ps[2][:].rearrange("p (a c) -> p a c", a=4)[:, :, 0:TT]
                p3 = ps[3][:].rearrange("p (a c) -> p a c", a=4)[:, :, 0:TT]
                bc = lambda col: col[:, sl].unsqueeze(2).broadcast_to([128, 4, TT])
                for j in range(4):
                    B.mm(ps[2][:, j * 128:j * 128 + TT], bones[:], YT[:, 4 * g + j, 0:TT])
                B.stt(yv, p2, -1.0 / 64, yv, ALU.mult, ALU.add)
                B.tt("pool", tv, yv, yv, ALU.mult)
                for j in range(4):
                    B.mm(ps[3][:, j * 128:j * 128 + TT], bones[:], TMP[:, 4 * g + j, 0:TT])
                B.ts("dve", tv, p3, 1.0 / 64, ALU.mult, GN_EPS, ALU.add)
                B.act(tv, tv, AF.Sqrt)
                B.recip(tv, tv)
                B.tt("dve", yv, yv, tv, ALU.mult)
                B.tt("pool", yv, yv, bc(ln_w), ALU.mult)
                B.tt("dve", yv, yv, bc(ln_b), ALU.add)
                B.tt("pool", yv, yv, bv, ALU.add)
            def cons_qg(f, pap):
                if f < 41:
                    B.evac(QM[:, f - 37, 0:TT], pap)
                else:
                    B.act(GT[:, f - 41, 0:TT], pap, AF.Silu)
            proj_chunks(WB["rw_in"], 37, 20, hT, TT, cons_qg)
            mem_attn(QM, TT, 0, YM)
            B.tt("pool", GTb[:, 0:12, 0:TT], YT[:, :, 0:TT], GT[:, 0:12, 0:TT], ALU.mult)
            B.tt("pool", GTb[:, 12:16, 0:TT], YM[:, :, 0:TT], GT[:, 12:16, 0:TT], ALU.mult)
            out_proj(0, 0, 16, TT, xt, x1t)
            rms_to_hT(x1t, TT, nw1, hT, xn)

            def cons_l1(f, pap):
                if f < 12:
                    B.copy("act", YTb[:, f, 0:TT], pap)
                    B.dma(sq["qT"][f, :, t0:t0 + TT], YTb[:, f, 0:TT], eng="act")
                elif f < 36:
                    g = (f - 12) % 4
                    B.evac(TMP[:, g, 0:TT], pap)
                    B.tr(ps[3][0:TT, g * 128:(g + 1) * 128], TMP[:, g, 0:TT])
                    if g == 3:
                        B.copy("act", KVT[0:TT, :], ps[3][0:TT, :])
                        dst = sq["ok"] if f < 24 else sq["ov"]
                        c0 = ((f - 12) % 12 - 3) * 128
                        B.dma(dst[t0:t0 + TT, c0:c0 + 512], KVT[0:TT, :], eng="act")
                elif f < 40:
                    B.evac(QM[:, f - 36, 0:TT], pap)
                else:
                    B.act(GT[:, f - 40, 0:TT], pap, AF.Silu)
            proj_chunks(WB["df_in"], 0, 56, hT, TT, cons_l1)
            B.dma(sq["sgT"][:, :, t0:t0 + TT].rearrange("c p t -> p c t"), GT[:, 0:12, 0:TT], eng="act")
            mem_attn(QM, TT, 1, YM)
            B.tt("pool", GTb[:, 12:16, 0:TT], YM[:, :, 0:TT], GT[:, 12:16, 0:TT], ALU.mult)
            out_proj(1, 12, 4, TT, x1t, xt)
            B.dma(sq["x1"][t0:t0 + TT, :], xt[0:TT, :], eng="act")

        def phase1_seq(sq):
            nm = sq["name"]; b = sq["b"]; H = Hbd[nm]
            phase_mem(sq)
            if sq["kind"] == "p":
                B.memset("pool", plast[:], 0.0)
                B.memset("pool", H[:], 0.0)
            else:
                B.memset("pool", plstg[:], 0.0)
                B.dma(plstg[0:37, :], I["state_shift"][b].rearrange("(c p) -> c p", p=128))
                B.tr(ps[0][:, 0:128], plstg[:])
                B.copy("dve", plast[:], ps[0][:, 0:37])
                B.memset("pool", TMP[:], 0.0)
                for pr in range(12):
                    for h in range(2):
                        B.dma(TMP[h * 64:h * 64 + 64, pr, h * 64:h * 64 + 64], I["state_rwkv"][b, 2 * pr + h])
                for pr in range(12):
                    B.tr(ps[pr % 2][:, 0:128], TMP[:, pr, :])
                    B.evac(H[:, pr, :], ps[pr % 2][:, 0:128])
            for ti in range(sq["T"] // sq["C"]):
                phase1_tile(sq, ti)
            B.tr(ps[0][0:37, 0:128], plast[:])
            B.copy("dve", plstg[0:37, :], ps[0][0:37, 0:128])
            B.dma(sq["oshift"].rearrange("(c p) -> c p", p=128), plstg[0:37, :])
            for pr in range(12):
                B.tr(ps[pr % 2][:, 0:128], H[:, pr, :])
                B.evac(TMP[:, pr, :], ps[pr % 2][:, 0:128])
            for pr in range(12):
                for h in range(2):
                    B.dma(sq["oS"][2 * pr + h], TMP[h * 64:h * 64 + 64, pr, h * 64:h * 64 + 64])

        def phase2(sq):
            T = sq["T"]; past = sq["past"]; b = sq["b"]
            L = past + T
            nkt = (L + 127) // 128
            Lp = nkt * 128
            with contextlib.ExitStack() as st2:
                sb2 = lambda n, s, dt=F32: st2.enter_context(nc.sbuf_tensor(n + sq["name"], list(s), dt))
                Ktm_ = [sb2("Ktm%d" % i, [128, nkt, 128]) for i in range(2)]; Vtm_ = [sb2("Vtm%d" % i, [128, nkt, 128]) for i in range(2)]
                Vb_ = [sb2("Vb%d" % i, [128, nkt, 128], BF16) for i in range(2)]
                KZ1_ = [sb2("KZ1%d" % i, [128, Lp], BF16) for i in range(2)]; KZ2_ = [sb2("KZ2%d" % i, [128, Lp], BF16) for i in range(2)]
                QT_ = [sb2("QT%d" % i, [128, T], BF16) for i in range(2)]
                NE = 3
                E1 = [sb2("E1%d" % i, [128, 512], BF16) for i in range(NE)]
                E2 = [sb2("E2%d" % i, [128, 512], BF16) for i in range(NE)]
                AC1 = sb2("AC1", [128, 512]); AC2 = sb2("AC2", [128, 512]); R1 = sb2("R1", [128, 512]); R2 = sb2("R2", [128, 512])
                OT = sb2("OT", [128, 512]); SQ2 = sb2("SQ2", [128, 512]); kval = sb2("kval", [128, 1])
                for i in range(2):
                    B.memset("pool", KZ1_[i][64:128, :], 0.0); B.memset("pool", KZ2_[i][0:64, :], 0.0)
                lastn = L - (nkt - 1) * 128
                if lastn < 128:
                    B.memset("pool", kval[:], 0.0); B.memset("pool", kval[0:lastn, :], 1.0)
                for h in range(12):
                    Ktm, Vtm, Vb, KZ1, KZ2, QT = Ktm_[h % 2], Vtm_[h % 2], Vb_[h % 2], KZ1_[h % 2], KZ2_[h % 2], QT_[h % 2]
                    if lastn < 128:
                        B.memset("pool", Ktm[:, nkt - 1, :], 0.0); B.memset("pool", Vtm[:, nkt - 1, :], 0.0)
                    def ld(dst, src, n0, ntl):
                        for g0 in range(0, ntl, 8):
                            g = min(8, ntl - g0)
                            B.dma(dst[:, n0 + g0:n0 + g0 + g, :],
                                  src[g0 * 128:(g0 + g) * 128, h * 128:(h + 1) * 128].rearrange("(n p) f -> p n f", p=128))
                    if past:
                        ld(Ktm, I["cache_k"][b], 0, past // 128)
                        ld(Vtm, I["cache_v"][b], 0, past // 128)
                    nk0 = past // 128
                    if T >= 128:
                        ld(Ktm, sq["ok"], nk0, T // 128)
                        ld(Vtm, sq["ov"], nk0, T // 128)
                    else:
                        B.dma(Ktm[0:T, nk0, :], sq["ok"][:, h * 128:(h + 1) * 128], rk=[sq["ok"]])
                        B.dma(Vtm[0:T, nk0, :], sq["ov"][:, h * 128:(h + 1) * 128], rk=[sq["ov"]])
                    B.dma(QT[:], sq["qT"][h], rk=[sq["qT"]])
                    hk = nkt // 2
                    if hk:
                        B.copy("act", Vb[:, 0:hk, :], Vtm[:, 0:hk, :])
                    B.copy("dve", Vb[:, hk:nkt, :], Vtm[:, hk:nkt, :])
                    for kt in range(nkt):
                        B.tr(ps[kt % 2][:, 0:128], Ktm[:, kt, :])
                        B.copy("act", KZ1[0:64, kt * 128:(kt + 1) * 128], ps[kt % 2][0:64, 0:128])
                        B.copy("dve", KZ2[64:128, kt * 128:(kt + 1) * 128], ps[kt % 2][64:128, 0:128])
                    QB = min(512, T)
                    for q0 in range(0, T, QB):
                        kend = (past + q0 + QB + 127) // 128
                        pvA, pvB = ps[4], ps[5]

                        def geom(kt):
                            if sq["kind"] == "p":
                                r = kt - q0 // 128
                                return r, max(0, r) * 128
                            return -1, 0

                        def qk(kt):
                            r, c0 = geom(kt)
                            sA, sB = ps[2 * (kt % 2)], ps[2 * (kt % 2) + 1]
                            B.mm(sA[:, c0:QB], KZ1[:, kt * 128:(kt + 1) * 128], QT[:, q0 + c0:q0 + QB])
                            B.mm(sB[:, c0:QB], KZ2[:, kt * 128:(kt + 1) * 128], QT[:, q0 + c0:q0 + QB])

                        qk(0)
                        for kt in range(kend):
                            r, c0 = geom(kt)
                            sA, sB = ps[2 * (kt % 2)], ps[2 * (kt % 2) + 1]
                            e1, e2 = E1[kt % NE], E2[kt % NE]
                            B.act(e1[:, c0:QB], sA[:, c0:QB], AF.Exp, scale=0.125)
                            B.act(e2[:, c0:QB], sB[:, c0:QB], AF.Exp, scale=0.125)
                            if kt + 1 < kend:
                                qk(kt + 1)
                            if r >= 0:
                                B.tt("dve", e1[:, c0:c0 + 128], e1[:, c0:c0 + 128], cmask[:], ALU.mult)
                                B.tt("dve", e2[:, c0:c0 + 128], e2[:, c0:c0 + 128], cmask[:], ALU.mult)
                            if kt == nkt - 1 and lastn < 128:
                                B.ts("pool", e1[:, c0:QB], e1[:, c0:QB], kval[:, 0:1], ALU.mult)
                                B.ts("pool", e2[:, c0:QB], e2[:, c0:QB], kval[:, 0:1], ALU.mult)
                            B.mm(pvA[:, c0:QB], Vb[:, kt, :], e1[:, c0:QB], start=(kt == 0), stop=(kt == kend - 1))
                            B.mm(pvB[:, c0:QB], Vb[:, kt, :], e2[:, c0:QB], start=(kt == 0), stop=(kt == kend - 1))
                            if kt == 0:
                                B.copy("dve", AC1[:, 0:QB], e1[:, 0:QB]); B.copy("dve", AC2[:, 0:QB], e2[:, 0:QB])
                            else:
                                B.tt("dve", AC1[:, c0:QB], AC1[:, c0:QB], e1[:, c0:QB], ALU.add)
                                B.tt("dve", AC2[:, c0:QB], AC2[:, c0:QB], e2[:, c0:QB], ALU.add)
                        B.mm(ps[6][:, 0:QB], ones[:], AC1[:, 0:QB]); B.mm(ps[7][:, 0:QB], ones[:], AC2[:, 0:QB])
                        B.recip(R1[:, 0:QB], ps[6][:, 0:QB]); B.recip(R2[:, 0:QB], ps[7][:, 0:QB])
                        B.tt("dve", R1[:, 0:QB], pvA[:, 0:QB], R1[:, 0:QB], ALU.mult)
                        B.tt("dve", R2[:, 0:QB], pvB[:, 0:QB], R2[:, 0:QB], ALU.mult)
                        B.stt(OT[:, 0:QB], R2[:, 0:QB], nlam[:, 0:1], R1[:, 0:QB], ALU.mult, ALU.add)
                        B.tt("pool", SQ2[:, 0:QB], OT[:, 0:QB], OT[:, 0:QB], ALU.mult)
                        B.mm(ps[6][:, 0:QB], ones[:], SQ2[:, 0:QB])
                        B.ts("dve", SQ2[:, 0:QB], ps[6][:, 0:QB], 1.0 / 128, ALU.mult, EPS, ALU.add)
                        B.act(SQ2[:, 0:QB], SQ2[:, 0:QB], AF.Sqrt)
                        B.recip(SQ2[:, 0:QB], SQ2[:, 0:QB])
                        B.tt("dve", OT[:, 0:QB], OT[:, 0:QB], SQ2[:, 0:QB], ALU.mult)
                        B.ts("dve", OT[:, 0:QB], OT[:, 0:QB], subln[:, 0:1], ALU.mult, 1.0 - lam_init, ALU.mult)
                        B.dma(sq["oT"][h, :, q0:q0 + QB], OT[:, 0:QB], eng="act")

        def phase3(sq):
            T = sq["T"]; TT = sq["C"]
            with contextlib.ExitStack() as st3:
                sb3 = lambda n, s, dt=F32: st3.enter_context(nc.sbuf_tensor(n + sq["name"], list(s), dt))
                WO = sb3("WO", [128, 12, D], BF16); GTb = sb3("GTb3", [128, 12, 128], BF16)
                fnw = sb3("fnw", [128, D])
                B.dma(fnw[:], I["final_norm_w"].partition_broadcast(128))
                YT = sb3("YT3", [128, 12, 128]); GT = sb3("GT3", [128, 12, 128]); x1t = sb3("x1t3", [128, D])
                xt = sb3("xt3", [128, D]); xn = sb3("xn3", [128, D]); rstat = sb3("rstat3", [128, 4])
                for c4 in range(0, 12, 4):
                    B.dma(WO[:, c4:c4 + 4, :], WB["w_out"][BR + c4 * 128:BR + (c4 + 4) * 128, :].rearrange("(c p) f -> p c f", p=128))
                for ti in range(T // TT):
                    t0 = ti * TT
                    B.dma(YT[:, :, 0:TT], sq["oT"][:, :, t0:t0 + TT].rearrange("c p t -> p c t"), rk=[sq["oT"]])
                    B.dma(GT[:, 0:12, 0:TT], sq["sgT"][:, :, t0:t0 + TT].rearrange("c p t -> p c t"), rk=[sq["sgT"]])
                    B.dma(x1t[0:TT, :], sq["x1"][t0:t0 + TT, :], rk=[sq["x1"]])
                    B.tt("pool", GTb[:, 0:12, 0:TT], YT[:, :, 0:TT], GT[:, 0:12, 0:TT], ALU.mult)
                    for c in range(12):
                        for hf in range(2):
                            B.mm(ps[4 + hf][0:TT, :], GTb[:, c, 0:TT], WO[:, c, hf * 512:(hf + 1) * 512], start=(c == 0), stop=(c == 11))
                    for hf in range(2):
                        B.tt("dve", xt[0:TT, hf * 512:(hf + 1) * 512], ps[4 + hf][0:TT, :], x1t[0:TT, hf * 512:(hf + 1) * 512], ALU.add)
                    B.act(xn[0:TT, :], xt[0:TT, :], AF.Square, accum_out=rstat[0:TT, 0:1])
                    rstd_from_ss(rstat[0:TT, 0:1], D, TT, EPS)
                    B.stt(xn[0:TT, :], xt[0:TT, :], rstat[0:TT, 0:1], fnw[0:TT, :], ALU.mult, ALU.mult)
                    B.dma(sq["y"][t0:t0 + TT, :], xn[0:TT, :], eng="act")

        for sq in seqs:
            if "1" in PHASES:
                phase1_seq(sq)
        P.barrier()
        st1.close()
        for sq in seqs:
            if "2" in PHASES:
                phase2(sq)
                P.barrier()
        for sq in seqs:
            if "3" in PHASES:
                phase3(sq)
                P.barrier()
        P.emit()
    return nc, P


_IN_ORDER = ["x_prompt", "mem_prompt", "x_sample", "state_rwkv", "state_shift", "cache_k", "cache_v", "cache_mem_k",
             "cache_mem_v"]


def shard_inputs(inputs, n_cores, NB):
    c = host_consts()
    maps = []
    f = lambda a: np.ascontiguousarray(np.asarray(a, dtype=np.float32))
    for ci in range(n_cores):
        sl = slice(ci * NB, (ci + 1) * NB)
        m = {}
        m["x_prompt"] = f(inputs["x_prompt"][sl]); m["mem_prompt"] = f(inputs["mem_prompt"][sl])
        m["x_sample"] = f(inputs["x_sample"][sl])
        m["state_rwkv"] = f(inputs["state_rwkv"][0, sl]); m["state_shift"] = f(inputs["state_shift"][0, sl])
        ck = np.asarray(inputs["cache_k"]); cv = np.asarray(inputs["cache_v"])
        m["cache_k"] = f(ck[0, sl].reshape(NB, ck.shape[2], MIXW)); m["cache_v"] = f(cv[0, sl].reshape(NB, cv.shape[2], MIXW))
        m["cache_mem_k"] = f(np.asarray(inputs["cache_mem_k"])[:, sl].reshape(2, NB, NMEM, MEMW))
        m["cache_mem_v"] = f(np.asarray(inputs["cache_mem_v"])[:, sl].reshape(2, NB, NMEM, MEMW))
        for n in ["norm_w", "mem_norm_w", "w_mem_k", "w_mem_v", "w_out", "final_norm_w"]:
            m[n] = f(inputs[n])
        for n in ["rw_in", "rw_mu", "rw_w0", "rw_w_up", "rw_a0", "rw_a_up", "rw_k_k", "rw_k_a", "rw_ln_w", "rw_ln_b",
                  "df_in", "df_lq1", "df_lk1", "df_lq2", "df_lk2", "df_subln"]:
            m[n] = f(np.asarray(inputs[n])[0])
        m["rw_r_k"] = f(np.asarray(inputs["rw_r_k"])[0].reshape(MIXW))
        m.update({"c_" + k: v for k, v in c.items()})
        maps.append(m)
    return maps


def gather_outputs(results, NB, TP, TS):
    cat = lambda n, ax: np.concatenate([np.asarray(r[n]) for r in results], axis=ax)
    nb = NB * len(results)
    y_p = cat("y_prompt", 0); y_s = cat("y_sample", 0)
    p_S = cat("p_S", 0)[None]; p_sh = cat("p_shift", 0)[None]
    p_k = cat("p_k", 0).reshape(1, nb, TP, 12, 128); p_v = cat("p_v", 0).reshape(1, nb, TP, 12, 128)
    p_mk = cat("p_mk", 1).reshape(2, nb, NMEM, 4, 128); p_mv = cat("p_mv", 1).reshape(2, nb, NMEM, 4, 128)
    s_S = cat("s_S", 0)[None]; s_sh = cat("s_shift", 0)[None]
    s_k = cat("s_k", 0).reshape(1, nb, TS, 12, 128); s_v = cat("s_v", 0).reshape(1, nb, TS, 12, 128)
    return (y_p, y_s, p_S, p_sh, p_k, p_v, p_mk, p_mv, s_S, s_sh, s_k, s_v)


def kernel(**inputs):
    n_cores = 8
    xp = np.asarray(inputs["x_prompt"]); xs = np.asarray(inputs["x_sample"])
    NB = xp.shape[0] // n_cores
    TP, TS = xp.shape[1], xs.shape[1]
    PAST = np.asarray(inputs["cache_k"]).shape[2]
    nc, _ = build(TP, TS, PAST, NB)
    maps = shard_inputs(inputs, n_cores, NB)
    res = run_bass_kernel_spmd(nc, maps, core_ids=list(range(n_cores)))
    return gather_outputs(res.results, NB, TP, TS)
```

```python
import contextlib
import math
import numpy as np
import concourse.bass as bass
import concourse.mybir as mybir
from concourse.bass_utils import run_bass_kernel_spmd

F32 = mybir.dt.float32
BF16 = mybir.dt.bfloat16
AF = mybir.ActivationFunctionType
ALU = mybir.AluOpType

ENGS = ["pe", "dve", "act", "pool", "sp"]
NDMA = 48
import os
POOL_ENG = os.environ.get("POOL_ENG", "pool")

D = 1024
MIXW = 1536
MEMW = 512
BR = 2048
RWP = 4736
RWIN = RWP + MEMW + BR
DFIN = 3 * MIXW + MEMW + BR
NMEM = 256
EPS = 1e-6
GN_EPS = 64e-5
DECAY_C = math.exp(-0.5)


def _key(x):
    if isinstance(x, (str, tuple)):
        return x
    if hasattr(x, "tensor"):
        return x.tensor.name
    return x.name


class Prog:
    def __init__(self, nc, same_engine_sync=True):
        self.nc = nc
        self.ops = {e: [] for e in ENGS}
        self.n = {e: 0 for e in ENGS}
        self.seen = {e: {} for e in ENGS}
        self.lastw = {}
        self.readers = {}
        self.dma_val = [0] * NDMA
        self.dma_next = 0
        self.same = same_engine_sync
        self.nwaits = 0

    def _add(self, eng, waits, tok, is_war=False):
        if tok is None:
            return
        sk, val = tok
        if sk == ("e", eng):
            if eng == "pe" or eng == "sp":
                return
            if not self.same:
                return
        if self.seen[eng].get(sk, 0) >= val:
            return
        if waits.get(sk, 0) < val:
            waits[sk] = val

    def op(self, eng, fn, reads=(), writes=(), dma=False, inc=True, attach=None):
        if eng == "pool" and not dma:
            eng = POOL_ENG
        rk = [_key(r) for r in reads if r is not None]
        wk = [_key(w) for w in writes if w is not None]
        wk += [k for k in rk if isinstance(k, str) and k[:2] == "ps" and k[2:].isdigit() and k not in wk]
        waits = {}
        for k in rk:
            self._add(eng, waits, self.lastw.get(k))
        for k in wk:
            self._add(eng, waits, self.lastw.get(k))
            for sk, val in self.readers.get(k, {}).items():
                self._add(eng, waits, (sk, val), is_war=True)
        if dma:
            slot = self.dma_next
            self.dma_next = (self.dma_next + 1) % NDMA
            prev = self.dma_val[slot]
            if prev > 0:
                self._add(eng, waits, (("d", slot), prev))
            self.dma_val[slot] = prev + 16
            tok = (("d", slot), prev + 16)
            incinfo = ("d", slot)
        else:
            if inc:
                self.n[eng] += 1
                tok = (("e", eng), self.n[eng])
                incinfo = ("e", eng)
            else:
                tok = (("e", eng), self.n[eng] + 1)
                incinfo = None
        for sk, val in waits.items():
            self.seen[eng][sk] = val
        self.nwaits += len(waits)
        self.lastw_prev = {_key(attach): self.lastw.get(_key(attach))} if attach is not None else {}
        for k in wk:
            self.lastw[k] = tok
            self.readers[k] = {}
        for k in rk:
            d = self.readers.setdefault(k, {})
            if d.get(tok[0], 0) < tok[1]:
                d[tok[0]] = tok[1]
        wl = list(waits.items())
        if attach is not None and len(wl) > 1:
            t = self.lastw_prev.get(_key(attach))
            if t is not None:
                wl.sort(key=lambda w: w[0] == t[0])
        self.ops[eng].append((fn, wl, incinfo))
        return tok

    def barrier(self):
        for eng in ENGS:
            waits = {}
            for o in ENGS:
                if o != "sp" and self.n[o] > self.seen[eng].get(("e", o), 0):
                    if not (o == eng and eng == "pe"):
                        waits[("e", o)] = self.n[o]
            for i, v in enumerate(self.dma_val):
                if v > self.seen[eng].get(("d", i), 0):
                    waits[("d", i)] = v
            for sk, val in waits.items():
                self.seen[eng][sk] = val
            self.nwaits += len(waits)
            self.ops[eng].append((None, list(waits.items()), None))

    def emit(self):
        nc = self.nc
        with contextlib.ExitStack() as st:
            esem = {e: st.enter_context(nc.semaphore("s_" + e)) for e in ENGS}
            dsem = [st.enter_context(nc.semaphore("d_%d" % i)) for i in range(NDMA)]
            block = st.enter_context(nc.Block())

            def sem_of(sk):
                return esem[sk[1]] if sk[0] == "e" else dsem[sk[1]]

            fin = [(("e", e), self.n[e]) for e in ENGS if e != "sp" and self.n[e] > 0]
            fin += [(("d", i), v) for i, v in enumerate(self.dma_val) if v > 0]

            def run(e, name):
                for fn, waits, incinfo in self.ops[name]:
                    if fn is None:
                        for sk, val in waits:
                            e.wait_ge(sem_of(sk), val)
                        continue
                    for sk, val in waits:
                        e.wait_ge(sem_of(sk), val)
                    ins = fn(e)
                    if incinfo is not None:
                        ins.then_inc(sem_of(incinfo), 16 if incinfo[0] == "d" else 1)
                if name == "sp":
                    for sk, val in fin:
                        e.wait_ge(sem_of(sk), val)

            @block.tensor
            def _(e):
                run(e, "pe")

            @block.vector
            def _(e):
                run(e, "dve")

            @block.scalar
            def _(e):
                run(e, "act")

            @block.gpsimd
            def _(e):
                run(e, "pool")

            @block.sync
            def _(e):
                run(e, "sp")


class Bld:
    def __init__(self, nc, P, ident):
        self.nc, self.P, self.ident = nc, P, ident
        self.rr = 0

    def mm(self, out, lhsT, rhs, start=True, stop=True, rk=(), wk=None):
        self.P.op("pe", lambda e: e.matmul(out, lhsT, rhs, start=start, stop=stop),
                  [lhsT, rhs] + list(rk), [out] if wk is None else wk, inc=True, attach=lhsT)

    def tr(self, out, in_):
        k = in_.shape[0]
        idn = self.ident[0:k, 0:k]
        self.P.op("pe", lambda e: e.transpose(out, in_, idn), [in_, idn], [out], attach=in_)

    def act(self, out, in_, func, scale=1.0, bias=0.0, accum_out=None, eng="act"):
        r = [in_] + [x for x in (scale, bias) if not isinstance(x, (int, float))]
        w = [out] + ([accum_out] if accum_out is not None else [])
        if accum_out is None:
            fn = lambda e: e.activation(out=out, in_=in_, func=func, bias=bias, scale=scale)
        else:
            fn = lambda e: e.activation(out=out, in_=in_, func=func, bias=bias, scale=scale, accum_out=accum_out)
        self.P.op("act", fn, r, w)

    def tt(self, eng, out, in0, in1, op):
        self.P.op(eng, lambda e: e.tensor_tensor(out=out, in0=in0, in1=in1, op=op), [in0, in1], [out])

    def ts(self, eng, out, in0, s1, op0, s2=None, op1=None):
        r = [in0] + [x for x in (s1, s2) if x is not None and not isinstance(x, (int, float))]
        if op1 is None:
            fn = lambda e: e.tensor_scalar(out=out, in0=in0, scalar1=s1, scalar2=None, op0=op0)
        else:
            fn = lambda e: e.tensor_scalar(out=out, in0=in0, scalar1=s1, scalar2=s2, op0=op0, op1=op1)
        self.P.op(eng, fn, r, [out])

    def stt(self, out, in0, scalar, in1, op0, op1):
        r = [in0, in1] + ([] if isinstance(scalar, (int, float)) else [scalar])
        self.P.op("dve", lambda e: e.scalar_tensor_tensor(out=out, in0=in0, scalar=scalar, in1=in1, op0=op0, op1=op1), r, [out])

    def copy(self, eng, out, in_):
        if eng == "act":
            self.P.op("act", lambda e: e.copy(out=out, in_=in_), [in_], [out])
        else:
            self.P.op(eng, lambda e: e.tensor_copy(out=out, in_=in_), [in_], [out])

    def evac(self, out, in_):
        self.rr ^= 1
        self.copy("act" if self.rr else "dve", out, in_)

    def memset(self, eng, ap, val):
        self.P.op(eng, lambda e: e.memset(ap, val), [], [ap])

    def recip(self, out, in_):
        self.P.op("dve", lambda e: e.reciprocal(out=out, in_=in_), [in_], [out])

    def scan_add(self, out, ones, data, init=0.0):
        self.P.op("dve", lambda e: e.tensor_tensor_scan(out=out, data0=ones, data1=data, initial=init,
                                                        op0=ALU.mult, op1=ALU.add), [ones, data], [out])

    def dma(self, out, in_, eng="sp", rk=None, wk=None):
        self.P.op(eng, lambda e: e.dma_start(out=out, in_=in_), [in_] if rk is None else rk,
                  [out] if wk is None else wk, dma=True)


def host_consts():
    i = np.arange(128)
    c = {}
    c["ident"] = np.eye(128, dtype=np.float32)
    su = (i[None, :] > i[:, None]).astype(np.float32)
    iu = (i[None, :] >= i[:, None]).astype(np.float32)
    c["mask2"] = np.concatenate([su, iu], axis=1)
    c["masksl"] = (i[None, :] < i[:, None]).astype(np.float32)
    c["bones"] = (i[None, :] // 64 == i[:, None] // 64).astype(np.float32)
    c["ones"] = np.ones((128, 128), np.float32)
    c["cmask"] = (i[:, None] < (i[None, :] // 64 + 1) * 64).astype(np.float32)
    return c


def build(TP, TS, PAST, NB=2, PHASES="123"):
    nc = bass.Bass("TRN2", target_bir_lowering=False)
    P = Prog(nc)
    din = lambda n, s: nc.dram_tensor(n, list(s), F32, kind="ExternalInput").ap()
    dout = lambda n, s: nc.dram_tensor(n, list(s), F32, kind="ExternalOutput").ap()
    dscr = lambda n, s, dt=F32: nc.dram_tensor(n, list(s), dt, kind="Internal").ap()
    WB = {"rw_in": dscr("rw_in_b", [RWIN // 128, 128, D], BF16), "df_in": dscr("df_in_b", [DFIN // 128, 128, D], BF16),
          "w_out": dscr("w_out_b", [2 * BR, D], BF16)}

    I = {}
    I["x_prompt"] = din("x_prompt", [NB, TP, D])
    I["mem_prompt"] = din("mem_prompt", [NB, NMEM, D])
    I["x_sample"] = din("x_sample", [NB, TS, D])
    I["state_rwkv"] = din("state_rwkv", [NB, 24, 64, 64])
    I["state_shift"] = din("state_shift", [NB, RWP])
    I["cache_k"] = din("cache_k", [NB, PAST, MIXW])
    I["cache_v"] = din("cache_v", [NB, PAST, MIXW])
    I["cache_mem_k"] = din("cache_mem_k", [2, NB, NMEM, MEMW])
    I["cache_mem_v"] = din("cache_mem_v", [2, NB, NMEM, MEMW])
    for n, s in [("norm_w", [2, D]), ("mem_norm_w", [2, D]), ("w_mem_k", [2, D, MEMW]), ("w_mem_v", [2, D, MEMW]),
                 ("w_out", [2, BR, D]), ("final_norm_w", [D]), ("rw_in", [D, RWIN]), ("rw_mu", [RWP]),
                 ("rw_w0", [MIXW]), ("rw_w_up", [64, MIXW]), ("rw_a0", [MIXW]), ("rw_a_up", [64, MIXW]),
                 ("rw_k_k", [MIXW]), ("rw_k_a", [MIXW]), ("rw_r_k", [MIXW]), ("rw_ln_w", [MIXW]),
                 ("rw_ln_b", [MIXW]), ("df_in", [D, DFIN]), ("df_lq1", [64]), ("df_lk1", [64]),
                 ("df_lq2", [64]), ("df_lk2", [64]), ("df_subln", [128])]:
        I[n] = din(n, s)
    for n, s in [("ident", [128, 128]), ("mask2", [128, 256]), ("masksl", [128, 128]), ("bones", [128, 128]),
                 ("ones", [128, 128]), ("cmask", [128, 128])]:
        I[n] = din("c_" + n, s)
    O = {}
    O["y_prompt"] = dout("y_prompt", [NB, TP, D])
    O["y_sample"] = dout("y_sample", [NB, TS, D])
    O["p_S"] = dout("p_S", [NB, 24, 64, 64])
    O["p_shift"] = dout("p_shift", [NB, RWP])
    O["p_k"] = dout("p_k", [NB, TP, MIXW])
    O["p_v"] = dout("p_v", [NB, TP, MIXW])
    O["p_mk"] = dout("p_mk", [2, NB, NMEM, MEMW])
    O["p_mv"] = dout("p_mv", [2, NB, NMEM, MEMW])
    O["s_S"] = dout("s_S", [NB, 24, 64, 64])
    O["s_shift"] = dout("s_shift", [NB, RWP])
    O["s_k"] = dout("s_k", [NB, TS, MIXW])
    O["s_v"] = dout("s_v", [NB, TS, MIXW])

    seqs = []
    for b in range(NB):
        seqs.append(dict(kind="p", b=b, T=TP, C=128, x=I["x_prompt"][b], past=0, name="p%d" % b,
                         y=O["y_prompt"][b], oS=O["p_S"][b], oshift=O["p_shift"][b], ok=O["p_k"][b], ov=O["p_v"][b]))
    for b in range(NB):
        seqs.append(dict(kind="s", b=b, T=TS, C=TS, x=I["x_sample"][b], past=PAST, name="s%d" % b,
                         y=O["y_sample"][b], oS=O["s_S"][b], oshift=O["s_shift"][b], ok=O["s_k"][b], ov=O["s_v"][b]))
    for sq in seqs:
        T = sq["T"]
        sq["qT"] = dscr("qT_" + sq["name"], [12, 128, T], BF16)
        sq["sgT"] = dscr("sgT_" + sq["name"], [12, 128, T])
        sq["oT"] = dscr("oT_" + sq["name"], [12, 128, T])
        sq["x1"] = dscr("x1_" + sq["name"], [T, D])

    with contextlib.ExitStack() as st:
        sb = lambda n, s: st.enter_context(nc.sbuf_tensor(n, list(s), F32))
        ps = [st.enter_context(nc.psum_tensor("ps%d" % i, [128, 512], F32)) for i in range(8)]
        ident = sb("ident", [128, 128]); mask2 = sb("mask2", [128, 256]); masksl = sb("masksl", [128, 128])
        bones = sb("bones", [128, 128]); ones = sb("ones", [128, 128]); cmask = sb("cmask", [128, 128])
        B = Bld(nc, P, ident)
        for t, n in [(ident, "ident"), (mask2, "mask2"), (masksl, "masksl"), (bones, "bones"), (ones, "ones"), (cmask, "cmask")]:
            B.dma(t[:], I[n])
        stgA = sb("stgA", [128, 128]); stgB = sb("stgB", [128, 128]); colA = sb("colA", [128, 128]); colB = sb("colB", [128, 128])
        B.memset("pool", stgA[:], 0.0); B.memset("pool", stgB[:], 0.0)
        rowsA = [("mu", I["rw_mu"], 37)] + [(n, I["rw_" + n], 12) for n in ["w0", "a0", "k_k", "k_a", "r_k", "ln_w", "ln_b"]]
        rowsB = [("nw0", I["norm_w"][0], 8), ("nw1", I["norm_w"][1], 8), ("mnw0", I["mem_norm_w"][0], 8),
                 ("mnw1", I["mem_norm_w"][1], 8), ("subln", I["df_subln"], 1)]
        cols = {}
        for stg, col, rows in ((stgA, colA, rowsA), (stgB, colB, rowsB)):
            r0 = 0
            for n, src, k in rows:
                B.dma(stg[r0:r0 + k, :], src.rearrange("(c p) -> c p", p=128))
                cols[n] = col[:, r0:r0 + k]
                r0 += k
            B.tr(ps[0][:, 0:128], stg[:])
            B.copy("dve", col[:], ps[0][:, 0:128])
        mu, w0, a0, k_k, k_a, r_k, ln_w, ln_b = [cols[n] for n in ["mu", "w0", "a0", "k_k", "k_a", "r_k", "ln_w", "ln_b"]]
        nw0, nw1, mnw0, mnw1, subln = [cols[n] for n in ["nw0", "nw1", "mnw0", "mnw1", "subln"]]
        WUP = sb("WUP", [128, MIXW]); AUP = sb("AUP", [128, MIXW])
        B.memset("pool", WUP[64:128, :], 0.0); B.memset("pool", AUP[0:64, :], 0.0)
        B.dma(WUP[0:64, :], I["rw_w_up"]); B.dma(AUP[64:128, :], I["rw_a_up"])
        lam_init = 0.8 - 0.6 * math.exp(-0.3 * 1)
        lq = sb("lq", [128, 4, 64]); lsum = sb("lsum", [128, 2]); nlam = sb("nlam", [128, 1]); lprod = sb("lprod", [128, 2, 64])
        for i, n in enumerate(["df_lq1", "df_lk1", "df_lq2", "df_lk2"]):
            B.dma(lq[:, i, :], I[n].partition_broadcast(128))
        B.tt("dve", lprod[:, 0, :], lq[:, 0, :], lq[:, 1, :], ALU.mult)
        B.tt("dve", lprod[:, 1, :], lq[:, 2, :], lq[:, 3, :], ALU.mult)
        P.op("dve", lambda e: e.tensor_reduce(out=lsum[:], in_=lprod[:], axis=mybir.AxisListType.X, op=ALU.add), [lprod], [lsum])
        B.act(lsum[:], lsum[:], AF.Exp)
        B.tt("dve", nlam[:], lsum[:, 1:2], lsum[:, 0:1], ALU.subtract)
        B.ts("dve", nlam[:], nlam[:], -lam_init, ALU.add)

        with contextlib.ExitStack() as st0:
            cin = [st0.enter_context(nc.sbuf_tensor("cin%d" % i, [128, 2048], F32)) for i in range(2)]
            cout = [st0.enter_context(nc.sbuf_tensor("cout%d" % i, [128, 2048], BF16)) for i in range(2)]
            ci = 0
            for src, dst, C_ in ((I["rw_in"], WB["rw_in"], RWIN), (I["df_in"], WB["df_in"], DFIN)):
                nch = C_ // 128
                for r0 in range(0, D, 128):
                    d = r0 // 128
                    for c0 in range(0, nch, 16):
                        n = min(16, nch - c0)
                        a_, b_ = cin[ci % 2], cout[ci % 2]; ci += 1
                        B.dma(a_[:, 0:n * 128], src[r0:r0 + 128, c0 * 128:(c0 + n) * 128])
                        B.copy("act" if ci % 2 else "dve", b_[:, 0:n * 128], a_[:, 0:n * 128])
                        B.dma(dst[c0:c0 + n, :, d * 128:(d + 1) * 128].rearrange("c p x -> p c x"),
                              b_[:, 0:n * 128].rearrange("p (c x) -> p c x", x=128))
            wsrc2 = I["w_out"].rearrange("l r c -> (l r) c")
            for r0 in range(0, 2 * BR, 256):
                a_, b_ = cin[ci % 2], cout[ci % 2]; ci += 1
                B.dma(a_[:].rearrange("p (r c) -> p r c", r=2), wsrc2[r0:r0 + 256, :].rearrange("(r p) c -> p r c", p=128))
                B.copy("act" if ci % 2 else "dve", b_[:], a_[:])
                B.dma(WB["w_out"][r0:r0 + 256, :].rearrange("(r p) c -> p r c", p=128), b_[:].rearrange("p (r c) -> p r c", r=2))
        P.barrier()
        st1 = contextlib.ExitStack()
        sb = lambda n, s, dt=F32: st1.enter_context(nc.sbuf_tensor(n, list(s), dt))
        xt = sb("xt", [128, D]); xn = sb("xn", [128, D]); hT = sb("hT", [128, 8, 128], BF16)
        rstat = sb("rstat", [128, 4])
        NWB = 4
        Wb = [sb("W%d" % i, [128, 4, 1024], BF16) for i in range(NWB)]
        GTb = sb("GTb", [128, 16, 128], BF16); YTb = sb("YTb", [128, 12, 128], BF16)
        raw = [sb("raw0", [128, 130]), sb("raw1", [128, 130])]
        dif = [sb("dif0", [128, 128]), sb("dif1", [128, 128])]
        PS = sb("PS", [128, 37, 128])
        plast = sb("plast", [128, 37]); plstg = stgA
        SG = sb("SG", [128, 12, 128]); AA = sb("AA", [128, 12, 128]); KK = sb("KK", [128, 12, 128])
        CS = sb("CS", [128, 12, 128]); EP = sb("EP", [128, 12, 128]); EM = sb("EM", [128, 12, 128]); EQ = sb("EQ", [128, 12, 128])
        TMP = sb("TMP", [128, 12, 128]); BON = sb("BON", [128, 12, 128]); YT = SG
        onesrow = sb("onesrow", [128, 128])
        ARs = [sb("AR_%d" % i, [128, 256]) for i in range(2)]
        AZs = [[sb("AZ0_%d" % i, [128, 128]), sb("AZ1_%d" % i, [128, 128])] for i in range(2)]
        BZs = [[sb("BZ0_%d" % i, [128, 128]), sb("BZ1_%d" % i, [128, 128])] for i in range(2)]
        KZs = [[sb("KZ0_%d" % i, [128, 128]), sb("KZ1_%d" % i, [128, 128])] for i in range(2)]
        BGs = [sb("BG_%d" % i, [128, 256]) for i in range(2)]
        TMXs = [sb("TMX_%d" % i, [128, 3, 128]) for i in range(2)]
        AZ = AZs[0] + AZs[1]; BZ = BZs[0] + BZs[1]; KZ = KZs[0] + KZs[1]
        MA = [sb("MA0", [128, 256]), sb("MA1", [128, 256])]
        MB = [sb("MB0", [128, 256]), sb("MB1", [128, 256])]
        PXa = [[sb("PX%d%d" % (h, i), [128, 192]) for i in range(2)] for h in range(2)]
        PTa = [[sb("PT%d%d" % (h, i), [128, 128]) for i in range(2)] for h in range(2)]
        UP = sb("UP", [128, 128]); STMP = sb("STMP", [128, 128])
        Hone = sb("Hstate", [128, 12, 128])
        Hbd = {sq["name"]: Hone for sq in seqs}
        GT = sb("GT", [128, 16, 128]); QM = sb("QM", [128, 4, 128]); EE = sb("EE", [128, 2, 128])
        YM = sb("YM", [128, 4, 128]); RD = sb("RD", [128, 128])
        x1t = sb("x1t", [128, D]); KVT = sb("KVT", [128, 512])
        mkT = [sb("mkT0", [128, 4, NMEM]), sb("mkT1", [128, 4, NMEM])]
        mvS = [sb("mv0", [128, 2, MEMW]), sb("mv1", [128, 2, MEMW])]
        memx = GT[:].rearrange("p a b -> p (a b)").rearrange("p (m d) -> p m d", m=2)
        memh = PS[:, 0:16, :].rearrange("p a b -> p (a b)").rearrange("p (d m) -> p d m", d=8)
        memo = TMP[:, 0:8, :].rearrange("p a b -> p (a b)").rearrange("p (m f) -> p m f", m=2)
        Wf = [BON[:, 0:8, :], CS[:, 0:8, :]]
        B.memset("pool", onesrow[:], 1.0)
        for t in AZ + BZ + KZ:
            B.memset("pool", t[:], 0.0)

        def rstd_from_ss(ss, n, TT, eps):
            B.ts("dve", ss, ss, 1.0 / n, ALU.mult, eps, ALU.add)
            B.act(ss, ss, AF.Ln)
            B.act(ss, ss, AF.Exp, scale=-0.5)

        def rms_to_hT(xsrc, TT, nwc, hdst, junk):
            B.act(junk[:TT, :], xsrc[:TT, :], AF.Square, accum_out=rstat[:TT, 0:1])
            rstd_from_ss(rstat[:TT, 0:1], D, TT, EPS)
            B.ts("dve", junk[:TT, :], xsrc[:TT, :], rstat[:TT, 0:1], ALU.mult)
            for g in range(2):
                for d4 in range(4):
                    d = g * 4 + d4
                    B.tr(ps[g][:, d4 * 128:d4 * 128 + TT], junk[:TT, d * 128:(d + 1) * 128])
                for d4 in range(4):
                    d = g * 4 + d4
                    B.ts("dve", hdst[:, d, 0:TT], ps[g][:, d4 * 128:d4 * 128 + TT], nwc[:, d:d + 1], ALU.mult)

        wctr = [0]

        def proj_chunks(Wsrc, f0, nf, hsrc, TT, consume):
            f = f0
            while f < f0 + nf:
                n = min(4, f0 + nf - f)
                wb = Wb[wctr[0] % NWB]; pb = ps[wctr[0] % 2]; wctr[0] += 1
                B.dma(wb[:, 0:n, :], Wsrc[f:f + n].rearrange("c p x -> p c x"))
                for j in range(n):
                    for d in range(8):
                        B.mm(pb[:, j * 128:j * 128 + TT], wb[:, j, d * 128:(d + 1) * 128], hsrc[:, d, 0:TT],
                             start=(d == 0), stop=(d == 7))
                    consume(f + j, pb[:, j * 128:j * 128 + TT])
                f += n

        def mem_attn(qm, TT, li, ymdst):
            for h in range(4):
                for m in range(2):
                    B.mm(ps[2][:, m * 128:m * 128 + TT], mkT[li][:, h, m * 128:(m + 1) * 128], qm[:, h, 0:TT])
                    B.act(EE[:, m, 0:TT], ps[2][:, m * 128:m * 128 + TT], AF.Exp, scale=128 ** -0.5)
                for m in range(2):
                    B.mm(ps[3][:, 0:TT], mvS[li][:, m, h * 128:(h + 1) * 128], EE[:, m, 0:TT], start=(m == 0), stop=(m == 1))
                for m in range(2):
                    B.mm(ps[3][:, 128:128 + TT], ones[:], EE[:, m, 0:TT], start=(m == 0), stop=(m == 1))
                B.recip(RD[:, 0:TT], ps[3][:, 128:128 + TT])
                B.tt("dve", ymdst[:, h, 0:TT], ps[3][:, 0:TT], RD[:, 0:TT], ALU.mult)

        def out_proj(li, c0, nchunks, TT, xres, xdst):
            c = 0
            first = True
            while c < nchunks:
                n = min(4, nchunks - c)
                wb = Wb[wctr[0] % NWB]; wctr[0] += 1
                wv = wb[:].rearrange("p a b -> p (a b)")
                r0 = li * BR + (c0 + c) * 128
                B.dma(wv[:, 0:n * D].rearrange("p (c f) -> p c f", f=D),
                      WB["w_out"][r0:r0 + n * 128, :].rearrange("(c p) f -> p c f", p=128))
                for j in range(n):
                    for hf in range(2):
                        B.mm(ps[4 + hf][0:TT, :], GTb[:, c0 + c + j, 0:TT], wv[:, j * D + hf * 512:j * D + hf * 512 + 512],
                             start=first, stop=(c + j == nchunks - 1))
                    first = False
                c += n
            for hf in range(2):
                B.tt("dve", xdst[0:TT, hf * 512:(hf + 1) * 512], ps[4 + hf][0:TT, :], xres[0:TT, hf * 512:(hf + 1) * 512], ALU.add)

        def phase_mem(sq):
            b = sq["b"]
            for li in range(2):
                if sq["kind"] == "p":
                    B.dma(memx[:], I["mem_prompt"][b].rearrange("(m p) d -> p m d", p=128))
                    mnw = mnw0 if li == 0 else mnw1
                    for m in range(2):
                        rms_to_hT(memx[:, m, :], 128, mnw, memh[:, :, m * 128:(m + 1) * 128], xn)
                    for h in range(4):
                        wb = Wf[wctr[0] % 2]; wctr[0] += 1
                        B.dma(wb, I["w_mem_k"][li][:, h * 128:(h + 1) * 128].rearrange("(d p) f -> p d f", p=128))
                        for d in range(8):
                            B.mm(ps[2][:, 0:NMEM], wb[:, d, :], memh[:, d, :], start=(d == 0), stop=(d == 7))
                        B.evac(mkT[li][:, h, :], ps[2][:, 0:NMEM])
                    for which, wsrc, odst in ((0, I["w_mem_k"][li], O["p_mk"][li, b]), (1, I["w_mem_v"][li], O["p_mv"][li, b])):
                        for fh in range(4):
                            wb = Wf[wctr[0] % 2]; wctr[0] += 1
                            B.dma(wb, wsrc[:, fh * 128:(fh + 1) * 128].rearrange("(d p) f -> p d f", p=128))
                            for m in range(2):
                                for d in range(8):
                                    B.mm(ps[3][:, m * 128:(m + 1) * 128], memh[:, d, m * 128:(m + 1) * 128], wb[:, d, :],
                                         start=(d == 0), stop=(d == 7))
                                dst = (mvS[li] if which == 1 else memo)
                                B.evac(dst[:, m, fh * 128:(fh + 1) * 128], ps[3][:, m * 128:(m + 1) * 128])
                        src = mvS[li] if which == 1 else memo
                        B.dma(odst.rearrange("(m p) f -> p m f", p=128), src[:])
                else:
                    B.dma(memo[:], I["cache_mem_k"][li, b].rearrange("(m p) f -> p m f", p=128))
                    B.dma(mvS[li][:], I["cache_mem_v"][li, b].rearrange("(m p) f -> p m f", p=128))
                    for h in range(4):
                        for m in range(2):
                            B.tr(ps[2][:, m * 128:(m + 1) * 128], memo[:, m, h * 128:(h + 1) * 128])
                        B.evac(mkT[li][:, h, :], ps[2][:, 0:NMEM])

        def rwkv_scan(sq, TT):
            C = TT
            H = Hbd[sq["name"]]
            nlev = int(math.ceil(math.log2(C)))
            m2v = mask2[:].rearrange("p (a c) -> p a c", a=2)
            for pr in range(12):
                AR, AZ, BZ, KZ, BG, TMX = ARs[pr % 2], AZs[pr % 2], BZs[pr % 2], KZs[pr % 2], BGs[pr % 2], TMXs[pr % 2]
                rP, kP, vP = PS[:, pr, 0:C], PS[:, 12 + pr, 0:C], PS[:, 24 + pr, 0:C]
                B.stt(AR[:, 0:C], KK[:, pr, 0:C], -1.0, EQ[:, pr, 0:C], ALU.mult, ALU.mult)
                B.tt("pool", AR[:, 128:128 + C], rP, EP[:, pr, 0:C], ALU.mult)
                for h in range(2):
                    hs = slice(h * 64, h * 64 + 64)
                    B.copy("pool", AZ[h][hs, 0:C], AR[hs, 0:C])
                    B.tt("dve" if h else "pool", BZ[h][hs, 0:C], AA[hs, pr, 0:C], EM[hs, pr, 0:C], ALU.mult)
                    B.tt("pool" if h else "dve", KZ[h][hs, 0:C], kP[hs, :], EM[hs, pr, 0:C], ALU.mult)
                gC = EP[:, pr, C - 1:C]
                B.stt(BG[:, 0:C], AA[:, pr, 0:C], gC, EM[:, pr, 0:C], ALU.mult, ALU.mult)
                B.stt(BG[:, 128:128 + C], kP, gC, EM[:, pr, 0:C], ALU.mult, ALU.mult)
                B.tr(ps[3][0:C, 0:128], BG[:, 0:C]); B.tr(ps[3][0:C, 128:256], BG[:, 128:128 + C]); B.tr(ps[3][0:C, 256:384], vP)
                B.evac(TMX[0:C, :, :].rearrange("p a b -> p (a b)"), ps[3][0:C, 0:384])
                Vp = TMX[0:C, 2, :]
                for h in range(2):
                    B.mm(ps[4][0:C, h * 256:h * 256 + 128 + C], BZ[h][:, 0:C], AR[:, 0:128 + C])
                    B.mm(ps[5][0:C, h * 256:h * 256 + 128 + C], KZ[h][:, 0:C], AR[:, 0:128 + C])
                for h in range(2):
                    B.tt("dve", MA[h][0:C, :].rearrange("p (a c) -> p a c", a=2)[:, :, 0:C],
                         ps[4][0:C, h * 256:(h + 1) * 256].rearrange("p (a c) -> p a c", a=2)[:, :, 0:C], m2v[0:C, :, 0:C], ALU.mult)
                    B.tt("dve", MB[h][0:C, :].rearrange("p (a c) -> p a c", a=2)[:, :, 0:C],
                         ps[5][0:C, h * 256:(h + 1) * 256].rearrange("p (a c) -> p a c", a=2)[:, :, 0:C], m2v[0:C, :, 0:C], ALU.mult)
                for h in range(2):
                    B.mm(ps[6][0:C, h * 128:h * 128 + C], AZ[h][:, 0:C], BZ[h][:, 0:C])
                    B.tt("dve", PXa[h][0][0:C, 0:C], ps[6][0:C, h * 128:h * 128 + C], masksl[0:C, 0:C], ALU.mult)
                B.mm(ps[7][0:C, 0:128], AR[:, 0:C], H[:, pr, :], start=True, stop=False)
                for h in range(2):
                    B.mm(ps[7][0:C, h * 64:h * 64 + 64], MB[h][0:C, 0:C], Vp[:, h * 64:h * 64 + 64], start=False, stop=(h == 1))
                for h in range(2):
                    B.evac(PXa[h][0][0:C, 128:192], ps[7][0:C, h * 64:h * 64 + 64])
                cur = [0, 0]
                for lv in range(nlev):
                    last = lv == nlev - 1
                    for h in range(2):
                        Pm = PXa[h][cur[h]]
                        PT = MA[h][0:C, 0:C] if lv == 0 else PTa[h][cur[h]][0:C, 0:C]
                        pa_, pb_ = ps[4 + 2 * h], ps[5 + 2 * h]
                        if last:
                            B.mm(pa_[0:C, 128:192], PT, Pm[0:C, 128:192])
                            B.tt("dve", UP[0:C, h * 64:h * 64 + 64], pa_[0:C, 128:192], Pm[0:C, 128:192], ALU.add)
                        else:
                            nxt = 1 - cur[h]
                            if C == 128:
                                B.mm(pa_[0:C, 0:192], PT, Pm[0:C, 0:192])
                            else:
                                B.mm(pa_[0:C, 0:C], PT, Pm[0:C, 0:C])
                                B.mm(pa_[0:C, 128:192], PT, Pm[0:C, 128:192])
                            B.mm(pb_[0:C, 0:C], Pm[0:C, 0:C], PT)
                            B.copy("act", PXa[h][nxt][0:C, 0:C], pa_[0:C, 0:C])
                            B.tt("dve", PXa[h][nxt][0:C, 128:192], pa_[0:C, 128:192], Pm[0:C, 128:192], ALU.add)
                            B.copy("act" if h else "dve", PTa[h][nxt][0:C, 0:C], pb_[0:C, 0:C])
                            cur[h] = nxt
                for h in range(2):
                    hs = slice(h * 64, h * 64 + 64)
                    B.mm(ps[2][:, h * 128:h * 128 + C], H[:, pr, :], AR[:, 128:128 + C], start=True, stop=False)
                    B.mm(ps[2][:, h * 128:h * 128 + C], UP[0:C, :], MA[h][0:C, 128:128 + C], start=False, stop=False)
                    B.mm(ps[2][:, h * 128:h * 128 + C], Vp, MB[h][0:C, 128:128 + C], start=False, stop=True)
                    B.copy("act", YT[hs, pr, 0:C], ps[2][hs, h * 128:h * 128 + C])
                B.mm(ps[3][:, 384:512], TMX[0:C, 0, :], UP[0:C, :], start=True, stop=False)
                B.mm(ps[3][:, 384:512], TMX[0:C, 1, :], Vp, start=False, stop=True)
                B.tt("dve", STMP[:], ps[3][:, 384:512], bones[:], ALU.mult)
                B.stt(H[:, pr, :], H[:, pr, :], gC, STMP[:], ALU.mult, ALU.add)

        def phase1_tile(sq, ti):
            TT = sq["C"]; t0 = ti * TT; nm = sq["name"]
            B.dma(xt[0:TT, :], sq["x"][t0:t0 + TT, :])
            rms_to_hT(xt, TT, nw0, hT, xn)
            cnt = [0]

            def cons_p(f, pap):
                rw = raw[cnt[0] % 2]; df = dif[cnt[0] % 2]; cnt[0] += 1
                B.copy("pool", rw[:, 0:1], plast[:, f:f + 1])
                B.copy("act", rw[:, 1:TT + 1], pap)
                B.copy("pool", plast[:, f:f + 1], rw[:, TT:TT + 1])
                B.tt("pool", df[:, 0:TT], rw[:, 0:TT], rw[:, 1:TT + 1], ALU.subtract)
                B.stt(PS[:, f, 0:TT], df[:, 0:TT], mu[:, f:f + 1], rw[:, 1:TT + 1], ALU.mult, ALU.add)
            proj_chunks(WB["rw_in"], 0, 37, hT, TT, cons_p)
            B.act(PS[0:64, 36, 0:TT], PS[0:64, 36, 0:TT], AF.Tanh)
            for f in range(12):
                B.mm(ps[2][:, (f % 4) * 128:(f % 4) * 128 + TT], WUP[:, f * 128:(f + 1) * 128], PS[:, 36, 0:TT])
                B.act(SG[:, f, 0:TT], ps[2][:, (f % 4) * 128:(f % 4) * 128 + TT], AF.Sigmoid, bias=w0[:, f:f + 1])
                B.mm(ps[3][:, (f % 4) * 128:(f % 4) * 128 + TT], AUP[:, f * 128:(f + 1) * 128], PS[:, 36, 0:TT])
                B.act(AA[:, f, 0:TT], ps[3][:, (f % 4) * 128:(f % 4) * 128 + TT], AF.Sigmoid, bias=a0[:, f:f + 1])
            for g in range(3):
                sl = slice(4 * g, 4 * g + 4)
                bc = lambda col: col[:, sl].unsqueeze(2).broadcast_to([128, 4, TT])
                kv = PS[:, 12 + 4 * g:16 + 4 * g, 0:TT]; rv = PS[:, 4 * g:4 * g + 4, 0:TT]; vv = PS[:, 24 + 4 * g:28 + 4 * g, 0:TT]
                kkv, tv, av, bv = KK[:, sl, 0:TT], TMP[:, sl, 0:TT], AA[:, sl, 0:TT], BON[:, sl, 0:TT]
                p2 = ps[2][:].rearrange("p (a c) -> p a c", a=4)[:, :, 0:TT]
                p3 = ps[3][:].rearrange("p (a c) -> p a c", a=4)[:, :, 0:TT]
                B.tt("pool", kkv, kv, bc(k_k), ALU.mult)
                B.tt("pool", tv, kkv, kkv, ALU.mult)
                for j in range(4):
                    B.mm(ps[2][:, j * 128:j * 128 + TT], bones[:], TMP[:, 4 * g + j, 0:TT])
                B.act(tv, p2, AF.Sqrt)
                B.ts("dve", tv, tv, 1e-12, ALU.max)
                B.recip(tv, tv)
                B.tt("dve", kkv, kkv, tv, ALU.mult)
                B.stt(tv, av, -1.0, bc(k_a), ALU.add, ALU.mult)
                B.stt(kv, tv, 1.0, kv, ALU.add, ALU.mult)
                B.tt("pool", av, kkv, av, ALU.mult)
                B.tt("pool", tv, rv, bc(r_k), ALU.mult)
                B.tt("dve", tv, tv, kv, ALU.mult)
                for j in range(4):
                    B.mm(ps[3][:, j * 128:j * 128 + TT], bones[:], TMP[:, 4 * g + j, 0:TT])
                B.tt("dve", bv, p3, vv, ALU.mult)
            for f in range(12):
                B.scan_add(CS[:, f, 0:TT], onesrow[:, 0:TT], SG[:, f, 0:TT])
            B.tt("pool", TMP[:, :, 0:TT], CS[:, :, 0:TT], SG[:, :, 0:TT], ALU.subtract)
            B.act(EP[:, :, 0:TT], CS[:, :, 0:TT], AF.Exp, scale=-DECAY_C)
            B.act(EM[:, :, 0:TT], CS[:, :, 0:TT], AF.Exp, scale=DECAY_C)
            B.act(EQ[:, :, 0:TT], TMP[:, :, 0:TT], AF.Exp, scale=-DECAY_C)
            rwkv_scan(sq, TT)
            for g in range(3):
                sl = slice(4 * g, 4 * g + 4)
                yv, tv, bv = YT[:, sl, 0:TT], TMP[:, sl, 0:TT], BON[:, sl, 0:TT]
                p2 = ps[2][:].rearrange("p (a c) -> p a c", a=4)[:, :, 0:TT]
                p3 = ps[3][:].rearrange("p (a c) -> p a c", a=4)[:, :, 0:TT]
                bc = lambda col: col[:, sl].unsqueeze(2).broadcast_to([128, 4, TT])
                for j in range(4):
                    B.mm(ps[2][:, j * 128:j * 128 + TT], bones[:], YT[:, 4 * g + j, 0:TT])
                B.stt(yv, p2, -1.0 / 64, yv, ALU.mult, ALU.add)
                B.tt("pool", tv, yv, yv, ALU.mult)
                for j in range(4):
                    B.mm(ps[3][:, j * 128:j * 128 + TT], bones[:], TMP[:, 4 * g + j, 0:TT])
                B.ts("dve", tv, p3, 1.0 / 64, ALU.mult, GN_EPS, ALU.add)
                B.act(tv, tv, AF.Sqrt)
                B.recip(tv, tv)
                B.tt("dve", yv, yv, tv, ALU.mult)
                B.tt("pool", yv, yv, bc(ln_w), ALU.mult)
                B.tt("dve", yv, yv, bc(ln_b), ALU.add)
                B.tt("pool", yv, yv, bv, ALU.add)
            def cons_qg(f, pap):
                if f < 41:
                    B.evac(QM[:, f - 37, 0:TT], pap)
                else:
                    B.act(GT[:, f - 41, 0:TT], pap, AF.Silu)
            proj_chunks(WB["rw_in"], 37, 20, hT, TT, cons_qg)
            mem_attn(QM, TT, 0, YM)
            B.tt("pool", GTb[:, 0:12, 0:TT], YT[:, :, 0:TT], GT[:, 0:12, 0:TT], ALU.mult)
            B.tt("pool", GTb[:, 12:16, 0:TT], YM[:, :, 0:TT], GT[:, 12:16, 0:TT], ALU.mult)
            out_proj(0, 0, 16, TT, xt, x1t)
            rms_to_hT(x1t, TT, nw1, hT, xn)

            def cons_l1(f, pap):
                if f < 12:
                    B.copy("act", YTb[:, f, 0:TT], pap)
                    B.dma(sq["qT"][f, :, t0:t0 + TT], YTb[:, f, 0:TT], eng="act")
                elif f < 36:
                    g = (f - 12) % 4
                    B.evac(TMP[:, g, 0:TT], pap)
                    B.tr(ps[3][0:TT, g * 128:(g + 1) * 128], TMP[:, g, 0:TT])
                    if g == 3:
                        B.copy("act", KVT[0:TT, :], ps[3][0:TT, :])
                        dst = sq["ok"] if f < 24 else sq["ov"]
                        c0 = ((f - 12) % 12 - 3) * 128
                        B.dma(dst[t0:t0 + TT, c0:c0 + 512], KVT[0:TT, :], eng="act")
                elif f < 40:
                    B.evac(QM[:, f - 36, 0:TT], pap)
                else:
                    B.act(GT[:, f - 40, 0:TT], pap, AF.Silu)
            proj_chunks(WB["df_in"], 0, 56, hT, TT, cons_l1)
            B.dma(sq["sgT"][:, :, t0:t0 + TT].rearrange("c p t -> p c t"), GT[:, 0:12, 0:TT], eng="act")
            mem_attn(QM, TT, 1, YM)
            B.tt("pool", GTb[:, 12:16, 0:TT], YM[:, :, 0:TT], GT[:, 12:16, 0:TT], ALU.mult)
            out_proj(1, 12, 4, TT, x1t, xt)
            B.dma(sq["x1"][t0:t0 + TT, :], xt[0:TT, :], eng="act")

        def phase1_seq(sq):
            nm = sq["name"]; b = sq["b"]; H = Hbd[nm]
            phase_mem(sq)
            if sq["kind"] == "p":
                B.memset("pool", plast[:], 0.0)
                B.memset("pool", H[:], 0.0)
            else:
                B.memset("pool", plstg[:], 0.0)
                B.dma(plstg[0:37, :], I["state_shift"][b].rearrange("(c p) -> c p", p=128))
                B.tr(ps[0][:, 0:128], plstg[:])
                B.copy("dve", plast[:], ps[0][:, 0:37])
                B.memset("pool", TMP[:], 0.0)
                for pr in range(12):
                    for h in range(2):
                        B.dma(TMP[h * 64:h * 64 + 64, pr, h * 64:h * 64 + 64], I["state_rwkv"][b, 2 * pr + h])
                for pr in range(12):
                    B.tr(ps[pr % 2][:, 0:128], TMP[:, pr, :])
                    B.evac(H[:, pr, :], ps[pr % 2][:, 0:128])
            for ti in range(sq["T"] // sq["C"]):
                phase1_tile(sq, ti)
            B.tr(ps[0][0:37, 0:128], plast[:])
            B.copy("dve", plstg[0:37, :], ps[0][0:37, 0:128])
            B.dma(sq["oshift"].rearrange("(c p) -> c p", p=128), plstg[0:37, :])
            for pr in range(12):
                B.tr(ps[pr % 2][:, 0:128], H[:, pr, :])
                B.evac(TMP[:, pr, :], ps[pr % 2][:, 0:128])
            for pr in range(12):
                for h in range(2):
                    B.dma(sq["oS"][2 * pr + h], TMP[h * 64:h * 64 + 64, pr, h * 64:h * 64 + 64])

        def phase2(sq):
            T = sq["T"]; past = sq["past"]; b = sq["b"]
            L = past + T
            nkt = (L + 127) // 128
            Lp = nkt * 128
            with contextlib.ExitStack() as st2:
                sb2 = lambda n, s, dt=F32: st2.enter_context(nc.sbuf_tensor(n + sq["name"], list(s), dt))
                Ktm_ = [sb2("Ktm%d" % i, [128, nkt, 128]) for i in range(2)]; Vtm_ = [sb2("Vtm%d" % i, [128, nkt, 128]) for i in range(2)]
                Vb_ = [sb2("Vb%d" % i, [128, nkt, 128], BF16) for i in range(2)]
                KZ1_ = [sb2("KZ1%d" % i, [128, Lp], BF16) for i in range(2)]; KZ2_ = [sb2("KZ2%d" % i, [128, Lp], BF16) for i in range(2)]
                QT_ = [sb2("QT%d" % i, [128, T], BF16) for i in range(2)]
                NE = 3
                E1 = [sb2("E1%d" % i, [128, 512], BF16) for i in range(NE)]
                E2 = [sb2("E2%d" % i, [128, 512], BF16) for i in range(NE)]
                AC1 = sb2("AC1", [128, 512]); AC2 = sb2("AC2", [128, 512]); R1 = sb2("R1", [128, 512]); R2 = sb2("R2", [128, 512])
                OT = sb2("OT", [128, 512]); SQ2 = sb2("SQ2", [128, 512]); kval = sb2("kval", [128, 1])
                for i in range(2):
                    B.memset("pool", KZ1_[i][64:128, :], 0.0); B.memset("pool", KZ2_[i][0:64, :], 0.0)
                lastn = L - (nkt - 1) * 128
                if lastn < 128:
                    B.memset("pool", kval[:], 0.0); B.memset("pool", kval[0:lastn, :], 1.0)
                for h in range(12):
                    Ktm, Vtm, Vb, KZ1, KZ2, QT = Ktm_[h % 2], Vtm_[h % 2], Vb_[h % 2], KZ1_[h % 2], KZ2_[h % 2], QT_[h % 2]
                    if lastn < 128:
                        B.memset("pool", Ktm[:, nkt - 1, :], 0.0); B.memset("pool", Vtm[:, nkt - 1, :], 0.0)
                    def ld(dst, src, n0, ntl):
                        for g0 in range(0, ntl, 8):
                            g = min(8, ntl - g0)
                            B.dma(dst[:, n0 + g0:n0 + g0 + g, :],
                                  src[g0 * 128:(g0 + g) * 128, h * 128:(h + 1) * 128].rearrange("(n p) f -> p n f", p=128))
                    if past:
                        ld(Ktm, I["cache_k"][b], 0, past // 128)
                        ld(Vtm, I["cache_v"][b], 0, past // 128)
                    nk0 = past // 128
                    if T >= 128:
                        ld(Ktm, sq["ok"], nk0, T // 128)
                        ld(Vtm, sq["ov"], nk0, T // 128)
                    else:
                        B.dma(Ktm[0:T, nk0, :], sq["ok"][:, h * 128:(h + 1) * 128], rk=[sq["ok"]])
                        B.dma(Vtm[0:T, nk0, :], sq["ov"][:, h * 128:(h + 1) * 128], rk=[sq["ov"]])
                    B.dma(QT[:], sq["qT"][h], rk=[sq["qT"]])
                    hk = nkt // 2
                    if hk:
                        B.copy("act", Vb[:, 0:hk, :], Vtm[:, 0:hk, :])
                    B.copy("dve", Vb[:, hk:nkt, :], Vtm[:, hk:nkt, :])
                    for kt in range(nkt):
                        B.tr(ps[kt % 2][:, 0:128], Ktm[:, kt, :])
                        B.copy("act", KZ1[0:64, kt * 128:(kt + 1) * 128], ps[kt % 2][0:64, 0:128])
                        B.copy("dve", KZ2[64:128, kt * 128:(kt + 1) * 128], ps[kt % 2][64:128, 0:128])
                    QB = min(512, T)
                    for q0 in range(0, T, QB):
                        kend = (past + q0 + QB + 127) // 128
                        pvA, pvB = ps[4], ps[5]

                        def geom(kt):
                            if sq["kind"] == "p":
                                r = kt - q0 // 128
                                return r, max(0, r) * 128
                            return -1, 0

                        def qk(kt):
                            r, c0 = geom(kt)
                            sA, sB = ps[2 * (kt % 2)], ps[2 * (kt % 2) + 1]
                            B.mm(sA[:, c0:QB], KZ1[:, kt * 128:(kt + 1) * 128], QT[:, q0 + c0:q0 + QB])
                            B.mm(sB[:, c0:QB], KZ2[:, kt * 128:(kt + 1) * 128], QT[:, q0 + c0:q0 + QB])

                        qk(0)
                        for kt in range(kend):
                            r, c0 = geom(kt)
                            sA, sB = ps[2 * (kt % 2)], ps[2 * (kt % 2) + 1]
                            e1, e2 = E1[kt % NE], E2[kt % NE]
                            B.act(e1[:, c0:QB], sA[:, c0:QB], AF.Exp, scale=0.125)
                            B.act(e2[:, c0:QB], sB[:, c0:QB], AF.Exp, scale=0.125)
                            if kt + 1 < kend:
                                qk(kt + 1)
                            if r >= 0:
                                B.tt("dve", e1[:, c0:c0 + 128], e1[:, c0:c0 + 128], cmask[:], ALU.mult)
                                B.tt("dve", e2[:, c0:c0 + 128], e2[:, c0:c0 + 128], cmask[:], ALU.mult)
                            if kt == nkt - 1 and lastn < 128:
                                B.ts("pool", e1[:, c0:QB], e1[:, c0:QB], kval[:, 0:1], ALU.mult)
                                B.ts("pool", e2[:, c0:QB], e2[:, c0:QB], kval[:, 0:1], ALU.mult)
                            B.mm(pvA[:, c0:QB], Vb[:, kt, :], e1[:, c0:QB], start=(kt == 0), stop=(kt == kend - 1))
                            B.mm(pvB[:, c0:QB], Vb[:, kt, :], e2[:, c0:QB], start=(kt == 0), stop=(kt == kend - 1))
                            if kt == 0:
                                B.copy("dve", AC1[:, 0:QB], e1[:, 0:QB]); B.copy("dve", AC2[:, 0:QB], e2[:, 0:QB])
                            else:
                                B.tt("dve", AC1[:, c0:QB], AC1[:, c0:QB], e1[:, c0:QB], ALU.add)
                                B.tt("dve", AC2[:, c0:QB], AC2[:, c0:QB], e2[:, c0:QB], ALU.add)
                        B.mm(ps[6][:, 0:QB], ones[:], AC1[:, 0:QB]); B.mm(ps[7][:, 0:QB], ones[:], AC2[:, 0:QB])
                        B.recip(R1[:, 0:QB], ps[6][:, 0:QB]); B.recip(R2[:, 0:QB], ps[7][:, 0:QB])
                        B.tt("dve", R1[:, 0:QB], pvA[:, 0:QB], R1[:, 0:QB], ALU.mult)
                        B.tt("dve", R2[:, 0:QB], pvB[:, 0:QB], R2[:, 0:QB], ALU.mult)
                        B.stt(OT[:, 0:QB], R2[:, 0:QB], nlam[:, 0:1], R1[:, 0:QB], ALU.mult, ALU.add)
                        B.tt("pool", SQ2[:, 0:QB], OT[:, 0:QB], OT[:, 0:QB], ALU.mult)
                        B.mm(ps[6][:, 0:QB], ones[:], SQ2[:, 0:QB])
                        B.ts("dve", SQ2[:, 0:QB], ps[6][:, 0:QB], 1.0 / 128, ALU.mult, EPS, ALU.add)
                        B.act(SQ2[:, 0:QB], SQ2[:, 0:QB], AF.Sqrt)
                        B.recip(SQ2[:, 0:QB], SQ2[:, 0:QB])
                        B.tt("dve", OT[:, 0:QB], OT[:, 0:QB], SQ2[:, 0:QB], ALU.mult)
                        B.ts("dve", OT[:, 0:QB], OT[:, 0:QB], subln[:, 0:1], ALU.mult, 1.0 - lam_init, ALU.mult)
                        B.dma(sq["oT"][h, :, q0:q0 + QB], OT[:, 0:QB], eng="act")

        def phase3(sq):
            T = sq["T"]; TT = sq["C"]
            with contextlib.ExitStack() as st3:
                sb3 = lambda n, s, dt=F32: st3.enter_context(nc.sbuf_tensor(n + sq["name"], list(s), dt))
                WO = sb3("WO", [128, 12, D], BF16); GTb = sb3("GTb3", [128, 12, 128], BF16)
                fnw = sb3("fnw", [128, D])
                B.dma(fnw[:], I["final_norm_w"].partition_broadcast(128))
                YT = sb3("YT3", [128, 12, 128]); GT = sb3("GT3", [128, 12, 128]); x1t = sb3("x1t3", [128, D])
                xt = sb3("xt3", [128, D]); xn = sb3("xn3", [128, D]); rstat = sb3("rstat3", [128, 4])
                for c4 in range(0, 12, 4):
                    B.dma(WO[:, c4:c4 + 4, :], WB["w_out"][BR + c4 * 128:BR + (c4 + 4) * 128, :].rearrange("(c p) f -> p c f", p=128))
                for ti in range(T // TT):
                    t0 = ti * TT
                    B.dma(YT[:, :, 0:TT], sq["oT"][:, :, t0:t0 + TT].rearrange("c p t -> p c t"), rk=[sq["oT"]])
                    B.dma(GT[:, 0:12, 0:TT], sq["sgT"][:, :, t0:t0 + TT].rearrange("c p t -> p c t"), rk=[sq["sgT"]])
                    B.dma(x1t[0:TT, :], sq["x1"][t0:t0 + TT, :], rk=[sq["x1"]])
                    B.tt("pool", GTb[:, 0:12, 0:TT], YT[:, :, 0:TT], GT[:, 0:12, 0:TT], ALU.mult)
                    for c in range(12):
                        for hf in range(2):
                            B.mm(ps[4 + hf][0:TT, :], GTb[:, c, 0:TT], WO[:, c, hf * 512:(hf + 1) * 512], start=(c == 0), stop=(c == 11))
                    for hf in range(2):
                        B.tt("dve", xt[0:TT, hf * 512:(hf + 1) * 512], ps[4 + hf][0:TT, :], x1t[0:TT, hf * 512:(hf + 1) * 512], ALU.add)
                    B.act(xn[0:TT, :], xt[0:TT, :], AF.Square, accum_out=rstat[0:TT, 0:1])
                    rstd_from_ss(rstat[0:TT, 0:1], D, TT, EPS)
                    B.stt(xn[0:TT, :], xt[0:TT, :], rstat[0:TT, 0:1], fnw[0:TT, :], ALU.mult, ALU.mult)
                    B.dma(sq["y"][t0:t0 + TT, :], xn[0:TT, :], eng="act")

        for sq in seqs:
            if "1" in PHASES:
                phase1_seq(sq)
        P.barrier()
        st1.close()
        for sq in seqs:
            if "2" in PHASES:
                phase2(sq)
                P.barrier()
        for sq in seqs:
            if "3" in PHASES:
                phase3(sq)
                P.barrier()
        P.emit()
    return nc, P


_IN_ORDER = ["x_prompt", "mem_prompt", "x_sample", "state_rwkv", "state_shift", "cache_k", "cache_v", "cache_mem_k",
             "cache_mem_v"]


def shard_inputs(inputs, n_cores, NB):
    c = host_consts()
    maps = []
    f = lambda a: np.ascontiguousarray(np.asarray(a, dtype=np.float32))
    for ci in range(n_cores):
        sl = slice(ci * NB, (ci + 1) * NB)
        m = {}
        m["x_prompt"] = f(inputs["x_prompt"][sl]); m["mem_prompt"] = f(inputs["mem_prompt"][sl])
        m["x_sample"] = f(inputs["x_sample"][sl])
        m["state_rwkv"] = f(inputs["state_rwkv"][0, sl]); m["state_shift"] = f(inputs["state_shift"][0, sl])
        ck = np.asarray(inputs["cache_k"]); cv = np.asarray(inputs["cache_v"])
        m["cache_k"] = f(ck[0, sl].reshape(NB, ck.shape[2], MIXW)); m["cache_v"] = f(cv[0, sl].reshape(NB, cv.shape[2], MIXW))
        m["cache_mem_k"] = f(np.asarray(inputs["cache_mem_k"])[:, sl].reshape(2, NB, NMEM, MEMW))
        m["cache_mem_v"] = f(np.asarray(inputs["cache_mem_v"])[:, sl].reshape(2, NB, NMEM, MEMW))
        for n in ["norm_w", "mem_norm_w", "w_mem_k", "w_mem_v", "w_out", "final_norm_w"]:
            m[n] = f(inputs[n])
        for n in ["rw_in", "rw_mu", "rw_w0", "rw_w_up", "rw_a0", "rw_a_up", "rw_k_k", "rw_k_a", "rw_ln_w", "rw_ln_b",
                  "df_in", "df_lq1", "df_lk1", "df_lq2", "df_lk2", "df_subln"]:
            m[n] = f(np.asarray(inputs[n])[0])
        m["rw_r_k"] = f(np.asarray(inputs["rw_r_k"])[0].reshape(MIXW))
        m.update({"c_" + k: v for k, v in c.items()})
        maps.append(m)
    return maps


def gather_outputs(results, NB, TP, TS):
    cat = lambda n, ax: np.concatenate([np.asarray(r[n]) for r in results], axis=ax)
    nb = NB * len(results)
    y_p = cat("y_prompt", 0); y_s = cat("y_sample", 0)
    p_S = cat("p_S", 0)[None]; p_sh = cat("p_shift", 0)[None]
    p_k = cat("p_k", 0).reshape(1, nb, TP, 12, 128); p_v = cat("p_v", 0).reshape(1, nb, TP, 12, 128)
    p_mk = cat("p_mk", 1).reshape(2, nb, NMEM, 4, 128); p_mv = cat("p_mv", 1).reshape(2, nb, NMEM, 4, 128)
    s_S = cat("s_S", 0)[None]; s_sh = cat("s_shift", 0)[None]
    s_k = cat("s_k", 0).reshape(1, nb, TS, 12, 128); s_v = cat("s_v", 0).reshape(1, nb, TS, 12, 128)
    return (y_p, y_s, p_S, p_sh, p_k, p_v, p_mk, p_mv, s_S, s_sh, s_k, s_v)


def kernel(**inputs):
    n_cores = 8
    xp = np.asarray(inputs["x_prompt"]); xs = np.asarray(inputs["x_sample"])
    NB = xp.shape[0] // n_cores
    TP, TS = xp.shape[1], xs.shape[1]
    PAST = np.asarray(inputs["cache_k"]).shape[2]
    nc, _ = build(TP, TS, PAST, NB)
    maps = shard_inputs(inputs, n_cores, NB)
    res = run_bass_kernel_spmd(nc, maps, core_ids=list(range(n_cores)))
    return gather_outputs(res.results, NB, TP, TS)
```

```python
import contextlib
import math
import numpy as np
import concourse.bass as bass
import concourse.mybir as mybir
from concourse.bass_utils import run_bass_kernel_spmd

F32 = mybir.dt.float32
BF16 = mybir.dt.bfloat16
AF = mybir.ActivationFunctionType
ALU = mybir.AluOpType

ENGS = ["pe", "dve", "act", "pool", "sp"]
NDMA = 48
import os
POOL_ENG = os.environ.get("POOL_ENG", "pool")

D = 1024
MIXW = 1536
MEMW = 512
BR = 2048
RWP = 4736
RWIN = RWP + MEMW + BR
DFIN = 3 * MIXW + MEMW + BR
NMEM = 256
EPS = 1e-6
GN_EPS = 64e-5
DECAY_C = math.exp(-0.5)


def _key(x):
    if isinstance(x, (str, tuple)):
        return x
    if hasattr(x, "tensor"):
        return x.tensor.name
    return x.name


class Prog:
    def __init__(self, nc, same_engine_sync=True):
        self.nc = nc
        self.ops = {e: [] for e in ENGS}
        self.n = {e: 0 for e in ENGS}
        self.seen = {e: {} for e in ENGS}
        self.lastw = {}
        self.readers = {}
        self.dma_val = [0] * NDMA
        self.dma_next = 0
        self.same = same_engine_sync
        self.nwaits = 0

    def _add(self, eng, waits, tok, is_war=False):
        if tok is None:
            return
        sk, val = tok
        if sk == ("e", eng):
            if eng == "pe" or eng == "sp":
                return
            if not self.same:
                return
        if self.seen[eng].get(sk, 0) >= val:
            return
        if waits.get(sk, 0) < val:
            waits[sk] = val

    def op(self, eng, fn, reads=(), writes=(), dma=False, inc=True, attach=None):
        if eng == "pool" and not dma:
            eng = POOL_ENG
        rk = [_key(r) for r in reads if r is not None]
        wk = [_key(w) for w in writes if w is not None]
        wk += [k for k in rk if isinstance(k, str) and k[:2] == "ps" and k[2:].isdigit() and k not in wk]
        waits = {}
        for k in rk:
            self._add(eng, waits, self.lastw.get(k))
        for k in wk:
            self._add(eng, waits, self.lastw.get(k))
            for sk, val in self.readers.get(k, {}).items():
                self._add(eng, waits, (sk, val), is_war=True)
        if dma:
            slot = self.dma_next
            self.dma_next = (self.dma_next + 1) % NDMA
            prev = self.dma_val[slot]
            if prev > 0:
                self._add(eng, waits, (("d", slot), prev))
            self.dma_val[slot] = prev + 16
            tok = (("d", slot), prev + 16)
            incinfo = ("d", slot)
        else:
            if inc:
                self.n[eng] += 1
                tok = (("e", eng), self.n[eng])
                incinfo = ("e", eng)
            else:
                tok = (("e", eng), self.n[eng] + 1)
                incinfo = None
        for sk, val in waits.items():
            self.seen[eng][sk] = val
        self.nwaits += len(waits)
        self.lastw_prev = {_key(attach): self.lastw.get(_key(attach))} if attach is not None else {}
        for k in wk:
            self.lastw[k] = tok
            self.readers[k] = {}
        for k in rk:
            d = self.readers.setdefault(k, {})
            if d.get(tok[0], 0) < tok[1]:
                d[tok[0]] = tok[1]
        wl = list(waits.items())
        if attach is not None and len(wl) > 1:
            t = self.lastw_prev.get(_key(attach))
            if t is not None:
                wl.sort(key=lambda w: w[0] == t[0])
        self.ops[eng].append((fn, wl, incinfo))
        return tok

    def barrier(self):
        for eng in ENGS:
            waits = {}
            for o in ENGS:
                if o != "sp" and self.n[o] > self.seen[eng].get(("e", o), 0):
                    if not (o == eng and eng == "pe"):
                        waits[("e", o)] = self.n[o]
            for i, v in enumerate(self.dma_val):
                if v > self.seen[eng].get(("d", i), 0):
                    waits[("d", i)] = v
            for sk, val in waits.items():
                self.seen[eng][sk] = val
            self.nwaits += len(waits)
            self.ops[eng].append((None, list(waits.items()), None))

    def emit(self):
        nc = self.nc
        with contextlib.ExitStack() as st:
            esem = {e: st.enter_context(nc.semaphore("s_" + e)) for e in ENGS}
            dsem = [st.enter_context(nc.semaphore("d_%d" % i)) for i in range(NDMA)]
            block = st.enter_context(nc.Block())

            def sem_of(sk):
                return esem[sk[1]] if sk[0] == "e" else dsem[sk[1]]

            fin = [(("e", e), self.n[e]) for e in ENGS if e != "sp" and self.n[e] > 0]
            fin += [(("d", i), v) for i, v in enumerate(self.dma_val) if v > 0]

            def run(e, name):
                for fn, waits, incinfo in self.ops[name]:
                    if fn is None:
                        for sk, val in waits:
                            e.wait_ge(sem_of(sk), val)
                        continue
                    for sk, val in waits:
                        e.wait_ge(sem_of(sk), val)
                    ins = fn(e)
                    if incinfo is not None:
                        ins.then_inc(sem_of(incinfo), 16 if incinfo[0] == "d" else 1)
                if name == "sp":
                    for sk, val in fin:
                        e.wait_ge(sem_of(sk), val)

            @block.tensor
            def _(e):
                run(e, "pe")

            @block.vector
            def _(e):
                run(e, "dve")

            @block.scalar
            def _(e):
                run(e, "act")

            @block.gpsimd
            def _(e):
                run(e, "pool")

            @block.sync
            def _(e):
                run(e, "sp")


class Bld:
    def __init__(self, nc, P, ident):
        self.nc, self.P, self.ident = nc, P, ident
        self.rr = 0

    def mm(self, out, lhsT, rhs, start=True, stop=True, rk=(), wk=None):
        self.P.op("pe", lambda e: e.matmul(out, lhsT, rhs, start=start, stop=stop),
                  [lhsT, rhs] + list(rk), [out] if wk is None else wk, inc=True, attach=lhsT)

    def tr(self, out, in_):
        k = in_.shape[0]
        idn = self.ident[0:k, 0:k]
        self.P.op("pe", lambda e: e.transpose(out, in_, idn), [in_, idn], [out], attach=in_)

    def act(self, out, in_, func, scale=1.0, bias=0.0, accum_out=None, eng="act"):
        r = [in_] + [x for x in (scale, bias) if not isinstance(x, (int, float))]
        w = [out] + ([accum_out] if accum_out is not None else [])
        if accum_out is None:
            fn = lambda e: e.activation(out=out, in_=in_, func=func, bias=bias, scale=scale)
        else:
            fn = lambda e: e.activation(out=out, in_=in_, func=func, bias=bias, scale=scale, accum_out=accum_out)
        self.P.op("act", fn, r, w)

    def tt(self, eng, out, in0, in1, op):
        self.P.op(eng, lambda e: e.tensor_tensor(out=out, in0=in0, in1=in1, op=op), [in0, in1], [out])

    def ts(self, eng, out, in0, s1, op0, s2=None, op1=None):
        r = [in0] + [x for x in (s1, s2) if x is not None and not isinstance(x, (int, float))]
        if op1 is None:
            fn = lambda e: e.tensor_scalar(out=out, in0=in0, scalar1=s1, scalar2=None, op0=op0)
        else:
            fn = lambda e: e.tensor_scalar(out=out, in0=in0, scalar1=s1, scalar2=s2, op0=op0, op1=op1)
        self.P.op(eng, fn, r, [out])

    def stt(self, out, in0, scalar, in1, op0, op1):
        r = [in0, in1] + ([] if isinstance(scalar, (int, float)) else [scalar])
        self.P.op("dve", lambda e: e.scalar_tensor_tensor(out=out, in0=in0, scalar=scalar, in1=in1, op0=op0, op1=op1), r, [out])

    def copy(self, eng, out, in_):
        if eng == "act":
            self.P.op("act", lambda e: e.copy(out=out, in_=in_), [in_], [out])
        else:
            self.P.op(eng, lambda e: e.tensor_copy(out=out, in_=in_), [in_], [out])

    def evac(self, out, in_):
        self.rr ^= 1
        self.copy("act" if self.rr else "dve", out, in_)

    def memset(self, eng, ap, val):
        self.P.op(eng, lambda e: e.memset(ap, val), [], [ap])

    def recip(self, out, in_):
        self.P.op("dve", lambda e: e.reciprocal(out=out, in_=in_), [in_], [out])

    def scan_add(self, out, ones, data, init=0.0):
        self.P.op("dve", lambda e: e.tensor_tensor_scan(out=out, data0=ones, data1=data, initial=init,
                                                        op0=ALU.mult, op1=ALU.add), [ones, data], [out])

    def dma(self, out, in_, eng="sp", rk=None, wk=None):
        self.P.op(eng, lambda e: e.dma_start(out=out, in_=in_), [in_] if rk is None else rk,
                  [out] if wk is None else wk, dma=True)


def host_consts():
    i = np.arange(128)
    c = {}
    c["ident"] = np.eye(128, dtype=np.float32)
    su = (i[None, :] > i[:, None]).astype(np.float32)
    iu = (i[None, :] >= i[:, None]).astype(np.float32)
    c["mask2"] = np.concatenate([su, iu], axis=1)
    c["masksl"] = (i[None, :] < i[:, None]).astype(np.float32)
    c["bones"] = (i[None, :] // 64 == i[:, None] // 64).astype(np.float32)
    c["ones"] = np.ones((128, 128), np.float32)
    c["cmask"] = (i[:, None] < (i[None, :] // 64 + 1) * 64).astype(np.float32)
    return c


def build(TP, TS, PAST, NB=2, PHASES="123"):
    nc = bass.Bass("TRN2", target_bir_lowering=False)
    P = Prog(nc)
    din = lambda n, s: nc.dram_tensor(n, list(s), F32, kind="ExternalInput").ap()
    dout = lambda n, s: nc.dram_tensor(n, list(s), F32, kind="ExternalOutput").ap()
    dscr = lambda n, s, dt=F32: nc.dram_tensor(n, list(s), dt, kind="Internal").ap()
    WB = {"rw_in": dscr("rw_in_b", [RWIN // 128, 128, D], BF16), "df_in": dscr("df_in_b", [DFIN // 128, 128, D], BF16),
          "w_out": dscr("w_out_b", [2 * BR, D], BF16)}

    I = {}
    I["x_prompt"] = din("x_prompt", [NB, TP, D])
    I["mem_prompt"] = din("mem_prompt", [NB, NMEM, D])
    I["x_sample"] = din("x_sample", [NB, TS, D])
    I["state_rwkv"] = din("state_rwkv", [NB, 24, 64, 64])
    I["state_shift"] = din("state_shift", [NB, RWP])
    I["cache_k"] = din("cache_k", [NB, PAST, MIXW])
    I["cache_v"] = din("cache_v", [NB, PAST, MIXW])
    I["cache_mem_k"] = din("cache_mem_k", [2, NB, NMEM, MEMW])
    I["cache_mem_v"] = din("cache_mem_v", [2, NB, NMEM, MEMW])
    for n, s in [("norm_w", [2, D]), ("mem_norm_w", [2, D]), ("w_mem_k", [2, D, MEMW]), ("w_mem_v", [2, D, MEMW]),
                 ("w_out", [2, BR, D]), ("final_norm_w", [D]), ("rw_in", [D, RWIN]), ("rw_mu", [RWP]),
                 ("rw_w0", [MIXW]), ("rw_w_up", [64, MIXW]), ("rw_a0", [MIXW]), ("rw_a_up", [64, MIXW]),
                 ("rw_k_k", [MIXW]), ("rw_k_a", [MIXW]), ("rw_r_k", [MIXW]), ("rw_ln_w", [MIXW]),
                 ("rw_ln_b", [MIXW]), ("df_in", [D, DFIN]), ("df_lq1", [64]), ("df_lk1", [64]),
                 ("df_lq2", [64]), ("df_lk2", [64]), ("df_subln", [128])]:
        I[n] = din(n, s)
    for n, s in [("ident", [128, 128]), ("mask2", [128, 256]), ("masksl", [128, 128]), ("bones", [128, 128]),
                 ("ones", [128, 128]), ("cmask", [128, 128])]:
        I[n] = din("c_" + n, s)
    O = {}
    O["y_prompt"] = dout("y_prompt", [NB, TP, D])
    O["y_sample"] = dout("y_sample", [NB, TS, D])
    O["p_S"] = dout("p_S", [NB, 24, 64, 64])
    O["p_shift"] = dout("p_shift", [NB, RWP])
    O["p_k"] = dout("p_k", [NB, TP, MIXW])
    O["p_v"] = dout("p_v", [NB, TP, MIXW])
    O["p_mk"] = dout("p_mk", [2, NB, NMEM, MEMW])
    O["p_mv"] = dout("p_mv", [2, NB, NMEM, MEMW])
    O["s_S"] = dout("s_S", [NB, 24, 64, 64])
    O["s_shift"] = dout("s_shift", [NB, RWP])
    O["s_k"] = dout("s_k", [NB, TS, MIXW])
    O["s_v"] = dout("s_v", [NB, TS, MIXW])

    seqs = []
    for b in range(NB):
        seqs.append(dict(kind="p", b=b, T=TP, C=128, x=I["x_prompt"][b], past=0, name="p%d" % b,
                         y=O["y_prompt"][b], oS=O["p_S"][b], oshift=O["p_shift"][b], ok=O["p_k"][b], ov=O["p_v"][b]))
    for b in range(NB):
        seqs.append(dict(kind="s", b=b, T=TS, C=TS, x=I["x_sample"][b], past=PAST, name="s%d" % b,
                         y=O["y_sample"][b], oS=O["s_S"][b], oshift=O["s_shift"][b], ok=O["s_k"][b], ov=O["s_v"][b]))
    for sq in seqs:
        T = sq["T"]
        sq["qT"] = dscr("qT_" + sq["name"], [12, 128, T], BF16)
        sq["sgT"] = dscr("sgT_" + sq["name"], [12, 128, T])
        sq["oT"] = dscr("oT_" + sq["name"], [12, 128, T])
        sq["x1"] = dscr("x1_" + sq["name"], [T, D])

    with contextlib.ExitStack() as st:
        sb = lambda n, s: st.enter_context(nc.sbuf_tensor(n, list(s), F32))
        ps = [st.enter_context(nc.psum_tensor("ps%d" % i, [128, 512], F32)) for i in range(8)]
        ident = sb("ident", [128, 128]); mask2 = sb("mask2", [128, 256]); masksl = sb("masksl", [128, 128])
        bones = sb("bones", [128, 128]); ones = sb("ones", [128, 128]); cmask = sb("cmask", [128, 128])
        B = Bld(nc, P, ident)
        for t, n in [(ident, "ident"), (mask2, "mask2"), (masksl, "masksl"), (bones, "bones"), (ones, "ones"), (cmask, "cmask")]:
            B.dma(t[:], I[n])
        stgA = sb("stgA", [128, 128]); stgB = sb("stgB", [128, 128]); colA = sb("colA", [128, 128]); colB = sb("colB", [128, 128])
        B.memset("pool", stgA[:], 0.0); B.memset("pool", stgB[:], 0.0)
        rowsA = [("mu", I["rw_mu"], 37)] + [(n, I["rw_" + n], 12) for n in ["w0", "a0", "k_k", "k_a", "r_k", "ln_w", "ln_b"]]
        rowsB = [("nw0", I["norm_w"][0], 8), ("nw1", I["norm_w"][1], 8), ("mnw0", I["mem_norm_w"][0], 8),
                 ("mnw1", I["mem_norm_w"][1], 8), ("subln", I["df_subln"], 1)]
        cols = {}
        for stg, col, rows in ((stgA, colA, rowsA), (stgB, colB, rowsB)):
            r0 = 0
            for n, src, k in rows:
                B.dma(stg[r0:r0 + k, :], src.rearrange("(c p) -> c p", p=128))
                cols[n] = col[:, r0:r0 + k]
                r0 += k
            B.tr(ps[0][:, 0:128], stg[:])
            B.copy("dve", col[:], ps[0][:, 0:128])
        mu, w0, a0, k_k, k_a, r_k, ln_w, ln_b = [cols[n] for n in ["mu", "w0", "a0", "k_k", "k_a", "r_k", "ln_w", "ln_b"]]
        nw0, nw1, mnw0, mnw1, subln = [cols[n] for n in ["nw0", "nw1", "mnw0", "mnw1", "subln"]]
        WUP = sb("WUP", [128, MIXW]); AUP = sb("AUP", [128, MIXW])
        B.memset("pool", WUP[64:128, :], 0.0); B.memset("pool", AUP[0:64, :], 0.0)
        B.dma(WUP[0:64, :], I["rw_w_up"]); B.dma(AUP[64:128, :], I["rw_a_up"])
        lam_init = 0.8 - 0.6 * math.exp(-0.3 * 1)
        lq = sb("lq", [128, 4, 64]); lsum = sb("lsum", [128, 2]); nlam = sb("nlam", [128, 1]); lprod = sb("lprod", [128, 2, 64])
        for i, n in enumerate(["df_lq1", "df_lk1", "df_lq2", "df_lk2"]):
            B.dma(lq[:, i, :], I[n].partition_broadcast(128))
        B.tt("dve", lprod[:, 0, :], lq[:, 0, :], lq[:, 1, :], ALU.mult)
        B.tt("dve", lprod[:, 1, :], lq[:, 2, :], lq[:, 3, :], ALU.mult)
        P.op("dve", lambda e: e.tensor_reduce(out=lsum[:], in_=lprod[:], axis=mybir.AxisListType.X, op=ALU.add), [lprod], [lsum])
        B.act(lsum[:], lsum[:], AF.Exp)
        B.tt("dve", nlam[:], lsum[:, 1:2], lsum[:, 0:1], ALU.subtract)
        B.ts("dve", nlam[:], nlam[:], -lam_init, ALU.add)

        with contextlib.ExitStack() as st0:
            cin = [st0.enter_context(nc.sbuf_tensor("cin%d" % i, [128, 2048], F32)) for i in range(2)]
            cout = [st0.enter_context(nc.sbuf_tensor("cout%d" % i, [128, 2048], BF16)) for i in range(2)]
            ci = 0
            for src, dst, C_ in ((I["rw_in"], WB["rw_in"], RWIN), (I["df_in"], WB["df_in"], DFIN)):
                nch = C_ // 128
                for r0 in range(0, D, 128):
                    d = r0 // 128
                    for c0 in range(0, nch, 16):
                        n = min(16, nch - c0)
                        a_, b_ = cin[ci % 2], cout[ci % 2]; ci += 1
                        B.dma(a_[:, 0:n * 128], src[r0:r0 + 128, c0 * 128:(c0 + n) * 128])
                        B.copy("act" if ci % 2 else "dve", b_[:, 0:n * 128], a_[:, 0:n * 128])
                        B.dma(dst[c0:c0 + n, :, d * 128:(d + 1) * 128].rearrange("c p x -> p c x"),
                              b_[:, 0:n * 128].rearrange("p (c x) -> p c x", x=128))
            wsrc2 = I["w_out"].rearrange("l r c -> (l r) c")
            for r0 in range(0, 2 * BR, 256):
                a_, b_ = cin[ci % 2], cout[ci % 2]; ci += 1
                B.dma(a_[:].rearrange("p (r c) -> p r c", r=2), wsrc2[r0:r0 + 256, :].rearrange("(r p) c -> p r c", p=128))
                B.copy("act" if ci % 2 else "dve", b_[:], a_[:])
                B.dma(WB["w_out"][r0:r0 + 256, :].rearrange("(r p) c -> p r c", p=128), b_[:].rearrange("p (r c) -> p r c", r=2))
        P.barrier()
        st1 = contextlib.ExitStack()
        sb = lambda n, s, dt=F32: st1.enter_context(nc.sbuf_tensor(n, list(s), dt))
        xt = sb("xt", [128, D]); xn = sb("xn", [128, D]); hT = sb("hT", [128, 8, 128], BF16)
        rstat = sb("rstat", [128, 4])
        NWB = 4
        Wb = [sb("W%d" % i, [128, 4, 1024], BF16) for i in range(NWB)]
        GTb = sb("GTb", [128, 16, 128], BF16); YTb = sb("YTb", [128, 12, 128], BF16)
        raw = [sb("raw0", [128, 130]), sb("raw1", [128, 130])]
        dif = [sb("dif0", [128, 128]), sb("dif1", [128, 128])]
        PS = sb("PS", [128, 37, 128])
        plast = sb("plast", [128, 37]); plstg = stgA
        SG = sb("SG", [128, 12, 128]); AA = sb("AA", [128, 12, 128]); KK = sb("KK", [128, 12, 128])
        CS = sb("CS", [128, 12, 128]); EP = sb("EP", [128, 12, 128]); EM = sb("EM", [128, 12, 128]); EQ = sb("EQ", [128, 12, 128])
        TMP = sb("TMP", [128, 12, 128]); BON = sb("BON", [128, 12, 128]); YT = SG
        onesrow = sb("onesrow", [128, 128])
        ARs = [sb("AR_%d" % i, [128, 256]) for i in range(2)]
        AZs = [[sb("AZ0_%d" % i, [128, 128]), sb("AZ1_%d" % i, [128, 128])] for i in range(2)]
        BZs = [[sb("BZ0_%d" % i, [128, 128]), sb("BZ1_%d" % i, [128, 128])] for i in range(2)]
        KZs = [[sb("KZ0_%d" % i, [128, 128]), sb("KZ1_%d" % i, [128, 128])] for i in range(2)]
        BGs = [sb("BG_%d" % i, [128, 256]) for i in range(2)]
        TMXs = [sb("TMX_%d" % i, [128, 3, 128]) for i in range(2)]
        AZ = AZs[0] + AZs[1]; BZ = BZs[0] + BZs[1]; KZ = KZs[0] + KZs[1]
        MA = [sb("MA0", [128, 256]), sb("MA1", [128, 256])]
        MB = [sb("MB0", [128, 256]), sb("MB1", [128, 256])]
        PXa = [[sb("PX%d%d" % (h, i), [128, 192]) for i in range(2)] for h in range(2)]
        PTa = [[sb("PT%d%d" % (h, i), [128, 128]) for i in range(2)] for h in range(2)]
        UP = sb("UP", [128, 128]); STMP = sb("STMP", [128, 128])
        Hone = sb("Hstate", [128, 12, 128])
        Hbd = {sq["name"]: Hone for sq in seqs}
        GT = sb("GT", [128, 16, 128]); QM = sb("QM", [128, 4, 128]); EE = sb("EE", [128, 2, 128])
        YM = sb("YM", [128, 4, 128]); RD = sb("RD", [128, 128])
        x1t = sb("x1t", [128, D]); KVT = sb("KVT", [128, 512])
        mkT = [sb("mkT0", [128, 4, NMEM]), sb("mkT1", [128, 4, NMEM])]
        mvS = [sb("mv0", [128, 2, MEMW]), sb("mv1", [128, 2, MEMW])]
        memx = GT[:].rearrange("p a b -> p (a b)").rearrange("p (m d) -> p m d", m=2)
        memh = PS[:, 0:16, :].rearrange("p a b -> p (a b)").rearrange("p (d m) -> p d m", d=8)
        memo = TMP[:, 0:8, :].rearrange("p a b -> p (a b)").rearrange("p (m f) -> p m f", m=2)
        Wf = [BON[:, 0:8, :], CS[:, 0:8, :]]
        B.memset("pool", onesrow[:], 1.0)
        for t in AZ + BZ + KZ:
            B.memset("pool", t[:], 0.0)

        def rstd_from_ss(ss, n, TT, eps):
            B.ts("dve", ss, ss, 1.0 / n, ALU.mult, eps, ALU.add)
            B.act(ss, ss, AF.Ln)
            B.act(ss, ss, AF.Exp, scale=-0.5)

        def rms_to_hT(xsrc, TT, nwc, hdst, junk):
            B.act(junk[:TT, :], xsrc[:TT, :], AF.Square, accum_out=rstat[:TT, 0:1])
            rstd_from_ss(rstat[:TT, 0:1], D, TT, EPS)
            B.ts("dve", junk[:TT, :], xsrc[:TT, :], rstat[:TT, 0:1], ALU.mult)
            for g in range(2):
                for d4 in range(4):
                    d = g * 4 + d4
                    B.tr(ps[g][:, d4 * 128:d4 * 128 + TT], junk[:TT, d * 128:(d + 1) * 128])
                for d4 in range(4):
                    d = g * 4 + d4
                    B.ts("dve", hdst[:, d, 0:TT], ps[g][:, d4 * 128:d4 * 128 + TT], nwc[:, d:d + 1], ALU.mult)

        wctr = [0]

        def proj_chunks(Wsrc, f0, nf, hsrc, TT, consume):
            f = f0
            while f < f0 + nf:
                n = min(4, f0 + nf - f)
                wb = Wb[wctr[0] % NWB]; pb = ps[wctr[0] % 2]; wctr[0] += 1
                B.dma(wb[:, 0:n, :], Wsrc[f:f + n].rearrange("c p x -> p c x"))
                for j in range(n):
                    for d in range(8):
                        B.mm(pb[:, j * 128:j * 128 + TT], wb[:, j, d * 128:(d + 1) * 128], hsrc[:, d, 0:TT],
                             start=(d == 0), stop=(d == 7))
                    consume(f + j, pb[:, j * 128:j * 128 + TT])
                f += n

        def mem_attn(qm, TT, li, ymdst):
            for h in range(4):
                for m in range(2):
                    B.mm(ps[2][:, m * 128:m * 128 + TT], mkT[li][:, h, m * 128:(m + 1) * 128], qm[:, h, 0:TT])
                    B.act(EE[:, m, 0:TT], ps[2][:, m * 128:m * 128 + TT], AF.Exp, scale=128 ** -0.5)
                for m in range(2):
                    B.mm(ps[3][:, 0:TT], mvS[li][:, m, h * 128:(h + 1) * 128], EE[:, m, 0:TT], start=(m == 0), stop=(m == 1))
                for m in range(2):
                    B.mm(ps[3][:, 128:128 + TT], ones[:], EE[:, m, 0:TT], start=(m == 0), stop=(m == 1))
                B.recip(RD[:, 0:TT], ps[3][:, 128:128 + TT])
                B.tt("dve", ymdst[:, h, 0:TT], ps[3][:, 0:TT], RD[:, 0:TT], ALU.mult)

        def out_proj(li, c0, nchunks, TT, xres, xdst):
            c = 0
            first = True
            while c < nchunks:
                n = min(4, nchunks - c)
                wb = Wb[wctr[0] % NWB]; wctr[0] += 1
                wv = wb[:].rearrange("p a b -> p (a b)")
                r0 = li * BR + (c0 + c) * 128
                B.dma(wv[:, 0:n * D].rearrange("p (c f) -> p c f", f=D),
                      WB["w_out"][r0:r0 + n * 128, :].rearrange("(c p) f -> p c f", p=128))
                for j in range(n):
                    for hf in range(2):
                        B.mm(ps[4 + hf][0:TT, :], GTb[:, c0 + c + j, 0:TT], wv[:, j * D + hf * 512:j * D + hf * 512 + 512],
                             start=first, stop=(c + j == nchunks - 1))
                    first = False
                c += n
            for hf in range(2):
                B.tt("dve", xdst[0:TT, hf * 512:(hf + 1) * 512], ps[4 + hf][0:TT, :], xres[0:TT, hf * 512:(hf + 1) * 512], ALU.add)

        def phase_mem(sq):
            b = sq["b"]
            for li in range(2):
                if sq["kind"] == "p":
                    B.dma(memx[:], I["mem_prompt"][b].rearrange("(m p) d -> p m d", p=128))
                    mnw = mnw0 if li == 0 else mnw1
                    for m in range(2):
                        rms_to_hT(memx[:, m, :], 128, mnw, memh[:, :, m * 128:(m + 1) * 128], xn)
                    for h in range(4):
                        wb = Wf[wctr[0] % 2]; wctr[0] += 1
                        B.dma(wb, I["w_mem_k"][li][:, h * 128:(h + 1) * 128].rearrange("(d p) f -> p d f", p=128))
                        for d in range(8):
                            B.mm(ps[2][:, 0:NMEM], wb[:, d, :], memh[:, d, :], start=(d == 0), stop=(d == 7))
                        B.evac(mkT[li][:, h, :], ps[2][:, 0:NMEM])
                    for which, wsrc, odst in ((0, I["w_mem_k"][li], O["p_mk"][li, b]), (1, I["w_mem_v"][li], O["p_mv"][li, b])):
                        for fh in range(4):
                            wb = Wf[wctr[0] % 2]; wctr[0] += 1
                            B.dma(wb, wsrc[:, fh * 128:(fh + 1) * 128].rearrange("(d p) f -> p d f", p=128))
                            for m in range(2):
                                for d in range(8):
                                    B.mm(ps[3][:, m * 128:(m + 1) * 128], memh[:, d, m * 128:(m + 1) * 128], wb[:, d, :],
                                         start=(d == 0), stop=(d == 7))
                                dst = (mvS[li] if which == 1 else memo)
                                B.evac(dst[:, m, fh * 128:(fh + 1) * 128], ps[3][:, m * 128:(m + 1) * 128])
                        src = mvS[li] if which == 1 else memo
                        B.dma(odst.rearrange("(m p) f -> p m f", p=128), src[:])
                else:
                    B.dma(memo[:], I["cache_mem_k"][li, b].rearrange("(m p) f -> p m f", p=128))
                    B.dma(mvS[li][:], I["cache_mem_v"][li, b].rearrange("(m p) f -> p m f", p=128))
                    for h in range(4):
                        for m in range(2):
                            B.tr(ps[2][:, m * 128:(m + 1) * 128], memo[:, m, h * 128:(h + 1) * 128])
                        B.evac(mkT[li][:, h, :], ps[2][:, 0:NMEM])

        def rwkv_scan(sq, TT):
            C = TT
            H = Hbd[sq["name"]]
            nlev = int(math.ceil(math.log2(C)))
            m2v = mask2[:].rearrange("p (a c) -> p a c", a=2)
            for pr in range(12):
                AR, AZ, BZ, KZ, BG, TMX = ARs[pr % 2], AZs[pr % 2], BZs[pr % 2], KZs[pr % 2], BGs[pr % 2], TMXs[pr % 2]
                rP, kP, vP = PS[:, pr, 0:C], PS[:, 12 + pr, 0:C], PS[:, 24 + pr, 0:C]
                B.stt(AR[:, 0:C], KK[:, pr, 0:C], -1.0, EQ[:, pr, 0:C], ALU.mult, ALU.mult)
                B.tt("pool", AR[:, 128:128 + C], rP, EP[:, pr, 0:C], ALU.mult)
                for h in range(2):
                    hs = slice(h * 64, h * 64 + 64)
                    B.copy("pool", AZ[h][hs, 0:C], AR[hs, 0:C])
                    B.tt("dve" if h else "pool", BZ[h][hs, 0:C], AA[hs, pr, 0:C], EM[hs, pr, 0:C], ALU.mult)
                    B.tt("pool" if h else "dve", KZ[h][hs, 0:C], kP[hs, :], EM[hs, pr, 0:C], ALU.mult)
                gC = EP[:, pr, C - 1:C]
                B.stt(BG[:, 0:C], AA[:, pr, 0:C], gC, EM[:, pr, 0:C], ALU.mult, ALU.mult)
                B.stt(BG[:, 128:128 + C], kP, gC, EM[:, pr, 0:C], ALU.mult, ALU.mult)
                B.tr(ps[3][0:C, 0:128], BG[:, 0:C]); B.tr(ps[3][0:C, 128:256], BG[:, 128:128 + C]); B.tr(ps[3][0:C, 256:384], vP)
                B.evac(TMX[0:C, :, :].rearrange("p a b -> p (a b)"), ps[3][0:C, 0:384])
                Vp = TMX[0:C, 2, :]
                for h in range(2):
                    B.mm(ps[4][0:C, h * 256:h * 256 + 128 + C], BZ[h][:, 0:C], AR[:, 0:128 + C])
                    B.mm(ps[5][0:C, h * 256:h * 256 + 128 + C], KZ[h][:, 0:C], AR[:, 0:128 + C])
                for h in range(2):
                    B.tt("dve", MA[h][0:C, :].rearrange("p (a c) -> p a c", a=2)[:, :, 0:C],
                         ps[4][0:C, h * 256:(h + 1) * 256].rearrange("p (a c) -> p a c", a=2)[:, :, 0:C], m2v[0:C, :, 0:C], ALU.mult)
                    B.tt("dve", MB[h][0:C, :].rearrange("p (a c) -> p a c", a=2)[:, :, 0:C],
                         ps[5][0:C, h * 256:(h + 1) * 256].rearrange("p (a c) -> p a c", a=2)[:, :, 0:C], m2v[0:C, :, 0:C], ALU.mult)
                for h in range(2):
                    B.mm(ps[6][0:C, h * 128:h * 128 + C], AZ[h][:, 0:C], BZ[h][:, 0:C])
                    B.tt("dve", PXa[h][0][0:C, 0:C], ps[6][0:C, h * 128:h * 128 + C], masksl[0:C, 0:C], ALU.mult)
                B.mm(ps[7][0:C, 0:128], AR[:, 0:C], H[:, pr, :], start=True, stop=False)
                for h in range(2):
                    B.mm(ps[7][0:C, h * 64:h * 64 + 64], MB[h][0:C, 0:C], Vp[:, h * 64:h * 64 + 64], start=False, stop=(h == 1))
                for h in range(2):
                    B.evac(PXa[h][0][0:C, 128:192], ps[7][0:C, h * 64:h * 64 + 64])
                cur = [0, 0]
                for lv in range(nlev):
                    last = lv == nlev - 1
                    for h in range(2):
                        Pm = PXa[h][cur[h]]
                        PT = MA[h][0:C, 0:C] if lv == 0 else PTa[h][cur[h]][0:C, 0:C]
                        pa_, pb_ = ps[4 + 2 * h], ps[5 + 2 * h]
                        if last:
                            B.mm(pa_[0:C, 128:192], PT, Pm[0:C, 128:192])
                            B.tt("dve", UP[0:C, h * 64:h * 64 + 64], pa_[0:C, 128:192], Pm[0:C, 128:192], ALU.add)
                        else:
                            nxt = 1 - cur[h]
                            if C == 128:
                                B.mm(pa_[0:C, 0:192], PT, Pm[0:C, 0:192])
                            else:
                                B.mm(pa_[0:C, 0:C], PT, Pm[0:C, 0:C])
                                B.mm(pa_[0:C, 128:192], PT, Pm[0:C, 128:192])
                            B.mm(pb_[0:C, 0:C], Pm[0:C, 0:C], PT)
                            B.copy("act", PXa[h][nxt][0:C, 0:C], pa_[0:C, 0:C])
                            B.tt("dve", PXa[h][nxt][0:C, 128:192], pa_[0:C, 128:192], Pm[0:C, 128:192], ALU.add)
                            B.copy("act" if h else "dve", PTa[h][nxt][0:C, 0:C], pb_[0:C, 0:C])
                            cur[h] = nxt
                for h in range(2):
                    hs = slice(h * 64, h * 64 + 64)
                    B.mm(ps[2][:, h * 128:h * 128 + C], H[:, pr, :], AR[:, 128:128 + C], start=True, stop=False)
                    B.mm(ps[2][:, h * 128:h * 128 + C], UP[0:C, :], MA[h][0:C, 128:128 + C], start=False, stop=False)
                    B.mm(ps[2][:, h * 128:h * 128 + C], Vp, MB[h][0:C, 128:128 + C], start=False, stop=True)
                    B.copy("act", YT[hs, pr, 0:C], ps[2][hs, h * 128:h * 128 + C])
                B.mm(ps[3][:, 384:512], TMX[0:C, 0, :], UP[0:C, :], start=True, stop=False)
                B.mm(ps[3][:, 384:512], TMX[0:C, 1, :], Vp, start=False, stop=True)
                B.tt("dve", STMP[:], ps[3][:, 384:512], bones[:], ALU.mult)
                B.stt(H[:, pr, :], H[:, pr, :], gC, STMP[:], ALU.mult, ALU.add)

        def phase1_tile(sq, ti):
            TT = sq["C"]; t0 = ti * TT; nm = sq["name"]
            B.dma(xt[0:TT, :], sq["x"][t0:t0 + TT, :])
            rms_to_hT(xt, TT, nw0, hT, xn)
            cnt = [0]

            def cons_p(f, pap):
                rw = raw[cnt[0] % 2]; df = dif[cnt[0] % 2]; cnt[0] += 1
                B.copy("pool", rw[:, 0:1], plast[:, f:f + 1])
                B.copy("act", rw[:, 1:TT + 1], pap)
                B.copy("pool", plast[:, f:f + 1], rw[:, TT:TT + 1])
                B.tt("pool", df[:, 0:TT], rw[:, 0:TT], rw[:, 1:TT + 1], ALU.subtract)
                B.stt(PS[:, f, 0:TT], df[:, 0:TT], mu[:, f:f + 1], rw[:, 1:TT + 1], ALU.mult, ALU.add)
            proj_chunks(WB["rw_in"], 0, 37, hT, TT, cons_p)
            B.act(PS[0:64, 36, 0:TT], PS[0:64, 36, 0:TT], AF.Tanh)
            for f in range(12):
                B.mm(ps[2][:, (f % 4) * 128:(f % 4) * 128 + TT], WUP[:, f * 128:(f + 1) * 128], PS[:, 36, 0:TT])
                B.act(SG[:, f, 0:TT], ps[2][:, (f % 4) * 128:(f % 4) * 128 + TT], AF.Sigmoid, bias=w0[:, f:f + 1])
                B.mm(ps[3][:, (f % 4) * 128:(f % 4) * 128 + TT], AUP[:, f * 128:(f + 1) * 128], PS[:, 36, 0:TT])
                B.act(AA[:, f, 0:TT], ps[3][:, (f % 4) * 128:(f % 4) * 128 + TT], AF.Sigmoid, bias=a0[:, f:f + 1])
            for g in range(3):
                sl = slice(4 * g, 4 * g + 4)
                bc = lambda col: col[:, sl].unsqueeze(2).broadcast_to([128, 4, TT])
                kv = PS[:, 12 + 4 * g:16 + 4 * g, 0:TT]; rv = PS[:, 4 * g:4 * g + 4, 0:TT]; vv = PS[:, 24 + 4 * g:28 + 4 * g, 0:TT]
                kkv, tv, av, bv = KK[:, sl, 0:TT], TMP[:, sl, 0:TT], AA[:, sl, 0:TT], BON[:, sl, 0:TT]
                p2 = ps[2][:].rearrange("p (a c) -> p a c", a=4)[:, :, 0:TT]
                p3 = ps[3][:].rearrange("p (a c) -> p a c", a=4)[:, :, 0:TT]
                B.tt("pool", kkv, kv, bc(k_k), ALU.mult)
                B.tt("pool", tv, kkv, kkv, ALU.mult)
                for j in range(4):
                    B.mm(ps[2][:, j * 128:j * 128 + TT], bones[:], TMP[:, 4 * g + j, 0:TT])
                B.act(tv, p2, AF.Sqrt)
                B.ts("dve", tv, tv, 1e-12, ALU.max)
                B.recip(tv, tv)
                B.tt("dve", kkv, kkv, tv, ALU.mult)
                B.stt(tv, av, -1.0, bc(k_a), ALU.add, ALU.mult)
                B.stt(kv, tv, 1.0, kv, ALU.add, ALU.mult)
                B.tt("pool", av, kkv, av, ALU.mult)
                B.tt("pool", tv, rv, bc(r_k), ALU.mult)
                B.tt("dve", tv, tv, kv, ALU.mult)
                for j in range(4):
                    B.mm(ps[3][:, j * 128:j * 128 + TT], bones[:], TMP[:, 4 * g + j, 0:TT])
                B.tt("dve", bv, p3, vv, ALU.mult)
            for f in range(12):
                B.scan_add(CS[:, f, 0:TT], onesrow[:, 0:TT], SG[:, f, 0:TT])
            B.tt("pool", TMP[:, :, 0:TT], CS[:, :, 0:TT], SG[:, :, 0:TT], ALU.subtract)
            B.act(EP[:, :, 0:TT], CS[:, :, 0:TT], AF.Exp, scale=-DECAY_C)
            B.act(EM[:, :, 0:TT], CS[:, :, 0:TT], AF.Exp, scale=DECAY_C)
            B.act(EQ[:, :, 0:TT], TMP[:, :, 0:TT], AF.Exp, scale=-DECAY_C)
            rwkv_scan(sq, TT)
            for g in range(3):
                sl = slice(4 * g, 4 * g + 4)
                yv, tv, bv = YT[:, sl, 0:TT], TMP[:, sl, 0:TT], BON[:, sl, 0:TT]
                p2 = ps[2][:].rearrange("p (a c) -> p a c", a=4)[:, :, 0:TT]
                p3 = ps[3][:].rearrange("p (a c) -> p a c", a=4)[:, :, 0:TT]
                bc = lambda col: col[:, sl].unsqueeze(2).broadcast_to([128, 4, TT])
                for j in range(4):
                    B.mm(ps[2][:, j * 128:j * 128 + TT], bones[:], YT[:, 4 * g + j, 0:TT])
                B.stt(yv, p2, -1.0 / 64, yv, ALU.mult, ALU.add)
                B.tt("pool", tv, yv, yv, ALU.mult)
                for j in range(4):
                    B.mm(ps[3][:, j * 128:j * 128 + TT], bones[:], TMP[:, 4 * g + j, 0:TT])
                B.ts("dve", tv, p3, 1.0 / 64, ALU.mult, GN_EPS, ALU.add)
                B.act(tv, tv, AF.Sqrt)
                B.recip(tv, tv)
                B.tt("dve", yv, yv, tv, ALU.mult)
                B.tt("pool", yv, yv, bc(ln_w), ALU.mult)
                B.tt("dve", yv, yv, bc(ln_b), ALU.add)
                B.tt("pool", yv, yv, bv, ALU.add)
            def cons_qg(f, pap):
                if f < 41:
                    B.evac(QM[:, f - 37, 0:TT], pap)
                else:
                    B.act(GT[:, f - 41, 0:TT], pap, AF.Silu)
            proj_chunks(WB["rw_in"], 37, 20, hT, TT, cons_qg)
            mem_attn(QM, TT, 0, YM)
            B.tt("pool", GTb[:, 0:12, 0:TT], YT[:, :, 0:TT], GT[:, 0:12, 0:TT], ALU.mult)
            B.tt("pool", GTb[:, 12:16, 0:TT], YM[:, :, 0:TT], GT[:, 12:16, 0:TT], ALU.mult)
            out_proj(0, 0, 16, TT, xt, x1t)
            rms_to_hT(x1t, TT, nw1, hT, xn)

            def cons_l1(f, pap):
                if f < 12:
                    B.copy("act", YTb[:, f, 0:TT], pap)
                    B.dma(sq["qT"][f, :, t0:t0 + TT], YTb[:, f, 0:TT], eng="act")
                elif f < 36:
                    g = (f - 12) % 4
                    B.evac(TMP[:, g, 0:TT], pap)
                    B.tr(ps[3][0:TT, g * 128:(g + 1) * 128], TMP[:, g, 0:TT])
                    if g == 3:
                        B.copy("act", KVT[0:TT, :], ps[3][0:TT, :])
                        dst = sq["ok"] if f < 24 else sq["ov"]
                        c0 = ((f - 12) % 12 - 3) * 128
                        B.dma(dst[t0:t0 + TT, c0:c0 + 512], KVT[0:TT, :], eng="act")
                elif f < 40:
                    B.evac(QM[:, f - 36, 0:TT], pap)
                else:
                    B.act(GT[:, f - 40, 0:TT], pap, AF.Silu)
            proj_chunks(WB["df_in"], 0, 56, hT, TT, cons_l1)
            B.dma(sq["sgT"][:, :, t0:t0 + TT].rearrange("c p t -> p c t"), GT[:, 0:12, 0:TT], eng="act")
            mem_attn(QM, TT, 1, YM)
            B.tt("pool", GTb[:, 12:16, 0:TT], YM[:, :, 0:TT], GT[:, 12:16, 0:TT], ALU.mult)
            out_proj(1, 12, 4, TT, x1t, xt)
            B.dma(sq["x1"][t0:t0 + TT, :], xt[0:TT, :], eng="act")

        def phase1_seq(sq):
            nm = sq["name"]; b = sq["b"]; H = Hbd[nm]
            phase_mem(sq)
            if sq["kind"] == "p":
                B.memset("pool", plast[:], 0.0)
                B.memset("pool", H[:], 0.0)
            else:
                B.memset("pool", plstg[:], 0.0)
                B.dma(plstg[0:37, :], I["state_shift"][b].rearrange("(c p) -> c p", p=128))
                B.tr(ps[0][:, 0:128], plstg[:])
                B.copy("dve", plast[:], ps[0][:, 0:37])
                B.memset("pool", TMP[:], 0.0)
                for pr in range(12):
                    for h in range(2):
                        B.dma(TMP[h * 64:h * 64 + 64, pr, h * 64:h * 64 + 64], I["state_rwkv"][b, 2 * pr + h])
                for pr in range(12):
                    B.tr(ps[pr % 2][:, 0:128], TMP[:, pr, :])
                    B.evac(H[:, pr, :], ps[pr % 2][:, 0:128])
            for ti in range(sq["T"] // sq["C"]):
                phase1_tile(sq, ti)
            B.tr(ps[0][0:37, 0:128], plast[:])
            B.copy("dve", plstg[0:37, :], ps[0][0:37, 0:128])
            B.dma(sq["oshift"].rearrange("(c p) -> c p", p=128), plstg[0:37, :])
            for pr in range(12):
                B.tr(ps[pr % 2][:, 0:128], H[:, pr, :])
                B.evac(TMP[:, pr, :], ps[pr % 2][:, 0:128])
            for pr in range(12):
                for h in range(2):
                    B.dma(sq["oS"][2 * pr + h], TMP[h * 64:h * 64 + 64, pr, h * 64:h * 64 + 64])

        def phase2(sq):
            T = sq["T"]; past = sq["past"]; b = sq["b"]
            L = past + T
            nkt = (L + 127) // 128
            Lp = nkt * 128
            with contextlib.ExitStack() as st2:
                sb2 = lambda n, s, dt=F32: st2.enter_context(nc.sbuf_tensor(n + sq["name"], list(s), dt))
                Ktm_ = [sb2("Ktm%d" % i, [128, nkt, 128]) for i in range(2)]; Vtm_ = [sb2("Vtm%d" % i, [128, nkt, 128]) for i in range(2)]
                Vb_ = [sb2("Vb%d" % i, [128, nkt, 128], BF16) for i in range(2)]
                KZ1_ = [sb2("KZ1%d" % i, [128, Lp], BF16) for i in range(2)]; KZ2_ = [sb2("KZ2%d" % i, [128, Lp], BF16) for i in range(2)]
                QT_ = [sb2("QT%d" % i, [128, T], BF16) for i in range(2)]
                NE = 3
                E1 = [sb2("E1%d" % i, [128, 512], BF16) for i in range(NE)]
                E2 = [sb2("E2%d" % i, [128, 512], BF16) for i in range(NE)]
                AC1 = sb2("AC1", [128, 512]); AC2 = sb2("AC2", [128, 512]); R1 = sb2("R1", [128, 512]); R2 = sb2("R2", [128, 512])
                OT = sb2("OT", [128, 512]); SQ2 = sb2("SQ2", [128, 512]); kval = sb2("kval", [128, 1])
                for i in range(2):
                    B.memset("pool", KZ1_[i][64:128, :], 0.0); B.memset("pool", KZ2_[i][0:64, :], 0.0)
                lastn = L - (nkt - 1) * 128
                if lastn < 128:
                    B.memset("pool", kval[:], 0.0); B.memset("pool", kval[0:lastn, :], 1.0)
                for h in range(12):
                    Ktm, Vtm, Vb, KZ1, KZ2, QT = Ktm_[h % 2], Vtm_[h % 2], Vb_[h % 2], KZ1_[h % 2], KZ2_[h % 2], QT_[h % 2]
                    if lastn < 128:
                        B.memset("pool", Ktm[:, nkt - 1, :], 0.0); B.memset("pool", Vtm[:, nkt - 1, :], 0.0)
                    def ld(dst, src, n0, ntl):
                        for g0 in range(0, ntl, 8):
                            g = min(8, ntl - g0)
                            B.dma(dst[:, n0 + g0:n0 + g0 + g, :],
                                  src[g0 * 128:(g0 + g) * 128, h * 128:(h + 1) * 128].rearrange("(n p) f -> p n f", p=128))
                    if past:
                        ld(Ktm, I["cache_k"][b], 0, past // 128)
                        ld(Vtm, I["cache_v"][b], 0, past // 128)
                    nk0 = past // 128
                    if T >= 128:
                        ld(Ktm, sq["ok"], nk0, T // 128)
                        ld(Vtm, sq["ov"], nk0, T // 128)
                    else:
                        B.dma(Ktm[0:T, nk0, :], sq["ok"][:, h * 128:(h + 1) * 128], rk=[sq["ok"]])
                        B.dma(Vtm[0:T, nk0, :], sq["ov"][:, h * 128:(h + 1) * 128], rk=[sq["ov"]])
                    B.dma(QT[:], sq["qT"][h], rk=[sq["qT"]])
                    hk = nkt // 2
                    if hk:
                        B.copy("act", Vb[:, 0:hk, :], Vtm[:, 0:hk, :])
                    B.copy("dve", Vb[:, hk:nkt, :], Vtm[:, hk:nkt, :])
                    for kt in range(nkt):
                        B.tr(ps[kt % 2][:, 0:128], Ktm[:, kt, :])
                        B.copy("act", KZ1[0:64, kt * 128:(kt + 1) * 128], ps[kt % 2][0:64, 0:128])
                        B.copy("dve", KZ2[64:128, kt * 128:(kt + 1) * 128], ps[kt % 2][64:128, 0:128])
                    QB = min(512, T)
                    for q0 in range(0, T, QB):
                        kend = (past + q0 + QB + 127) // 128
                        pvA, pvB = ps[4], ps[5]

                        def geom(kt):
                            if sq["kind"] == "p":
                                r = kt - q0 // 128
                                return r, max(0, r) * 128
                            return -1, 0

                        def qk(kt):
                            r, c0 = geom(kt)
                            sA, sB = ps[2 * (kt % 2)], ps[2 * (kt % 2) + 1]
                            B.mm(sA[:, c0:QB], KZ1[:, kt * 128:(kt + 1) * 128], QT[:, q0 + c0:q0 + QB])
                            B.mm(sB[:, c0:QB], KZ2[:, kt * 128:(kt + 1) * 128], QT[:, q0 + c0:q0 + QB])

                        qk(0)
                        for kt in range(kend):
                            r, c0 = geom(kt)
                            sA, sB = ps[2 * (kt % 2)], ps[2 * (kt % 2) + 1]
                            e1, e2 = E1[kt % NE], E2[kt % NE]
                            B.act(e1[:, c0:QB], sA[:, c0:QB], AF.Exp, scale=0.125)
                            B.act(e2[:, c0:QB], sB[:, c0:QB], AF.Exp, scale=0.125)
                            if kt + 1 < kend:
                                qk(kt + 1)
                            if r >= 0:
                                B.tt("dve", e1[:, c0:c0 + 128], e1[:, c0:c0 + 128], cmask[:], ALU.mult)
                                B.tt("dve", e2[:, c0:c0 + 128], e2[:, c0:c0 + 128], cmask[:], ALU.mult)
                            if kt == nkt - 1 and lastn < 128:
                                B.ts("pool", e1[:, c0:QB], e1[:, c0:QB], kval[:, 0:1], ALU.mult)
                                B.ts("pool", e2[:, c0:QB], e2[:, c0:QB], kval[:, 0:1], ALU.mult)
                            B.mm(pvA[:, c0:QB], Vb[:, kt, :], e1[:, c0:QB], start=(kt == 0), stop=(kt == kend - 1))
                            B.mm(pvB[:, c0:QB], Vb[:, kt, :], e2[:, c0:QB], start=(kt == 0), stop=(kt == kend - 1))
                            if kt == 0:
                                B.copy("dve", AC1[:, 0:QB], e1[:, 0:QB]); B.copy("dve", AC2[:, 0:QB], e2[:, 0:QB])
                            else:
                                B.tt("dve", AC1[:, c0:QB], AC1[:, c0:QB], e1[:, c0:QB], ALU.add)
                                B.tt("dve", AC2[:, c0:QB], AC2[:, c0:QB], e2[:, c0:QB], ALU.add)
                        B.mm(ps[6][:, 0:QB], ones[:], AC1[:, 0:QB]); B.mm(ps[7][:, 0:QB], ones[:], AC2[:, 0:QB])
                        B.recip(R1[:, 0:QB], ps[6][:, 0:QB]); B.recip(R2[:, 0:QB], ps[7][:, 0:QB])
                        B.tt("dve", R1[:, 0:QB], pvA[:, 0:QB], R1[:, 0:QB], ALU.mult)
                        B.tt("dve", R2[:, 0:QB], pvB[:, 0:QB], R2[:, 0:QB], ALU.mult)
                        B.stt(OT[:, 0:QB], R2[:, 0:QB], nlam[:, 0:1], R1[:, 0:QB], ALU.mult, ALU.add)
                        B.tt("pool", SQ2[:, 0:QB], OT[:, 0:QB], OT[:, 0:QB], ALU.mult)
                        B.mm(ps[6][:, 0:QB], ones[:], SQ2[:, 0:QB])
                        B.ts("dve", SQ2[:, 0:QB], ps[6][:, 0:QB], 1.0 / 128, ALU.mult, EPS, ALU.add)
                        B.act(SQ2[:, 0:QB], SQ2[:, 0:QB], AF.Sqrt)
                        B.recip(SQ2[:, 0:QB], SQ2[:, 0:QB])
                        B.tt("dve", OT[:, 0:QB], OT[:, 0:QB], SQ2[:, 0:QB], ALU.mult)
                        B.ts("dve", OT[:, 0:QB], OT[:, 0:QB], subln[:, 0:1], ALU.mult, 1.0 - lam_init, ALU.mult)
                        B.dma(sq["oT"][h, :, q0:q0 + QB], OT[:, 0:QB], eng="act")

        def phase3(sq):
            T = sq["T"]; TT = sq["C"]
            with contextlib.ExitStack() as st3:
                sb3 = lambda n, s, dt=F32: st3.enter_context(nc.sbuf_tensor(n + sq["name"], list(s), dt))
                WO = sb3("WO", [128, 12, D], BF16); GTb = sb3("GTb3", [128, 12, 128], BF16)
                fnw = sb3("fnw", [128, D])
                B.dma(fnw[:], I["final_norm_w"].partition_broadcast(128))
                YT_ = [sb3("YT3%d" % i, [128, 12, 128]) for i in range(2)]; GT_ = [sb3("GT3%d" % i, [128, 12, 128]) for i in range(2)]
                x1t_ = [sb3("x1t3%d" % i, [128, D]) for i in range(2)]; GTb_ = [sb3("GTb3%d" % i, [128, 12, 128], BF16) for i in range(2)]
                xt_ = [sb3("xt3%d" % i, [128, D]) for i in range(2)]; xn_ = [sb3("xn3%d" % i, [128, D]) for i in range(2)]
                rstat_ = [sb3("rstat3%d" % i, [128, 4]) for i in range(2)]
                for c4 in range(0, 12, 4):
                    B.dma(WO[:, c4:c4 + 4, :], WB["w_out"][BR + c4 * 128:BR + (c4 + 4) * 128, :].rearrange("(c p) f -> p c f", p=128))
                for ti in range(T // TT):
                    t0 = ti * TT
                    YT, GT, x1t, GTb, xt, xn, rstat = (YT_[ti % 2], GT_[ti % 2], x1t_[ti % 2], GTb_[ti % 2], xt_[ti % 2],
                                                       xn_[ti % 2], rstat_[ti % 2])
                    B.dma(YT[:, :, 0:TT], sq["oT"][:, :, t0:t0 + TT].rearrange("c p t -> p c t"), rk=[sq["oT"]])
                    B.dma(GT[:, 0:12, 0:TT], sq["sgT"][:, :, t0:t0 + TT].rearrange("c p t -> p c t"), rk=[sq["sgT"]])
                    B.dma(x1t[0:TT, :], sq["x1"][t0:t0 + TT, :], rk=[sq["x1"]])
                    B.tt("pool", GTb[:, 0:12, 0:TT], YT[:, :, 0:TT], GT[:, 0:12, 0:TT], ALU.mult)
                    for c in range(12):
                        for hf in range(2):
                            B.mm(ps[4 + hf][0:TT, :], GTb[:, c, 0:TT], WO[:, c, hf * 512:(hf + 1) * 512], start=(c == 0), stop=(c == 11))
                    for hf in range(2):
                        B.tt("dve", xt[0:TT, hf * 512:(hf + 1) * 512], ps[4 + hf][0:TT, :], x1t[0:TT, hf * 512:(hf + 1) * 512], ALU.add)
                    B.act(xn[0:TT, :], xt[0:TT, :], AF.Square, accum_out=rstat[0:TT, 0:1])
                    rstd_from_ss(rstat[0:TT, 0:1], D, TT, EPS)
                    B.stt(xn[0:TT, :], xt[0:TT, :], rstat[0:TT, 0:1], fnw[0:TT, :], ALU.mult, ALU.mult)
                    B.dma(sq["y"][t0:t0 + TT, :], xn[0:TT, :], eng="act")

        for sq in seqs:
            if "1" in PHASES:
                phase1_seq(sq)
        P.barrier()
        st1.close()
        for sq in seqs:
            if "2" in PHASES:
                phase2(sq)
                P.barrier()
        for sq in seqs:
            if "3" in PHASES:
                phase3(sq)
                P.barrier()
        P.emit()
    return nc, P


_IN_ORDER = ["x_prompt", "mem_prompt", "x_sample", "state_rwkv", "state_shift", "cache_k", "cache_v", "cache_mem_k",
             "cache_mem_v"]


def shard_inputs(inputs, n_cores, NB):
    c = host_consts()
    maps = []
    f = lambda a: np.ascontiguousarray(np.asarray(a, dtype=np.float32))
    for ci in range(n_cores):
        sl = slice(ci * NB, (ci + 1) * NB)
        m = {}
        m["x_prompt"] = f(inputs["x_prompt"][sl]); m["mem_prompt"] = f(inputs["mem_prompt"][sl])
        m["x_sample"] = f(inputs["x_sample"][sl])
        m["state_rwkv"] = f(inputs["state_rwkv"][0, sl]); m["state_shift"] = f(inputs["state_shift"][0, sl])
        ck = np.asarray(inputs["cache_k"]); cv = np.asarray(inputs["cache_v"])
        m["cache_k"] = f(ck[0, sl].reshape(NB, ck.shape[2], MIXW)); m["cache_v"] = f(cv[0, sl].reshape(NB, cv.shape[2], MIXW))
        m["cache_mem_k"] = f(np.asarray(inputs["cache_mem_k"])[:, sl].reshape(2, NB, NMEM, MEMW))
        m["cache_mem_v"] = f(np.asarray(inputs["cache_mem_v"])[:, sl].reshape(2, NB, NMEM, MEMW))
        for n in ["norm_w", "mem_norm_w", "w_mem_k", "w_mem_v", "w_out", "final_norm_w"]:
            m[n] = f(inputs[n])
        for n in ["rw_in", "rw_mu", "rw_w0", "rw_w_up", "rw_a0", "rw_a_up", "rw_k_k", "rw_k_a", "rw_ln_w", "rw_ln_b",
                  "df_in", "df_lq1", "df_lk1", "df_lq2", "df_lk2", "df_subln"]:
            m[n] = f(np.asarray(inputs[n])[0])
        m["rw_r_k"] = f(np.asarray(inputs["rw_r_k"])[0].reshape(MIXW))
        m.update({"c_" + k: v for k, v in c.items()})
        maps.append(m)
    return maps


def gather_outputs(results, NB, TP, TS):
    cat = lambda n, ax: np.concatenate([np.asarray(r[n]) for r in results], axis=ax)
    nb = NB * len(results)
    y_p = cat("y_prompt", 0); y_s = cat("y_sample", 0)
    p_S = cat("p_S", 0)[None]; p_sh = cat("p_shift", 0)[None]
    p_k = cat("p_k", 0).reshape(1, nb, TP, 12, 128); p_v = cat("p_v", 0).reshape(1, nb, TP, 12, 128)
    p_mk = cat("p_mk", 1).reshape(2, nb, NMEM, 4, 128); p_mv = cat("p_mv", 1).reshape(2, nb, NMEM, 4, 128)
    s_S = cat("s_S", 0)[None]; s_sh = cat("s_shift", 0)[None]
    s_k = cat("s_k", 0).reshape(1, nb, TS, 12, 128); s_v = cat("s_v", 0).reshape(1, nb, TS, 12, 128)
    return (y_p, y_s, p_S, p_sh, p_k, p_v, p_mk, p_mv, s_S, s_sh, s_k, s_v)


def kernel(**inputs):
    n_cores = 8
    xp = np.asarray(inputs["x_prompt"]); xs = np.asarray(inputs["x_sample"])
    NB = xp.shape[0] // n_cores
    TP, TS = xp.shape[1], xs.shape[1]
    PAST = np.asarray(inputs["cache_k"]).shape[2]
    nc, _ = build(TP, TS, PAST, NB)
    maps = shard_inputs(inputs, n_cores, NB)
    res = run_bass_kernel_spmd(nc, maps, core_ids=list(range(n_cores)))
    return gather_outputs(res.results, NB, TP, TS)
```
